# Optimizing a Trainium2 kernel written in Bass

```python
import math
import jax
import jax.numpy as jnp
from jax import lax
import numpy as np

D_MODEL = 1024
BATCH = 8
SEQ = 2048
DEPTH = 4
DEC_BATCH = 128
DEC_SEQ = 8
PAST_LEN = 16384
PAGE_SIZE = 128

N_MIXERS = 3
CONV_W = 4
EPS = 1e-6
H_A = 8
DK_A = 128
DV_A = 128
D_QK_A = H_A * DK_A
D_V_A = H_A * DV_A
D_CONV_A = 2 * D_QK_A + D_V_A
D_IN_A = D_CONV_A + D_V_A + 2 * H_A
CHUNK_A = 64
CHUNK_B = 128
D_B = 2 * D_MODEL
H_B = 8
DG_B = D_B // H_B
D_IN_B = 2 * D_B
D_INNER_C = 2 * D_MODEL
P_C = 64
H_C = D_INNER_C // P_C
G_C = 4
R_C = H_C // G_C
N_C = 128
D_XBC_C = D_INNER_C + 2 * G_C * N_C
D_IN_C = D_INNER_C + D_XBC_C + H_C
CHUNK_C = 128
D_FF = 2816
N_LAYERS_A = (DEPTH + 2) // 3
N_LAYERS_B = (DEPTH + 1) // 3
N_LAYERS_C = DEPTH // 3

kernel_name = "hybrid_gdn_chunkmlp_ssd_decoder_step"


def rms_norm(x, w):
    xf = x.astype(jnp.float32)
    y = xf * lax.rsqrt(jnp.mean(xf * xf, axis=-1, keepdims=True) + EPS)
    return (y * w.astype(jnp.float32)).astype(x.dtype)


def layer_norm(x, w, b):
    xf = x.astype(jnp.float32)
    mu = jnp.mean(xf, axis=-1, keepdims=True)
    var = jnp.mean(jnp.square(xf - mu), axis=-1, keepdims=True)
    return ((xf - mu) * lax.rsqrt(var + EPS) * w.astype(jnp.float32) + b.astype(jnp.float32)).astype(x.dtype)


def l2norm(x):
    xf = x.astype(jnp.float32)
    return xf * lax.rsqrt(jnp.sum(xf * xf, axis=-1, keepdims=True) + EPS)


def swiglu(x, w_in, w_out):
    g, u = jnp.split(x @ w_in, 2, axis=-1)
    return (jax.nn.silu(g) * u) @ w_out


def half_ffn(x, w_pre, w_post, w_in, w_out):
    return x + 0.5 * rms_norm(swiglu(rms_norm(x, w_pre), w_in, w_out), w_post)


def causal_dwconv(x, buf, w):
    L = x.shape[1]
    xp = jnp.concatenate([buf.astype(x.dtype), x], axis=1)
    y = sum(xp[:, j:j + L] * w[j] for j in range(CONV_W))
    return y, xp[:, L:]


def to_chunks(a, c):
    B, L = a.shape[:2]
    n = -(-L // c)
    a = jnp.pad(a, [(0, 0), (0, n * c - L)] + [(0, 0)] * (a.ndim - 2))
    return a.reshape((B, n, c) + a.shape[2:])


def gated_delta_chunked(q, k, v, g, beta, s0):
    B, L = q.shape[:2]
    c = min(CHUNK_A, L)
    f32 = jnp.float32
    q, k, v = [to_chunks(a.astype(f32), c).swapaxes(2, 3) for a in (q, k, v)]
    g, beta = [to_chunks(a.astype(f32), c).swapaxes(2, 3) for a in (g, beta)]
    G = jnp.cumsum(g, axis=-1)
    causal = jnp.tril(jnp.ones((c, c), bool))
    strict = jnp.tril(jnp.ones((c, c), bool), -1)
    gamma = jnp.exp(jnp.where(causal, G[..., :, None] - G[..., None, :], -jnp.inf))
    kb = k * beta[..., None]
    m = jnp.where(strict, jnp.einsum('bzhid,bzhjd->bzhij', kb, k) * gamma, 0.0)
    rhs = jnp.concatenate([v * beta[..., None], kb * jnp.exp(G)[..., None]], axis=-1)
    sol = lax.linalg.triangular_solve(jnp.eye(c, dtype=f32) + m, rhs, left_side=True, lower=True,
                                      unit_diagonal=True)
    u_ps, w_ps = sol[..., :DV_A], sol[..., DV_A:]
    attn = jnp.einsum('bzhid,bzhjd->bzhij', q, k) * gamma
    q_dec = q * jnp.exp(G)[..., None]
    k_dec = k * jnp.exp(G[..., -1:] - G)[..., None]
    g_tot = jnp.exp(G[..., -1])

    def step(S, inp):
        u_c, w_c, a_c, qd_c, kd_c, gt_c = inp
        v_new = u_c - jnp.einsum('bhid,bhde->bhie', w_c, S)
        o = jnp.einsum('bhid,bhde->bhie', qd_c, S) + jnp.einsum('bhij,bhje->bhie', a_c, v_new)
        S = S * gt_c[..., None, None] + jnp.einsum('bhjd,bhje->bhde', kd_c, v_new)
        return S, o

    xs = tuple(jnp.moveaxis(a, 1, 0) for a in (u_ps, w_ps, attn, q_dec, k_dec, g_tot))
    s_fin, o = lax.scan(step, s0.astype(f32), xs)
    o = jnp.moveaxis(o, 0, 1).swapaxes(2, 3)
    o = o.reshape(B, -1, H_A, DV_A)[:, :L]
    return o, s_fin


def ssd_chunked(x, dt, a, bm, cm, s0):
    B, L = x.shape[:2]
    c = min(CHUNK_C, L)
    f32 = jnp.float32
    x = to_chunks(x.astype(f32), c).reshape(B, -1, c, G_C, R_C, P_C)
    dt = to_chunks(dt.astype(f32), c).reshape(B, -1, c, G_C, R_C)
    bm = to_chunks(bm.astype(f32), c)
    cm = to_chunks(cm.astype(f32), c)
    G = jnp.cumsum(dt * a.astype(f32).reshape(G_C, R_C), axis=2)
    causal = jnp.tril(jnp.ones((c, c), bool))[:, :, None, None]
    seg = jnp.exp(jnp.where(causal, G[:, :, :, None] - G[:, :, None, :], -jnp.inf))
    xdt = x * dt[..., None]
    cb = jnp.einsum('bzigs,bzjgs->bzgij', cm, bm)
    y_diag = jnp.einsum('bzgij,bzijgr,bzjgrp->bzigrp', cb, seg, xdt)
    dec_end = jnp.exp(G[:, :, -1:] - G)
    states = jnp.einsum('bzjgs,bzjgr,bzjgrp->bzgrps', bm, dec_end, xdt)
    chunk_dec = jnp.exp(G[:, :, -1])

    def step(S, inp):
        st, gd = inp
        return S * gd[..., None, None] + st, S

    s_init = s0.astype(f32).reshape(B, G_C, R_C, P_C, N_C)
    s_fin, s_prev = lax.scan(step, s_init, (jnp.moveaxis(states, 1, 0), jnp.moveaxis(chunk_dec, 1, 0)))
    s_prev = jnp.moveaxis(s_prev, 0, 1)
    y_off = jnp.einsum('bzigs,bzigr,bzgrps->bzigrp', cm, jnp.exp(G), s_prev)
    y = (y_diag + y_off).reshape(B, -1, H_C, P_C)[:, :L]
    return y, s_fin.reshape(B, H_C, P_C, N_C)


def gdn_mixer(h, conv_buf, s0, w_in, conv_w, a_log, dt_bias, norm_w, w_out):
    B, L, _ = h.shape
    qkv, gate, a_in, b_in = jnp.split(h @ w_in, [D_CONV_A, D_CONV_A + D_V_A, D_CONV_A + D_V_A + H_A], axis=-1)
    qkv, new_buf = causal_dwconv(qkv, conv_buf, conv_w)
    qkv = jax.nn.silu(qkv)
    q, k, v = jnp.split(qkv, [D_QK_A, 2 * D_QK_A], axis=-1)
    q = l2norm(q.reshape(B, L, H_A, DK_A)) * (DK_A ** -0.5)
    k = l2norm(k.reshape(B, L, H_A, DK_A))
    v = v.reshape(B, L, H_A, DV_A)
    g = -jnp.exp(a_log.astype(jnp.float32)) * jax.nn.softplus(a_in.astype(jnp.float32) + dt_bias.astype(jnp.float32))
    beta = jax.nn.sigmoid(b_in.astype(jnp.float32))
    o, s_new = gated_delta_chunked(q, k, v, g, beta, s0)
    o = rms_norm(o.astype(h.dtype), norm_w) * jax.nn.silu(gate.reshape(B, L, H_A, DV_A))
    return o.reshape(B, L, D_V_A) @ w_out, (new_buf, s_new.astype(s0.dtype))


def chunk_mlp_mixer(h, w_in, b_in, ln_w, ln_b, w_s, b_s, w_out):
    B, L, _ = h.shape
    u, v = jnp.split(jax.nn.gelu(h @ w_in + b_in, approximate=False), 2, axis=-1)
    v = layer_norm(v, ln_w, ln_b)
    vc = to_chunks(v, CHUNK_B).reshape(B, -1, CHUNK_B, H_B, DG_B)
    ws = jnp.where(jnp.tril(jnp.ones((CHUNK_B, CHUNK_B), bool)), w_s, 0.0)
    mixed = jnp.einsum('hts,bzshd->bzthd', ws, vc) + b_s.T[None, None, :, :, None]
    mixed = mixed.reshape(B, -1, D_B)[:, :L]
    return (u * mixed) @ w_out, (v,)


def ssd_mixer(h, conv_buf, s0, w_in, conv_w, conv_b, dt_bias, a_log, d_skip, norm_w, w_out):
    B, L, _ = h.shape
    z, xbc, dt = jnp.split(h @ w_in, [D_INNER_C, D_INNER_C + D_XBC_C], axis=-1)
    xbc, new_buf = causal_dwconv(xbc, conv_buf, conv_w)
    xbc = jax.nn.silu(xbc + conv_b)
    x, bm, cm = jnp.split(xbc, [D_INNER_C, D_INNER_C + G_C * N_C], axis=-1)
    x = x.reshape(B, L, H_C, P_C)
    bm = bm.reshape(B, L, G_C, N_C)
    cm = cm.reshape(B, L, G_C, N_C)
    dt = jax.nn.softplus(dt.astype(jnp.float32) + dt_bias.astype(jnp.float32))
    a = -jnp.exp(a_log.astype(jnp.float32))
    y, s_new = ssd_chunked(x, dt, a, bm, cm, s0)
    y = y + x.astype(jnp.float32) * d_skip.astype(jnp.float32)[:, None]
    yg = (y.reshape(B, L, D_INNER_C).astype(h.dtype) * jax.nn.silu(z)).reshape(B, L, G_C, D_INNER_C // G_C)
    y = rms_norm(yg, norm_w.reshape(G_C, -1)).reshape(B, L, D_INNER_C)
    return y @ w_out, (new_buf, s_new.astype(s0.dtype))


def layer(i, x, conv_buf, rec_state, W):
    kind, j = i % N_MIXERS, i // N_MIXERS
    nw = W['norm_w'][i]
    x = half_ffn(x, nw[0], nw[1], W['ffn_w_in'][i, 0], W['ffn_w_out'][i, 0])
    h = rms_norm(x, nw[2])
    if kind == 0:
        m, new_state = gdn_mixer(h, conv_buf, rec_state, W['gdn_w_in'][j], W['gdn_conv_w'][j], W['gdn_a_log'][j],
                                 W['gdn_dt_bias'][j], W['gdn_norm_w'][j], W['gdn_w_out'][j])
    elif kind == 1:
        m, new_state = chunk_mlp_mixer(h, W['cmlp_w_in'][j], W['cmlp_b_in'][j], W['cmlp_ln_w'][j], W['cmlp_ln_b'][j],
                                       W['cmlp_w_s'][j], W['cmlp_b_s'][j], W['cmlp_w_out'][j])
    else:
        m, new_state = ssd_mixer(h, conv_buf, rec_state, W['ssd_w_in'][j], W['ssd_conv_w'][j], W['ssd_conv_b'][j],
                                 W['ssd_dt_bias'][j], W['ssd_a_log'][j], W['ssd_d'][j], W['ssd_norm_w'][j],
                                 W['ssd_w_out'][j])
    x = x + rms_norm(m, nw[3])
    x = half_ffn(x, nw[4], nw[5], W['ffn_w_in'][i, 1], W['ffn_w_out'][i, 1])
    return x, new_state


def _dt_bias(k, shape):
    dt = jnp.exp(jax.random.uniform(k, shape, minval=math.log(1e-3), maxval=math.log(1e-1)))
    return dt + jnp.log(-jnp.expm1(-dt))


def setup_inputs(seed: int = 0) -> dict:
    key = jax.random.key(seed)
    ks = iter(jax.random.split(key, 40))
    nrm = lambda shape, s: jax.random.normal(next(ks), shape, jnp.float32) * s
    gain = lambda shape: 1.0 + nrm(shape, 0.05)
    a_log = lambda shape: jnp.log(jax.random.uniform(next(ks), shape, minval=1.0, maxval=16.0))
    return {
        'x_prompt': nrm((BATCH, SEQ, D_MODEL), 1.0),
        'x_sample': nrm((DEC_BATCH, DEC_SEQ, D_MODEL), 1.0),
        'state_gdn': nrm((N_LAYERS_A, DEC_BATCH, H_A, DK_A, DV_A), 0.5),
        'state_gdn_conv': nrm((N_LAYERS_A, DEC_BATCH, CONV_W - 1, D_CONV_A), 1.0),
        'state_ssd': nrm((N_LAYERS_C, DEC_BATCH, H_C, P_C, N_C), 0.1),
        'state_ssd_conv': nrm((N_LAYERS_C, DEC_BATCH, CONV_W - 1, D_XBC_C), 1.0),
        'norm_w': gain((DEPTH, 6, D_MODEL)),
        'ffn_w_in': nrm((DEPTH, 2, D_MODEL, 2 * D_FF), D_MODEL ** -0.5),
        'ffn_w_out': nrm((DEPTH, 2, D_FF, D_MODEL), D_FF ** -0.5),
        'gdn_w_in': nrm((N_LAYERS_A, D_MODEL, D_IN_A), D_MODEL ** -0.5),
        'gdn_conv_w': nrm((N_LAYERS_A, CONV_W, D_CONV_A), CONV_W ** -0.5),
        'gdn_a_log': a_log((N_LAYERS_A, H_A)),
        'gdn_dt_bias': _dt_bias(next(ks), (N_LAYERS_A, H_A)),
        'gdn_norm_w': gain((N_LAYERS_A, DV_A)),
        'gdn_w_out': nrm((N_LAYERS_A, D_V_A, D_MODEL), D_V_A ** -0.5),
        'cmlp_w_in': nrm((N_LAYERS_B, D_MODEL, D_IN_B), D_MODEL ** -0.5),
        'cmlp_b_in': nrm((N_LAYERS_B, D_IN_B), 0.02),
        'cmlp_ln_w': gain((N_LAYERS_B, D_B)),
        'cmlp_ln_b': nrm((N_LAYERS_B, D_B), 0.02),
        'cmlp_w_s': nrm((N_LAYERS_B, H_B, CHUNK_B, CHUNK_B), CHUNK_B ** -0.5),
        'cmlp_b_s': 1.0 + nrm((N_LAYERS_B, H_B, CHUNK_B), 0.1),
        'cmlp_w_out': nrm((N_LAYERS_B, D_B, D_MODEL), D_B ** -0.5),
        'ssd_w_in': nrm((N_LAYERS_C, D_MODEL, D_IN_C), D_MODEL ** -0.5),
        'ssd_conv_w': nrm((N_LAYERS_C, CONV_W, D_XBC_C), CONV_W ** -0.5),
        'ssd_conv_b': nrm((N_LAYERS_C, D_XBC_C), 0.02),
        'ssd_dt_bias': _dt_bias(next(ks), (N_LAYERS_C, H_C)),
        'ssd_a_log': a_log((N_LAYERS_C, H_C)),
        'ssd_d': gain((N_LAYERS_C, H_C)),
        'ssd_norm_w': gain((N_LAYERS_C, D_INNER_C)),
        'ssd_w_out': nrm((N_LAYERS_C, D_INNER_C, D_MODEL), D_INNER_C ** -0.5),
    }


def reference(x_prompt, x_sample, state_gdn, state_gdn_conv, state_ssd, state_ssd_conv,
              norm_w, ffn_w_in, ffn_w_out,
              gdn_w_in, gdn_conv_w, gdn_a_log, gdn_dt_bias, gdn_norm_w, gdn_w_out,
              cmlp_w_in, cmlp_b_in, cmlp_ln_w, cmlp_ln_b, cmlp_w_s, cmlp_b_s, cmlp_w_out,
              ssd_w_in, ssd_conv_w, ssd_conv_b, ssd_dt_bias, ssd_a_log, ssd_d, ssd_norm_w, ssd_w_out):
    W = dict(norm_w=norm_w, ffn_w_in=ffn_w_in, ffn_w_out=ffn_w_out,
             gdn_w_in=gdn_w_in, gdn_conv_w=gdn_conv_w, gdn_a_log=gdn_a_log, gdn_dt_bias=gdn_dt_bias,
             gdn_norm_w=gdn_norm_w, gdn_w_out=gdn_w_out,
             cmlp_w_in=cmlp_w_in, cmlp_b_in=cmlp_b_in, cmlp_ln_w=cmlp_ln_w, cmlp_ln_b=cmlp_ln_b,
             cmlp_w_s=cmlp_w_s, cmlp_b_s=cmlp_b_s, cmlp_w_out=cmlp_w_out,
             ssd_w_in=ssd_w_in, ssd_conv_w=ssd_conv_w, ssd_conv_b=ssd_conv_b, ssd_dt_bias=ssd_dt_bias,
             ssd_a_log=ssd_a_log, ssd_d=ssd_d, ssd_norm_w=ssd_norm_w, ssd_w_out=ssd_w_out)
    xp, xs = x_prompt, x_sample
    nb, dt_p = xp.shape[0], xp.dtype
    gdn_p, gdn_conv_p, ssd_p, ssd_conv_p = [], [], [], []
    gdn_s, gdn_conv_s, ssd_s, ssd_conv_s, cmlp_s = [], [], [], [], []
    for i in range(DEPTH):
        kind, j = i % N_MIXERS, i // N_MIXERS
        if kind == 0:
            bp = jnp.zeros((nb, CONV_W - 1, D_CONV_A), dt_p)
            sp = jnp.zeros((nb, H_A, DK_A, DV_A), dt_p)
            bs, ss = state_gdn_conv[j], state_gdn[j]
        elif kind == 2:
            bp = jnp.zeros((nb, CONV_W - 1, D_XBC_C), dt_p)
            sp = jnp.zeros((nb, H_C, P_C, N_C), dt_p)
            bs, ss = state_ssd_conv[j], state_ssd[j]
        else:
            bp = sp = bs = ss = None
        xp, new_p = layer(i, xp, bp, sp, W)
        xs, new_s = layer(i, xs, bs, ss, W)
        if kind == 0:
            gdn_conv_p.append(new_p[0]); gdn_p.append(new_p[1])
            gdn_conv_s.append(new_s[0]); gdn_s.append(new_s[1])
        elif kind == 1:
            cmlp_s.append(new_s[0])
        else:
            ssd_conv_p.append(new_p[0]); ssd_p.append(new_p[1])
            ssd_conv_s.append(new_s[0]); ssd_s.append(new_s[1])
    return (xp, xs,
            jnp.stack(gdn_p), jnp.stack(gdn_conv_p), jnp.stack(ssd_p), jnp.stack(ssd_conv_p),
            jnp.stack(gdn_s), jnp.stack(gdn_conv_s), jnp.stack(ssd_s), jnp.stack(ssd_conv_s),
            jnp.stack(cmlp_s))
```

```python
import numpy as np
from contextlib import ExitStack
import concourse.bass as bass
import concourse.mybir as mybir
from concourse.alu_op_type import AluOpType as ALU
from concourse.bass_utils import run_bass_kernel_spmd

F32 = mybir.dt.float32
BF16 = mybir.dt.bfloat16
U8 = mybir.dt.uint8
AF = mybir.ActivationFunctionType
AX = mybir.AxisListType

NCORES = 8
D = 1024
DC = 8
SEQ = 2048
NS = 16
LS = 8
T = SEQ + NS * LS
DEPTH = 4
DFF = 2816
FC = 22
EPS = 1e-6
WBUF = 2048
NRING = 5
NSTG = 3
WC_UNITS = 440
LOOKAHEAD = 3
CAST_PATTERN = ['act', 'dve']

FFN_PASSES = [[(0, 512), (512, 256)], [(768, 512), (1280, 256)], [(1536, 512), (2048, 128)]]
PMAX = 768
import os
BLOCKS128 = [(t, 128) for t in range(0, T, 128)][int(os.environ.get('B128_0', '0')):int(os.environ.get('B128_1', '17'))]
BLOCKS = [(0, 512), (512, 512), (1024, 512), (1536, 512), (2048, 128)][int(os.environ.get("BLK0", "0")):int(os.environ.get("NBLK", "5"))]


class Sched:
    NDMA = 28

    def __init__(self, nc, stack):
        self.nc = nc
        self.names = ['pe', 'act', 'dve', 'pool', 'sp']
        self.prog = {e: [] for e in self.names}
        self.sem = {e: stack.enter_context(nc.semaphore('s_' + e)) for e in self.names}
        self.cnt = {e: 0 for e in self.names}
        self.pend = {e: False for e in self.names}
        self.seen = {e: {} for e in self.names}
        self.dsem = [stack.enter_context(nc.semaphore('d%d' % i)) for i in range(self.NDMA)]
        self.dcnt = 0
        self.lastw = {}
        self.reads = {}
        self.sp_events = []
        self.all_dma = {}
        self.last_barrier = []
        self.sw_last = {}

    def _deps(self, e, reads, writes):
        deps = {}

        def add(ev):
            s, v = ev
            k = id(s)
            if k not in deps or deps[k][1] < v:
                deps[k] = (s, v)
        for k in reads:
            ev = self.lastw.get(k)
            if ev is not None:
                add(ev)
        for k in writes:
            ev = self.lastw.get(k)
            if ev is not None:
                add(ev)
            for ev in self.reads.get(k, {}).values():
                add(ev)
        return self._filter(e, deps.values())

    def _filter(self, e, evs):
        out = []
        for s, v in evs:
            if s is self.sem[e] and (v > self.cnt[e] or e == 'pe'):
                continue
            if self.seen[e].get(id(s), 0) < v:
                self.seen[e][id(s)] = v
                out.append((s, v))
        return out

    def _commit(self, ev, reads, writes):
        for k in writes:
            self.lastw[k] = ev
            self.reads[k] = {}
        s, v = ev
        for k in reads:
            d = self.reads.setdefault(k, {})
            old = d.get(id(s))
            if old is None or old[1] < v:
                d[id(s)] = ev

    def op(self, e, fn, reads=(), writes=(), inc=True):
        waits = self._deps(e, reads, writes)
        sem = self.sem[e]
        ev = (sem, self.cnt[e] + 1)
        if inc:
            self.cnt[e] += 1
            self.pend[e] = False
        else:
            self.pend[e] = True

        def emit(eng, fn=fn, waits=waits, inc=inc, sem=sem):
            for s, v in waits:
                eng.wait_ge(s, v)
            ins = fn(eng)
            if inc:
                ins.then_inc(sem, 1)
        self.prog[e].append(emit)
        self._commit(ev, reads, writes)

    def dma(self, e, out, in_, reads=(), writes=(), fence=False, **kw):
        i = self.dcnt % self.NDMA
        gen = self.dcnt // self.NDMA
        self.dcnt += 1
        ds = self.dsem[i]
        waits = self._deps(e, reads, writes)
        if fence:
            waits += self._filter(e, self.last_barrier)
        if gen > 0:
            waits += self._filter(e, [(ds, 16 * gen)])
        ev = (ds, 16 * (gen + 1))
        self.all_dma[i] = ev

        def emit(eng, waits=waits, ds=ds):
            for s, v in waits:
                eng.wait_ge(s, v)
            eng.dma_start(out=out, in_=in_, **kw).then_inc(ds, 16)
        self.prog[e].append(emit)
        self._commit(ev, reads, writes)
        if e == 'sp':
            self.sp_events.append(ev)
        return ev

    def swdma(self, sem, ev, out, in_, reads=(), writes=(), **kw):
        e = 'pool'
        waits = self._deps(e, reads, writes)

        def emit(eng, waits=waits, sem=sem):
            for s, v in waits:
                eng.wait_ge(s, v)
            eng.sem_clear(sem)
            eng.dma_start(out=out, in_=in_, **kw).then_inc(sem, 16)
        self.prog[e].append(emit)
        self._commit(ev, reads, writes)

    def publish(self, sem, ready_sem):
        def emit(eng, sem=sem, ready_sem=ready_sem):
            eng.wait_ge(sem, 16)
            eng.sem_inc(ready_sem, 1)
        self.prog['pool'].append(emit)

    def barrier(self, engines=('pe', 'act', 'dve', 'pool')):
        evs = [(self.sem[w], self.cnt[w]) for w in engines if self.cnt[w] > 0] + list(self.sp_events)
        self.last_barrier = [(self.sem[w], self.cnt[w]) for w in engines if self.cnt[w] > 0]
        self.sp_events = []
        for e in engines:
            waits = self._filter(e, evs)

            def emit(eng, waits=waits):
                for s, v in waits:
                    eng.wait_ge(s, v)
            self.prog[e].append(emit)

    def final_wait(self, e='sp'):
        evs = list(self.all_dma.values()) + [(self.sem[w], self.cnt[w]) for w in self.names if self.cnt[w] > 0 and w != e]
        waits = self._filter(e, evs)

        def emit(eng, waits=waits):
            for s, v in waits:
                eng.wait_ge(s, v)
        self.prog[e].append(emit)

    def finish(self, block, nc):
        for e in self.names:
            assert not self.pend[e], e
        decos = {'pe': block.tensor, 'act': block.scalar, 'dve': block.vector, 'pool': block.gpsimd, 'sp': block.sync}
        for name in self.names:
            prog = self.prog[name]

            def body(eng, prog=prog):
                for f in prog:
                    f(eng)
            decos[name](body)


def make_consts():
    i = np.arange(128)
    c = {}
    c['ident'] = np.eye(128, dtype=np.float32)
    c['ones'] = np.ones((128, 128), np.float32)
    c['A'] = (i[:, None] > i[None, :]).astype(np.float32)
    c['B'] = (i[:, None] <= i[None, :]).astype(np.float32)
    c['Ms'] = (i[:, None] < i[None, :]).astype(np.float32)
    blk = (i[:, None] // 8 == i[None, :] // 8)
    c['Mblk'] = (blk & ((i[:, None] % 8) <= (i[None, :] % 8))).astype(np.float32)
    rep = np.zeros((128, 128), np.float32)
    for r in range(8):
        rep[r, (i % 8) == r] = 1.0
    c['Rep8'] = rep
    sel = np.zeros((128, 128), np.float32)
    sel[127, :] = 1.0
    c['SelLast'] = sel
    c['Ablk'] = (blk & ((i[:, None] % 8) > (i[None, :] % 8))).astype(np.float32)
    c['SelLastS'] = (blk & ((i[:, None] % 8) == 7)).astype(np.float32)
    ss = np.zeros((128, 128), np.float32)
    for q in range(16):
        ss[q * 8:(q + 1) * 8, q] = 1.0
        ss[q * 8 + 7, 16 + q] = 1.0
    c['SeqSel'] = ss
    names = ['ident', 'ones', 'A', 'B', 'Ms', 'Mblk', 'Rep8', 'SelLast', 'Ablk', 'SelLastS', 'SeqSel']
    return names, np.concatenate([c[n] for n in names], axis=1)


CONST_NAMES, CONST_ARR = make_consts()


class Builder:
    def __init__(self, stages=999, dbg=False, plan=None):
        self.stages = stages
        if plan is None:
            plan = [(l, p) for l in range(DEPTH) for p in range(3)][:min(stages, 12)]
        self.plan = plan
        self.nl = DEPTH
        self.nc = bass.Bass("TRN2", target_bir_lowering=False)
        self.dbg = dbg

    def sb(self, name, shape, dt, off):
        return self.nc.alloc_sbuf_tensor_at(name, shape, dt, offset=self.arena0 + off)

    def din(self, name, shape, dt=F32):
        return self.nc.dram_tensor(name, list(shape), dt, kind="ExternalInput").ap()

    def dout(self, name, shape, dt=F32):
        return self.nc.dram_tensor(name, list(shape), dt, kind="ExternalOutput").ap()

    def wstream(self, sources):
        state = {'issued': 0, 'taken': 0, 'slots': []}

        def issue():
            k = state['issued']
            src, nelem, cid = sources[k]
            gi = self.wcnt
            self.wcnt += 1
            ri = gi % NRING
            ring = self.ring[ri]
            if cid in self.wc_idx:
                idx = self.wc_idx[cid]
                self.S.dma('sp', ring[:, 0:nelem], self.wcache[idx, :, 0:nelem], reads=['wc%d' % idx], writes=['ring%d' % ri])
            else:
                si = self.scnt % NSTG
                self.scnt += 1
                stg = self.stg[si]
                self.S.dma('sp', stg[:, 0:nelem], src, writes=['stg%d' % si])
                ce = CAST_PATTERN[self.scnt % len(CAST_PATTERN)]
                if ce == 'act':
                    self.S.op('act', lambda e: e.activation(out=ring[:, 0:nelem], in_=stg[:, 0:nelem], func=AF.Copy),
                              reads=['stg%d' % si], writes=['ring%d' % ri])
                else:
                    self.S.op(ce, lambda e: e.tensor_copy(out=ring[:, 0:nelem], in_=stg[:, 0:nelem]),
                              reads=['stg%d' % si], writes=['ring%d' % ri])
                if cid is not None and len(self.wc_idx) < WC_UNITS:
                    idx = len(self.wc_idx)
                    self.wc_idx[cid] = idx
                    self.S.dma('sp', self.wcache[idx, :, 0:nelem], ring[:, 0:nelem], reads=['ring%d' % ri], writes=['wc%d' % idx])
            state['slots'].append((ring, 'ring%d' % ri))
            state['issued'] += 1

        def nxt():
            while state['issued'] < len(sources) and state['issued'] <= state['taken'] + LOOKAHEAD:
                issue()
            r = state['slots'][state['taken']]
            state['taken'] += 1
            return r
        return nxt

    def flush_w(self, upto=None):
        if upto is None:
            upto = self.wcnt - 1
        while self.wpub <= upto:
            self.S.publish(self.rsem[self.wpub % NRING], self.wready)
            self.wpub += 1

    def build(self):
        nc = self.nc
        with ExitStack() as st:
            self.st = st
            S = self.S = Sched(nc, st)
            self.wcnt = 0
            self.scnt = 0
            self.wc_idx = {}
            self.wcache = nc.dram_tensor('wcache', [WC_UNITS, 128, WBUF], BF16, kind='Internal').ap()
            self.wpub = 0
            self.wready = st.enter_context(nc.semaphore('wready'))
            self.rsem = [st.enter_context(nc.semaphore('r%d' % i)) for i in range(NRING)]
            I = self.I = {}
            I['xT'] = self.din('xT', [D, T])
            I['consts'] = self.din('consts', [128, CONST_ARR.shape[1]])
            I['nw'] = self.din('nw', [128, DEPTH * 6 * DC])
            I['ffn_w_in'] = self.din('ffn_w_in', [self.nl, 2, FC, 128, 2048])
            I['ffn_w_out'] = self.din('ffn_w_out', [self.nl, 2, 8, 2, 128, 11 * 128])
            I['cm_wv'] = self.din('cm_wv', [8, 128, 2048])
            I['cm_wu'] = self.din('cm_wu', [8, 128, 2048])
            I['cm_wo'] = self.din('cm_wo', [8, 128, 2048])
            I['cm_bv'] = self.din('cm_bv', [128, 2048])
            I['cm_lnw'] = self.din('cm_lnw', [128, 2048])
            I['cm_lnb'] = self.din('cm_lnb', [128, 2048])
            I['cm_cols'] = self.din('cm_cols', [128, 48])
            I['cm_wsT'] = self.din('cm_wsT', [128, 8, 128])
            I['cm_wsblk'] = self.din('cm_wsblk', [128, 8, 128])
            I['cm_bs'] = self.din('cm_bs', [128, 8, 128])
            I['cm_bss'] = self.din('cm_bss', [128, 8, 128])
            I['ss_wz'] = self.din('ss_wz', [8, 128, 2048])
            I['ss_wx'] = self.din('ss_wx', [12, 128, 2048])
            I['ss_wdt'] = self.din('ss_wdt', [128, 256])
            I['ss_wo'] = self.din('ss_wo', [8, 128, 2048])
            I['ss_cols'] = self.din('ss_cols', [128, 120])
            I['ss_nwc'] = self.din('ss_nwc', [128, 16])
            I['ss_rows'] = self.din('ss_rows', [128, 96])
            I['ss_s0T'] = self.din('ss_s0T', [NS, 128, 2048])
            I['ss_cs'] = self.din('ss_cs', [128, 24 * 48])
            I['gd_wx'] = self.din('gd_wx', [2, 12, 128, 2048])
            I['gd_wg'] = self.din('gd_wg', [2, 4, 128, 2048])
            I['gd_wab'] = self.din('gd_wab', [2, 128, 128])
            I['gd_wo'] = self.din('gd_wo', [2, 8, 128, 1024])
            I['gd_cols'] = self.din('gd_cols', [2, 128, 96])
            I['gd_rows'] = self.din('gd_rows', [2, 128, 16])
            I['gd_nwc'] = self.din('gd_nwc', [2, 128, 1])
            I['gd_s0'] = self.din('gd_s0', [2, NS, 8, 128, 128])
            I['gd_cs'] = self.din('gd_cs', [2, 128, 24 * 48])
            O = self.O = {}
            O['yT'] = self.dout('yT', [D, T])
            O['gdn_state_p'] = self.dout('gdn_state_p', [2, 8, 128, 128])
            O['gdn_state_s'] = self.dout('gdn_state_s', [2, NS, 8, 128, 128])
            O['gdn_conv_p'] = self.dout('gdn_conv_p', [2, 128, 72])
            O['gdn_conv_s'] = self.dout('gdn_conv_s', [2, 128, 24 * 48])
            O['ssd_pT'] = self.dout('ssd_pT', [128, 2048])
            O['ssd_sT'] = self.dout('ssd_sT', [NS, 128, 2048])
            O['ssd_conv_p'] = self.dout('ssd_conv_p', [128, 72])
            O['ssd_conv_s'] = self.dout('ssd_conv_s', [128, 24 * 48])
            O['cmlp_v'] = self.dout('cmlp_v', [NS * LS, 2048])
            base0 = nc.sbuf_base
            self.arena0 = (base0 + 31) // 32 * 32
            total = nc.sbuf_top - self.arena0 - 64
            self.arena = st.enter_context(nc.sbuf_tensor("arena", [128, total], U8))
            off = 0
            self.X = self.sb("X", [128, DC, T], F32, off); off += DC * T * 4
            self.ring = []
            for i in range(NRING):
                self.ring.append(self.sb("ring%d" % i, [128, WBUF], BF16, off)); off += WBUF * 2
            self.stg = []
            for i in range(NSTG):
                self.stg.append(self.sb("stg%d" % i, [128, WBUF], F32, off)); off += WBUF * 4
            ncst = CONST_ARR.shape[1]
            self.cst = self.sb("cst", [128, ncst], F32, off); off += ncst * 4
            self.C = {n: self.cst[:, k * 128:(k + 1) * 128] for k, n in enumerate(CONST_NAMES)}
            self.ones_bf = self.sb("ones_bf", [128, 128], BF16, off); off += 256
            self.ident_bf = self.sb("ident_bf", [128, 128], BF16, off); off += 256
            self.nw = self.sb("nw", [128, DEPTH * 6 * DC], F32, off); off += DEPTH * 6 * DC * 4
            self.nwh = self.sb("nwh", [128, DEPTH * 6 * DC], F32, off); off += DEPTH * 6 * DC * 4
            self.epsc = self.sb("epsc", [128, 1], F32, off); off += 32
            self.scr0 = off
            self.scr_size = (total - off) // 64 * 64
            self.scratch = self.sb("scratch", [128, self.scr_size // 4], F32, off)
            self.ps = [st.enter_context(nc.psum_tensor("ps%d" % i, [128, 512], F32)) for i in range(8)]
            block = st.enter_context(nc.Block())
            S.dma('sp', self.cst[:], I['consts'], writes=['cst'])
            S.dma('sp', self.nw[:], I['nw'], writes=['nw'])
            for c in range(DC):
                S.dma('sp', self.X[:, c, :], I['xT'][c * 128:(c + 1) * 128, :], writes=['X'])
            S.op('dve', lambda e: e.tensor_copy(out=self.ones_bf[:], in_=self.C['ones']), reads=['cst'], writes=['ones_bf'])
            S.op('dve', lambda e: e.tensor_copy(out=self.ident_bf[:], in_=self.C['ident']), reads=['cst'], writes=['ident_bf'])
            S.op('dve', lambda e: e.tensor_scalar(out=self.nwh[:], in0=self.nw[:], scalar1=0.5, scalar2=None, op0=ALU.mult),
                 reads=['nw'], writes=['nwh'])
            S.op('dve', lambda e: e.memset(self.epsc[:], EPS), writes=['epsc'])
            S.barrier()
            for (l, part) in self.plan:
                if part == 0:
                    self.ffn(l, 0)
                elif part == 2:
                    self.ffn(l, 1)
                elif l % 3 == 1:
                    self.cmlp(l)
                elif l % 3 == 2:
                    self.ssd(l)
                else:
                    self.gdn(l)
                S.barrier()
            for c in range(DC):
                S.dma('sp', O['yT'][c * 128:(c + 1) * 128, :], self.X[:, c, :], reads=['X'])
            S.final_wait('sp')
            S.finish(block, nc)
        return nc

    def nwcol(self, l, i, c, half=False):
        k = (l * 6 + i) * DC + c
        return (self.nwh if half else self.nw)[:, k:k + 1]

    def scr(self, name, shape, dt, off):
        nb = int(np.prod(shape[1:])) * (2 if dt == BF16 else 4)
        assert off + nb <= self.scr_size, (name, off + nb, self.scr_size)
        assert off % 4 == 0 and nb % 4 == 0
        v = self.scratch[:, off // 4:(off + nb) // 4]
        if dt == BF16:
            v = v.bitcast(BF16)
        if len(shape) == 3:
            v = v.rearrange("p (a b) -> p a b", b=shape[2])
        return v

    def prenorm(self, l, idx, t0, n, hn, sq2, rstd, tag):
        S, X = self.S, self.X
        self.pn = getattr(self, 'pn', 0) + 1
        st_ps = self.ps[6 + self.pn % 2]
        kst = 'ps%d' % (6 + self.pn % 2)
        S.op('act', lambda e: e.activation(out=hn[:, :, 0:n], in_=X[:, :, t0:t0 + n], func=AF.Square), reads=['X'], writes=[tag + 'hn'])
        for c in range(DC):
            S.op('pe', lambda e, c=c: e.matmul(st_ps[:, 0:n], lhsT=self.ones_bf[:], rhs=hn[:, c, 0:n], start=(c == 0), stop=(c == DC - 1)),
                 reads=[tag + 'hn', 'ones_bf'], writes=[kst], inc=(c == DC - 1))
        S.op('act', lambda e: e.activation(out=rstd[:, 0:n], in_=st_ps[:, 0:n], func=AF.Ln, scale=1.0 / D, bias=self.epsc[:]),
             reads=[kst, 'epsc'], writes=[tag + 'rstd'])
        S.op('act', lambda e: e.activation(out=rstd[:, 0:n], in_=rstd[:, 0:n], func=AF.Exp, scale=-0.5),
             reads=[tag + 'rstd'], writes=[tag + 'rstd'])
        for c in range(DC):
            S.op('dve', lambda e, c=c: e.scalar_tensor_tensor(
                out=hn[:, c, 0:n], in0=X[:, c, t0:t0 + n], scalar=self.nwcol(l, idx, c), in1=rstd[:, 0:n],
                op0=ALU.mult, op1=ALU.mult), reads=['X', 'nw', tag + 'rstd'], writes=[tag + 'hn'])

    def outproj_residual(self, l, idx, t0, n, wnext, rhs_fn, nk, rhs_key, ysb, sqb, rstd, tag, half=False):
        S, X = self.S, self.X
        for d in range(DC):
            wb, wkey = wnext()
            self.yc = getattr(self, 'yc', 0) + 1
            yi = self.yc % 2
            yps = self.ps[yi]
            for k in range(nk):
                S.op('pe', lambda e, k=k, wb=wb, yps=yps: e.matmul(yps[:, 0:n], lhsT=wb[:, k * 128:(k + 1) * 128], rhs=rhs_fn(k), start=(k == 0), stop=(k == nk - 1)),
                     reads=[wkey, rhs_key], writes=['ps%d' % yi], inc=(k == nk - 1))
            S.op('dve', lambda e, d=d, yps=yps: e.tensor_copy(out=ysb[:, d, 0:n], in_=yps[:, 0:n]), reads=['ps%d' % yi], writes=[tag + 'ysb'])
            S.op('act', lambda e, d=d: e.activation(out=sqb[:, d, 0:n], in_=ysb[:, d, 0:n], func=AF.Square), reads=[tag + 'ysb'], writes=[tag + 'sqb'])
        self.pn = getattr(self, 'pn', 0) + 1
        st_ps = self.ps[6 + self.pn % 2]
        kst = 'ps%d' % (6 + self.pn % 2)
        for d in range(DC):
            S.op('pe', lambda e, d=d: e.matmul(st_ps[:, 0:n], lhsT=self.ones_bf[:], rhs=sqb[:, d, 0:n], start=(d == 0), stop=(d == DC - 1)),
                 reads=[tag + 'sqb', 'ones_bf'], writes=[kst], inc=(d == DC - 1))
        S.op('act', lambda e: e.activation(out=rstd[:, 0:n], in_=st_ps[:, 0:n], func=AF.Ln, scale=1.0 / D, bias=self.epsc[:]),
             reads=[kst, 'epsc'], writes=[tag + 'rstd'])
        S.op('act', lambda e: e.activation(out=rstd[:, 0:n], in_=rstd[:, 0:n], func=AF.Exp, scale=-0.5),
             reads=[tag + 'rstd'], writes=[tag + 'rstd'])
        for d in range(DC):
            S.op('dve', lambda e, d=d: e.tensor_tensor(out=ysb[:, d, 0:n], in0=ysb[:, d, 0:n], in1=rstd[:, 0:n], op=ALU.mult),
                 reads=[tag + 'ysb', tag + 'rstd'], writes=[tag + 'ysb'])
            S.op('dve', lambda e, d=d: e.scalar_tensor_tensor(
                out=X[:, d, t0:t0 + n], in0=ysb[:, d, 0:n], scalar=self.nwcol(l, idx, d, half=half), in1=X[:, d, t0:t0 + n],
                op0=ALU.mult, op1=ALU.add), reads=[tag + 'ysb', 'nw', 'nwh', 'X'], writes=['X'])

    def proj_conv(self, tag, hn, P, xc, cols, wnext, samp, has_bias, cn):
        S, ps = self.S, self.ps
        N = 128
        pend = []

        def silu(c):
            if has_bias:
                S.op('act', lambda e, c=c: e.activation(out=xc[:, c, :], in_=xc[:, c, :], func=AF.Silu, bias=cols[:, c, 4:5], scale=1.0),
                     reads=[tag + '_xc%d' % c, tag + '_cols'], writes=[tag + '_xc%d' % c])
            else:
                S.op('act', lambda e, c=c: e.activation(out=xc[:, c, :], in_=xc[:, c, :], func=AF.Silu), reads=[tag + '_xc%d' % c], writes=[tag + '_xc%d' % c])
        for u in range(12):
            wb, wkey = wnext()
            wv = wb[:, 0:2048].rearrange("p (c n) -> p c n", n=256)
            for jj in range(2):
                ch = 2 * u + jj
                cn[0] += 1
                bi_ = cn[0] % 2
                b = ps[bi_]
                for c in range(DC):
                    S.op('pe', lambda e, c=c, jj=jj, b=b, wv=wv: e.matmul(b[:, 0:N], lhsT=wv[:, c, jj * 128:(jj + 1) * 128], rhs=hn[:, c, :], start=(c == 0), stop=(c == DC - 1)),
                         reads=[wkey, tag + '_hn'], writes=['ps%d' % bi_], inc=(c == DC - 1))
                if samp:
                    S.op('act', lambda e, ch=ch, b=b: e.activation(out=P[:, ch, :].rearrange("p (s k) -> p s k", k=11)[:, :, 3:11],
                                                                 in_=b[:, 0:N].rearrange("p (s k) -> p s k", k=8), func=AF.Copy),
                         reads=['ps%d' % bi_], writes=[tag + '_P%d' % ch])
                else:
                    S.op('act', lambda e, ch=ch, b=b: e.activation(out=P[:, ch, 3:3 + N], in_=b[:, 0:N], func=AF.Copy),
                         reads=['ps%d' % bi_], writes=[tag + '_P%d' % ch])
            for jj in range(2):
                c = 2 * u + jj
                if samp:
                    pv = lambda j, c=c: P[:, c, :].rearrange("p (s k) -> p s k", k=11)[:, :, j:j + 8]
                    ov = xc[:, c, :].rearrange("p (s k) -> p s k", k=8)
                else:
                    pv = lambda j, c=c: P[:, c, j:j + N]
                    ov = xc[:, c, :]
                S.op('act', lambda e, c=c, pv=pv, ov=ov: e.activation(out=ov, in_=pv(0), func=AF.Copy, scale=cols[:, c, 0:1]),
                     reads=[tag + '_P', tag + '_P%d' % c, tag + '_cols'], writes=[tag + '_xc%d' % c])
                for j in range(1, 4):
                    S.op('dve', lambda e, c=c, j=j, pv=pv, ov=ov: e.scalar_tensor_tensor(out=ov, in0=pv(j), scalar=cols[:, c, j:j + 1], in1=ov, op0=ALU.mult, op1=ALU.add),
                         reads=[tag + '_P', tag + '_P%d' % c, tag + '_cols', tag + '_xc%d' % c], writes=[tag + '_xc%d' % c])
            for c in pend:
                silu(c)
            pend = [2 * u, 2 * u + 1]
        for c in pend:
            silu(c)

    def cmlp(self, l):
        S, I, ps = self.S, self.I, self.ps
        KB = 1024
        hn = self.scr("hn", [128, DC, 512], BF16, 0)
        vt = self.scr("vt", [128, 4, 2048], F32, 8 * KB)
        um_p = self.scr("um", [128, 16, 512], BF16, 8 * KB)
        ysb_p = self.scr("ysb", [128, DC, 512], F32, 24 * KB)
        vbf = self.scr("vbf", [128, 4, 2048], BF16, 40 * KB)
        um_s = self.scr("ums", [128, 16, 128], BF16, 44 * KB)
        ysb_s = self.scr("ysbs", [128, DC, 128], F32, 48 * KB)
        lnw_t = self.scr("lnwt", [128, 2048], F32, 16 * KB)
        lnb_t = self.scr("lnbt", [128, 2048], F32, 24 * KB)
        vout = self.scr("vout", [128, 2048], F32, 32 * KB)
        bv = self.scr("bv", [128, 2048], F32, 56 * KB)
        R = self.scr("R", [128, 16, 128], F32, 64 * KB)
        ws_bf = self.scr("wsbf", [128, 8, 128], BF16, 72 * KB)
        wsb_bf = self.scr("wsbbf", [128, 8, 128], BF16, 74 * KB)
        cols = self.scr("cols", [128, 48], F32, 76 * KB)
        ug = self.scr("ug", [128, 512], F32, 76 * KB + 256)
        rstd = self.scr("rstd", [128, 512], F32, 78 * KB + 256)
        mb = self.scr("mb", [128, 512], F32, 80 * KB + 256)
        sq2 = self.scr("sq2", [128, 2, 512], BF16, 82 * KB + 256)
        bst = self.scr("bst", [128, 4, 6], F32, 84 * KB + 256)
        mv = self.scr("mv", [128, 2], F32, 84 * KB + 384)
        rs1 = self.scr("rs1", [128, 1], F32, 84 * KB + 416)
        ws_st = self.scr("wsst", [128, 8, 128], F32, 8 * KB)
        bs_st = self.scr("bsst", [128, 8, 128], F32, 12 * KB)
        S.barrier()
        S.dma('sp', bv[:], I['cm_bv'], writes=['cm_bv'], fence=True)
        S.dma('sp', cols[:], I['cm_cols'], writes=['cm_cols'], fence=True)

        def setup_R(ws_src, bs_src, mask, wdst, tagk):
            S.dma('sp', ws_st[:], ws_src, writes=['cm_wsst'], fence=True)
            S.dma('sp', bs_st[:], bs_src, writes=['cm_bsst'], fence=True)
            S.op('dve', lambda e: e.tensor_tensor(out=wdst[:], in0=ws_st[:], in1=mask.unsqueeze(1).broadcast_to([128, 8, 128]), op=ALU.mult),
                 reads=['cm_wsst', 'cst'], writes=[tagk])
            for h in range(8):
                b = ps[2 + h // 4]
                S.op('pe', lambda e, h=h, b=b: e.matmul(b[:, (h % 4) * 128:(h % 4 + 1) * 128], lhsT=self.ones_bf[:], rhs=wdst[:, h, :], start=True, stop=True),
                     reads=[tagk, 'ones_bf'], writes=['ps%d' % (2 + h // 4)], inc=True)
            for fc in range(16):
                h = fc // 2
                b = ps[2 + h // 4]
                S.op('dve', lambda e, fc=fc, h=h, b=b: e.scalar_tensor_tensor(
                    out=R[:, fc, :], in0=b[:, (h % 4) * 128:(h % 4 + 1) * 128], scalar=cols[:, 32 + fc:33 + fc], in1=bs_st[:, h, :],
                    op0=ALU.mult, op1=ALU.add), reads=['ps%d' % (2 + h // 4), 'cm_cols', 'cm_bsst'], writes=['cm_R'])

        setup_R(I['cm_wsT'], I['cm_bs'], self.C['B'], ws_bf, 'cm_ws')
        srcs = []
        for _ in BLOCKS:
            srcs += [(I['cm_wv'][q], 2048, ('cv', q)) for q in range(8)]
            srcs += [(I['cm_wu'][q], 2048, ('cu', q)) for q in range(8)]
            srcs += [(I['cm_wo'][d], 2048, ('co', d)) for d in range(8)]
        wnext = self.wstream(srcs)
        vcl = [0]

        def do_block(bi, t0, n):
            samp = (t0 >= SEQ)
            NB = n // 128
            S.barrier()
            if samp:
                setup_R(I['cm_wsblk'], I['cm_bss'], self.C['Mblk'], wsb_bf, 'cm_wsb')
                S.barrier()
                S.dma('sp', lnw_t[:], I['cm_lnw'], writes=['cm_lnwt'], fence=True)
                S.dma('sp', lnb_t[:], I['cm_lnb'], writes=['cm_lnbt'], fence=True)
            wmix = wsb_bf if samp else ws_bf
            wmk = 'cm_wsb' if samp else 'cm_ws'
            um = um_s if samp else um_p
            ysb = ysb_s if samp else ysb_p
            self.prenorm(l, 2, t0, n, hn, sq2, rstd, 'cm_')
            for q in range(8):
                wb, wkey = wnext()
                wv = wb[:, 0:2048].rearrange("p (c n) -> p c n", n=256)
                for tt in range(NB):
                    vi = vcl[0] % 2
                    vcl[0] += 1
                    vps = ps[vi]
                    for c in range(DC):
                        S.op('pe', lambda e, c=c, tt=tt, vps=vps, wv=wv: e.matmul(vps[:, 0:256], lhsT=hn[:, c, tt * 128:(tt + 1) * 128], rhs=wv[:, c, :], start=(c == 0), stop=(c == DC - 1)),
                             reads=[wkey, 'cm_hn'], writes=['ps%d' % vi], inc=(c == DC - 1))
                    S.op('dve', lambda e, q=q, tt=tt, vps=vps: e.tensor_tensor(out=vt[:, tt, q * 256:(q + 1) * 256], in0=vps[:, 0:256], in1=bv[:, q * 256:(q + 1) * 256], op=ALU.add),
                         reads=['ps%d' % vi, 'cm_bv'], writes=['cm_vt%d' % tt])
                    S.op('act', lambda e, q=q, tt=tt: e.activation(out=vt[:, tt, q * 256:(q + 1) * 256], in_=vt[:, tt, q * 256:(q + 1) * 256], func=AF.Gelu),
                         reads=['cm_vt%d' % tt], writes=['cm_vt%d' % tt])
            for tt in range(NB):
                for g in range(4):
                    S.op('dve', lambda e, tt=tt, g=g: e.bn_stats(out=bst[:, g, :], in_=vt[:, tt, g * 512:(g + 1) * 512]),
                         reads=['cm_vt%d' % tt], writes=['cm_bst'])
                S.op('dve', lambda e: e.bn_aggr(out=mv[:], in_=bst[:].rearrange("p a b -> p (a b)")), reads=['cm_bst'], writes=['cm_mv'])
                S.op('act', lambda e: e.activation(out=rs1[:], in_=mv[:, 1:2], func=AF.Ln, scale=1.0, bias=self.epsc[:]),
                     reads=['cm_mv', 'epsc'], writes=['cm_rs1'])
                S.op('act', lambda e: e.activation(out=rs1[:], in_=rs1[:], func=AF.Exp, scale=-0.5),
                     reads=['cm_rs1'], writes=['cm_rs1'])
                S.op('dve', lambda e, tt=tt: e.tensor_scalar(out=vbf[:, tt, :], in0=vt[:, tt, :], scalar1=mv[:, 0:1], scalar2=rs1[:], op0=ALU.subtract, op1=ALU.mult),
                     reads=['cm_vt%d' % tt, 'cm_mv', 'cm_rs1'], writes=['cm_vbf'])
                if samp:
                    S.op('dve', lambda e, tt=tt: e.tensor_scalar(out=vout[:], in0=vt[:, tt, :], scalar1=mv[:, 0:1], scalar2=rs1[:], op0=ALU.subtract, op1=ALU.mult),
                         reads=['cm_vt%d' % tt, 'cm_mv', 'cm_rs1'], writes=['cm_vout'])
                    S.op('dve', lambda e: e.tensor_tensor(out=vout[:], in0=vout[:], in1=lnw_t[:], op=ALU.mult), reads=['cm_vout', 'cm_lnwt'], writes=['cm_vout'])
                    S.op('dve', lambda e: e.tensor_tensor(out=vout[:], in0=vout[:], in1=lnb_t[:], op=ALU.add), reads=['cm_vout', 'cm_lnbt'], writes=['cm_vout'])
                    S.dma('sp', self.O['cmlp_v'], vout[:], reads=['cm_vout'])
            for q in range(8):
                wb, wkey = wnext()
                wv = wb[:, 0:2048].rearrange("p (c n) -> p c n", n=256)
                for jj in range(2):
                    fc = 2 * q + jj
                    mi = fc % 2
                    mps, ups = ps[2 + mi], ps[4 + mi]
                    for tt in range(NB):
                        S.op('pe', lambda e, tt=tt, fc=fc, q=q, mps=mps: e.matmul(mps[:, tt * 128:(tt + 1) * 128], lhsT=vbf[:, tt, fc * 128:(fc + 1) * 128], rhs=wmix[:, q, :], start=True, stop=True),
                             reads=['cm_vbf', wmk], writes=['ps%d' % (2 + mi)], inc=(tt == NB - 1))
                    for c in range(DC):
                        S.op('pe', lambda e, c=c, jj=jj, ups=ups, wv=wv: e.matmul(ups[:, 0:n], lhsT=wv[:, c, jj * 128:(jj + 1) * 128], rhs=hn[:, c, 0:n], start=(c == 0), stop=(c == DC - 1)),
                             reads=[wkey, 'cm_hn'], writes=['ps%d' % (4 + mi)], inc=(c == DC - 1))
                    S.op('act', lambda e, fc=fc, ups=ups: e.activation(out=ug[:, 0:n], in_=ups[:, 0:n], func=AF.Gelu, bias=cols[:, fc:fc + 1], scale=1.0),
                         reads=['ps%d' % (4 + mi), 'cm_cols'], writes=['cm_ug'])
                    S.op('dve', lambda e, fc=fc, mps=mps: e.scalar_tensor_tensor(
                        out=mb[:, 0:n].rearrange("p (a b) -> p a b", b=128), in0=mps[:, 0:n].rearrange("p (a b) -> p a b", b=128),
                        scalar=cols[:, 16 + fc:17 + fc], in1=R[:, fc:fc + 1, :].broadcast_to([128, NB, 128]),
                        op0=ALU.mult, op1=ALU.add), reads=['ps%d' % (2 + mi), 'cm_cols', 'cm_R'], writes=['cm_mb'])
                    S.op('dve', lambda e, fc=fc: e.tensor_tensor(out=um[:, fc, 0:n], in0=ug[:, 0:n], in1=mb[:, 0:n], op=ALU.mult),
                         reads=['cm_ug', 'cm_mb'], writes=['cm_um'])
            self.outproj_residual(l, 3, t0, n, wnext, lambda k: um[:, k, 0:n], 16, 'cm_um', ysb, hn, rstd, 'cm_')

        for bi, (t0, n) in enumerate(BLOCKS):
            do_block(bi, t0, n)

    def ssd(self, l):
        S, I, O, ps, C = self.S, self.I, self.O, self.ps, self.C
        N = 128
        o = [0]

        def take(name, shape, dt):
            v = self.scr(name, shape, dt, o[0])
            o[0] += (int(np.prod(shape[1:])) * (2 if dt == BF16 else 4) + 63) // 64 * 64
            return v
        hn = take("hn", [128, DC, N], BF16)
        p0 = o[0]
        P = take("P", [128, 24, 176], F32)
        p1 = o[0]
        xc = take("xc", [128, 24, N], F32)
        xtok = take("xtok", [128, 2048], F32)
        zt = take("zt", [128, 2048], F32)
        yacc = take("yacc", [128, 2048], F32)
        tmpf = take("tmpf", [128, 2048], F32)
        ST = take("ST", [128, 2048], F32)
        ST_bf = take("STbf", [128, 2048], BF16)
        cols = take("cols", [128, 24, 5], F32)
        nwc = take("nwc", [128, 16], F32)
        rows = take("rows", [128, 3, 32], F32)
        dtt = take("dtt", [128, 32], F32)
        ga = take("ga", [128, 32], F32)
        Gt = take("Gt", [128, 32], F32)
        Et = take("Et", [128, 32], F32)
        dec = take("dec", [128, 32], F32)
        cdbc = take("cdbc", [128, 32], F32)
        arow = take("arow", [128, 32], F32)
        ss4 = take("ss4", [128, 4], F32)
        cdall = take("cdall", [128, 16, 32], F32)
        carry = take("carry", [128, 24, 3], F32)
        rstd = take("rstd", [128, N], F32)
        sq2 = take("sq2", [128, 2, N], BF16)
        stg2 = take("stg2", [128, 2, 512], F32)
        sbf2 = take("sbf2", [128, 2, 512], BF16)
        o[0] = p0
        ynT = take("ynT", [128, 16, N], BF16)
        ysb = take("ysb", [128, DC, N], F32)
        MT = take("MT", [128, 8, N], BF16)
        BT_bf = take("BTbf", [128, 4, N], BF16)
        CT_bf = take("CTbf", [128, 4, N], BF16)
        Btok = take("Btok", [128, 4, N], BF16)
        cbTm = take("cbTm", [128, N], F32)
        gsel = take("gsel", [128, 16, 32], F32)
        Bm2 = take("Bm2", [128, 2, N], BF16)
        assert o[0] <= p1
        o[0] = p1
        xdt = take("xdt", [128, 2048], BF16)
        xdtd = take("xdtd", [128, 2048], BF16)
        CTm = take("CTm", [128, 16, N], BF16)
        assert o[0] <= p1 + 24 * N * 4
        tP = tmpf[:, 0:1152].rearrange("p (c k) -> p c k", k=48)
        gB = tmpf[:, 0:1024].rearrange("p (r i) -> p r i", i=N)
        tY = tmpf[:, 1024:1536]
        outb = tmpf[:, 1536:2048]

        S.barrier()
        S.dma('sp', cols[:], I['ss_cols'].rearrange("p (c k) -> p c k", k=5), writes=['ss_cols'], fence=True)
        S.dma('sp', nwc[:], I['ss_nwc'], writes=['ss_nwc'], fence=True)
        S.dma('sp', rows[:], I['ss_rows'].rearrange("p (a b) -> p a b", b=32), writes=['ss_rows'], fence=True)
        S.op('act', lambda e: e.activation(out=arow[:], in_=rows[:, 0, :], func=AF.Exp), reads=['ss_rows'], writes=['ss_arow'])
        S.op('dve', lambda e: e.tensor_scalar(out=arow[:], in0=arow[:], scalar1=-1.0, scalar2=None, op0=ALU.mult), reads=['ss_arow'], writes=['ss_arow'])
        S.op('dve', lambda e: e.memset(ST[:], 0.0), writes=['ss_ST'])
        S.op('dve', lambda e: e.memset(ST_bf[:], 0.0), writes=['ss_STbf'])
        S.op('dve', lambda e: e.memset(carry[:], 0.0), writes=['ss_carry'])
        srcs = []
        for _ in BLOCKS128:
            srcs += [(I['ss_wx'][u], 2048, ('sx', u)) for u in range(12)]
            srcs += [(I['ss_wdt'], 256, ('sd',))]
            srcs += [(I['ss_wz'][q], 2048, ('sz', q)) for q in range(8)]
            srcs += [(I['ss_wo'][d], 2048, ('so', d)) for d in range(8)]
        wnext = self.wstream(srcs)
        cn = [0]

        def do_block(bi, t0):
            samp = t0 >= SEQ
            last_p = (t0 == SEQ - N)
            Am, Bmk, Sel = (C['Ablk'], C['Mblk'], C['SelLastS']) if samp else (C['A'], C['B'], C['SelLast'])
            S.barrier()
            self.prenorm(l, 2, t0, N, hn, sq2, rstd, 'ss_')
            if samp:
                S.dma('sp', tP, I['ss_cs'].rearrange("p (c k) -> p c k", k=48), writes=['ss_tmpf'], fence=True)
                for c in range(24):
                    S.op('dve', lambda e, c=c: e.tensor_copy(out=P[:, c, :].rearrange("p (s k) -> p s k", k=11)[:, :, 0:3],
                                                            in_=tP[:, c, :].rearrange("p (s k) -> p s k", k=3)),
                         reads=['ss_tmpf'], writes=['ss_P'])
            else:
                S.op('dve', lambda e: e.tensor_copy(out=P[:, :, 0:3], in_=carry[:]), reads=['ss_carry'], writes=['ss_P'])
            self.proj_conv('ss', hn, P, xc, cols, wnext, samp, True, cn)
            if samp:
                for c in range(24):
                    S.op('dve', lambda e, c=c: e.tensor_copy(out=tP[:, c, :].rearrange("p (s k) -> p s k", k=3),
                                                            in_=P[:, c, :].rearrange("p (s k) -> p s k", k=11)[:, :, 8:11]),
                         reads=['ss_P'] + ['ss_P%d' % c_ for c_ in range(24)], writes=['ss_tmpf'])
                S.dma('sp', O['ssd_conv_s'].rearrange("p (c k) -> p c k", k=48), tP, reads=['ss_tmpf'])
                S.barrier()
            else:
                S.op('dve', lambda e: e.tensor_copy(out=carry[:], in_=P[:, :, N:N + 3]), reads=['ss_P'] + ['ss_P%d' % c_ for c_ in range(24)], writes=['ss_carry'])
                if last_p:
                    S.dma('sp', O['ssd_conv_p'].rearrange("p (c k) -> p c k", k=3), carry[:], reads=['ss_carry'])
            xck = ['ss_xc%d' % c for c in range(24)]
            for q in range(4):
                b = ps[2 + q % 2]
                for k in range(4):
                    S.op('pe', lambda e, q=q, k=k, b=b: e.transpose(b[:, k * 128:(k + 1) * 128], xc[:, 4 * q + k, :], C['ident']),
                         reads=xck[4 * q:4 * q + 4] + ['cst'], writes=['ps%d' % (2 + q % 2)], inc=(k == 3))
                S.op('act', lambda e, q=q, b=b: e.activation(out=xtok[:, q * 512:(q + 1) * 512], in_=b[:], func=AF.Copy),
                     reads=['ps%d' % (2 + q % 2)], writes=['ss_xtok'])
            b = ps[2]
            for k in range(4):
                S.op('pe', lambda e, k=k, b=b: e.transpose(b[:, k * 128:(k + 1) * 128], xc[:, 16 + k, :], C['ident']),
                     reads=xck[16:20] + ['cst'], writes=['ps2'], inc=(k == 3))
            S.op('act', lambda e, b=b: e.activation(out=Btok[:].rearrange("p a b -> p (a b)"), in_=b[:], func=AF.Copy), reads=['ps2'], writes=['ss_Btok'])
            S.op('dve', lambda e: e.tensor_copy(out=BT_bf[:], in_=xc[:, 16:20, :]), reads=xck[16:20], writes=['ss_BT'])
            S.op('dve', lambda e: e.tensor_copy(out=CT_bf[:], in_=xc[:, 20:24, :]), reads=xck[20:24], writes=['ss_CT'])
            wb, wkey = wnext()
            wv = wb[:, 0:256].rearrange("p (c n) -> p c n", n=32)
            b = ps[4]
            for c in range(DC):
                S.op('pe', lambda e, c=c, b=b, wv=wv: e.matmul(b[:, 0:32], lhsT=hn[:, c, :], rhs=wv[:, c, :], start=(c == 0), stop=(c == DC - 1)),
                     reads=[wkey, 'ss_hn'], writes=['ps4'], inc=(c == DC - 1))
            S.op('dve', lambda e, b=b: e.tensor_tensor(out=dtt[:], in0=b[:, 0:32], in1=rows[:, 1, :], op=ALU.add), reads=['ps4', 'ss_rows'], writes=['ss_dtt'])
            S.op('act', lambda e: e.activation(out=dtt[:], in_=dtt[:], func=AF.Exp), reads=['ss_dtt'], writes=['ss_dtt'])
            S.op('act', lambda e: e.activation(out=dtt[:], in_=dtt[:], func=AF.Ln, bias=C['ones'][:, 0:1], scale=1.0), reads=['ss_dtt', 'cst'], writes=['ss_dtt'])
            S.op('dve', lambda e: e.tensor_tensor(out=ga[:], in0=dtt[:], in1=arow[:], op=ALU.mult), reads=['ss_dtt', 'ss_arow'], writes=['ss_ga'])
            S.op('pe', lambda e, b=b: e.matmul(b[:, 32:64], lhsT=Bmk, rhs=ga[:], start=True, stop=True), reads=['cst', 'ss_ga'], writes=['ps4'])
            S.op('dve', lambda e, b=b: e.tensor_copy(out=Gt[:], in_=b[:, 32:64]), reads=['ps4'], writes=['ss_Gt'])
            S.op('act', lambda e: e.activation(out=Et[:], in_=Gt[:], func=AF.Exp), reads=['ss_Gt'], writes=['ss_Et'])
            S.op('pe', lambda e, b=b: e.matmul(b[:, 64:96], lhsT=Sel, rhs=Gt[:], start=True, stop=True), reads=['cst', 'ss_Gt'], writes=['ps4'])
            S.op('act', lambda e, b=b: e.activation(out=cdbc[:], in_=b[:, 64:96], func=AF.Exp), reads=['ps4'], writes=['ss_cdbc'])
            S.op('dve', lambda e, b=b: e.tensor_tensor(out=dec[:], in0=b[:, 64:96], in1=Gt[:], op=ALU.subtract), reads=['ps4', 'ss_Gt', 'ss_cdbc'], writes=['ss_dec'])
            S.op('act', lambda e: e.activation(out=dec[:], in_=dec[:], func=AF.Exp), reads=['ss_dec'], writes=['ss_dec'])
            S.op('dve', lambda e: e.tensor_tensor(out=tmpf[:].rearrange("p (h q) -> p h q", q=64), in0=xtok[:].rearrange("p (h q) -> p h q", q=64),
                                                  in1=dtt[:].unsqueeze(2).broadcast_to([128, 32, 64]), op=ALU.mult),
                 reads=['ss_xtok', 'ss_dtt', 'ss_tmpf'], writes=['ss_tmpf'])
            S.op('act', lambda e: e.activation(out=xdt[:], in_=tmpf[:], func=AF.Copy), reads=['ss_tmpf'] + xck, writes=['ss_xdt'])
            S.op('pool', lambda e: e.tensor_tensor(out=xdtd[:].rearrange("p (h q) -> p h q", q=64), in0=tmpf[:].rearrange("p (h q) -> p h q", q=64),
                                                  in1=dec[:].unsqueeze(2).broadcast_to([128, 32, 64]), op=ALU.mult),
                 reads=['ss_tmpf', 'ss_dec'] + xck, writes=['ss_xdtd'])
            for q in range(8):
                wb, wkey = wnext()
                wv = wb[:, 0:2048].rearrange("p (c n) -> p c n", n=256)
                cn[0] += 1
                bi_ = cn[0] % 2
                b = ps[bi_]
                for c in range(DC):
                    S.op('pe', lambda e, c=c, b=b, wv=wv: e.matmul(b[:, 0:256], lhsT=hn[:, c, :], rhs=wv[:, c, :], start=(c == 0), stop=(c == DC - 1)),
                         reads=[wkey, 'ss_hn'], writes=['ps%d' % bi_], inc=(c == DC - 1))
                S.op('act', lambda e, q=q, b=b: e.activation(out=zt[:, q * 256:(q + 1) * 256], in_=b[:, 0:256], func=AF.Silu), reads=['ps%d' % bi_], writes=['ss_zt'])
            S.op('pool', lambda e: e.tensor_tensor(out=yacc[:].rearrange("p (h q) -> p h q", q=64), in0=xtok[:].rearrange("p (h q) -> p h q", q=64),
                                                  in1=rows[:, 2, :].unsqueeze(2).broadcast_to([128, 32, 64]), op=ALU.mult),
                 reads=['ss_xtok', 'ss_rows'], writes=['ss_yacc'])
            if samp:
                S.op('dve', lambda e: e.tensor_tensor(out=gsel[:], in0=Gt[:].unsqueeze(1).broadcast_to([128, 16, 32]),
                                                      in1=C['SeqSel'][:, 16:32].unsqueeze(2).broadcast_to([128, 16, 32]), op=ALU.mult),
                     reads=['ss_Gt', 'cst'], writes=['ss_gsel'])
                S.op('pe', lambda e: e.matmul(ps[4][:], lhsT=C['ones'], rhs=gsel[:].rearrange("p a b -> p (a b)"), start=True, stop=True),
                     reads=['cst', 'ss_gsel'], writes=['ps4'])
                S.op('act', lambda e: e.activation(out=cdall[:].rearrange("p a b -> p (a b)"), in_=ps[4][:], func=AF.Exp), reads=['ps4'], writes=['ss_cdall'])
            for g in range(4):
                S.op('pe', lambda e, g=g: e.matmul(ps[2][:, 0:N], lhsT=BT_bf[:, g, :], rhs=CT_bf[:, g, :], start=True, stop=True),
                     reads=['ss_BT', 'ss_CT'], writes=['ps2'])
                S.op('dve', lambda e: e.tensor_tensor(out=cbTm[:], in0=ps[2][:, 0:N], in1=Bmk, op=ALU.mult), reads=['ps2', 'cst'], writes=['ss_cbTm'])
                S.op('pool', lambda e, g=g: e.tensor_tensor(out=gB, in0=ga[:, g * 8:(g + 1) * 8].unsqueeze(2).broadcast_to([128, 8, N]),
                                                           in1=Bmk.unsqueeze(1).broadcast_to([128, 8, N]), op=ALU.mult),
                     reads=['ss_ga', 'cst', 'ss_tmpf'], writes=['ss_tmpf'])
                for hh in range(2):
                    S.op('pe', lambda e, hh=hh: e.matmul(ps[3 + hh][:], lhsT=Am,
                                                         rhs=gB[:, hh * 4:(hh + 1) * 4, :].rearrange("p a b -> p (a b)"), start=True, stop=True),
                         reads=['cst', 'ss_tmpf'], writes=['ps%d' % (3 + hh)])
                    S.op('act', lambda e, hh=hh: e.activation(out=gB[:, hh * 4:(hh + 1) * 4, :].rearrange("p a b -> p (a b)"), in_=ps[3 + hh][:], func=AF.Exp),
                         reads=['ps%d' % (3 + hh), 'ss_tmpf'], writes=['ss_tmpf'])
                S.op('dve', lambda e: e.tensor_tensor(out=MT[:], in0=gB, in1=cbTm[:].unsqueeze(1).broadcast_to([128, 8, N]), op=ALU.mult),
                     reads=['ss_tmpf', 'ss_cbTm'], writes=['ss_MT'])
                for r in range(8):
                    h = g * 8 + r
                    S.op('pe', lambda e, r=r, h=h: e.matmul(ps[5][:, r * 64:(r + 1) * 64], lhsT=MT[:, r, :], rhs=xdt[:, h * 64:(h + 1) * 64], start=True, stop=True),
                         reads=['ss_MT', 'ss_xdt'], writes=['ps5'], inc=(r == 7))
                S.op('dve', lambda e, g=g: e.tensor_tensor(out=yacc[:, g * 512:(g + 1) * 512], in0=ps[5][:], in1=yacc[:, g * 512:(g + 1) * 512], op=ALU.add),
                     reads=['ps5', 'ss_yacc'], writes=['ss_yacc'])
                if not samp:
                    S.op('pe', lambda e, g=g: e.matmul(ps[6][:], lhsT=CT_bf[:, g, :], rhs=ST_bf[:, g * 512:(g + 1) * 512], start=True, stop=True),
                         reads=['ss_CT', 'ss_STbf'], writes=['ps6'])
                else:
                    S.op('dve', lambda e: e.memset(CTm[:], 0.0), reads=['ss_CTm'], writes=['ss_CTm'])
                    for s_ in range(NS):
                        S.op('dve', lambda e, g=g, s_=s_: e.tensor_copy(out=CTm[:, s_, s_ * 8:(s_ + 1) * 8], in_=CT_bf[:, g, s_ * 8:(s_ + 1) * 8]),
                             reads=['ss_CT', 'ss_CTm'], writes=['ss_CTm'])
                    for s_ in range(NS):
                        k2 = (g * NS + s_) % 2
                        S.dma('sp', stg2[:, k2, :], I['ss_s0T'][s_, :, g * 512:(g + 1) * 512], writes=['ss_stg%d' % k2])
                        S.op('act', lambda e, k2=k2: e.activation(out=sbf2[:, k2, :], in_=stg2[:, k2, :], func=AF.Copy), reads=['ss_stg%d' % k2], writes=['ss_sbf%d' % k2])
                        S.op('pe', lambda e, s_=s_, k2=k2: e.matmul(ps[6][:], lhsT=CTm[:, s_, :], rhs=sbf2[:, k2, :], start=(s_ == 0), stop=(s_ == NS - 1)),
                             reads=['ss_CTm', 'ss_sbf%d' % k2], writes=['ps6'], inc=True)
                        S.op('dve', lambda e, g=g, s_=s_, k2=k2: e.tensor_scalar(out=Bm2[:, k2, :], in0=Btok[:, g, :], scalar1=C['SeqSel'][:, s_:s_ + 1], scalar2=None, op0=ALU.mult),
                             reads=['ss_Btok', 'cst'], writes=['ss_Bm%d' % k2])
                        S.op('pe', lambda e, g=g, k2=k2: e.matmul(ps[7][:], lhsT=Bm2[:, k2, :], rhs=xdtd[:, g * 512:(g + 1) * 512], start=True, stop=True),
                             reads=['ss_Bm%d' % k2, 'ss_xdtd'], writes=['ps7'])
                        S.op('dve', lambda e, g=g, s_=s_, k2=k2: e.tensor_tensor(out=outb.rearrange("p (r q) -> p r q", q=64), in0=stg2[:, k2, :].rearrange("p (r q) -> p r q", q=64),
                                                                       in1=cdall[:, s_, g * 8:(g + 1) * 8].unsqueeze(2).broadcast_to([128, 8, 64]), op=ALU.mult),
                             reads=['ss_stg%d' % k2, 'ss_cdall', 'ss_outb'], writes=['ss_outb'])
                        S.op('dve', lambda e: e.tensor_tensor(out=outb, in0=ps[7][:], in1=outb, op=ALU.add), reads=['ps7', 'ss_outb'], writes=['ss_outb'])
                        S.dma('sp', O['ssd_sT'][s_, :, g * 512:(g + 1) * 512], outb, reads=['ss_outb'])
                S.op('dve', lambda e, g=g: e.tensor_tensor(out=tY.rearrange("p (r q) -> p r q", q=64), in0=ps[6][:].rearrange("p (r q) -> p r q", q=64),
                                                           in1=Et[:, g * 8:(g + 1) * 8].unsqueeze(2).broadcast_to([128, 8, 64]), op=ALU.mult),
                     reads=['ps6', 'ss_Et', 'ss_tY'], writes=['ss_tY'])
                S.op('dve', lambda e, g=g: e.tensor_tensor(out=yacc[:, g * 512:(g + 1) * 512], in0=tY, in1=yacc[:, g * 512:(g + 1) * 512], op=ALU.add),
                     reads=['ss_tY', 'ss_yacc'], writes=['ss_yacc'])
                if not samp:
                    S.op('pe', lambda e, g=g: e.matmul(ps[7][:], lhsT=Btok[:, g, :], rhs=xdtd[:, g * 512:(g + 1) * 512], start=True, stop=True),
                         reads=['ss_Btok', 'ss_xdtd'], writes=['ps7'])
                    S.op('pool', lambda e, g=g: e.tensor_tensor(out=ST[:, g * 512:(g + 1) * 512].rearrange("p (r q) -> p r q", q=64), in0=ST[:, g * 512:(g + 1) * 512].rearrange("p (r q) -> p r q", q=64),
                                                               in1=cdbc[:, g * 8:(g + 1) * 8].unsqueeze(2).broadcast_to([128, 8, 64]), op=ALU.mult),
                         reads=['ss_ST', 'ss_cdbc', 'ss_STbf'], writes=['ss_ST'])
                    S.op('dve', lambda e, g=g: e.tensor_tensor(out=ST[:, g * 512:(g + 1) * 512], in0=ps[7][:], in1=ST[:, g * 512:(g + 1) * 512], op=ALU.add),
                         reads=['ps7', 'ss_ST'], writes=['ss_ST'])
            if not samp:
                S.op('act', lambda e: e.activation(out=ST_bf[:], in_=ST[:], func=AF.Copy), reads=['ss_ST', 'ps6'], writes=['ss_STbf'])
                if last_p:
                    S.dma('sp', O['ssd_pT'], ST[:], reads=['ss_ST'])
            S.op('dve', lambda e: e.tensor_tensor(out=yacc[:], in0=yacc[:], in1=zt[:], op=ALU.mult), reads=['ss_yacc', 'ss_zt'], writes=['ss_yacc'])
            S.op('act', lambda e: e.activation(out=tmpf[:], in_=yacc[:], func=AF.Square), reads=['ss_yacc', 'ss_tmpf', 'ss_tY', 'ss_outb'], writes=['ss_tmpf'])
            S.op('dve', lambda e: e.tensor_reduce(out=ss4[:], in_=tmpf[:].rearrange("p (g q) -> p g q", q=512), axis=AX.X, op=ALU.add), reads=['ss_tmpf'], writes=['ss_ss4'])
            S.op('act', lambda e: e.activation(out=ss4[:], in_=ss4[:], func=AF.Sqrt, scale=1.0 / 512, bias=self.epsc[:]), reads=['ss_ss4', 'epsc'], writes=['ss_ss4'])
            S.op('dve', lambda e: e.reciprocal(out=ss4[:], in_=ss4[:]), reads=['ss_ss4'], writes=['ss_ss4'])
            S.op('dve', lambda e: e.tensor_tensor(out=yacc[:].rearrange("p (g q) -> p g q", q=512), in0=yacc[:].rearrange("p (g q) -> p g q", q=512),
                                                  in1=ss4[:].unsqueeze(2).broadcast_to([128, 4, 512]), op=ALU.mult), reads=['ss_yacc', 'ss_ss4'], writes=['ss_yacc'])
            for q in range(4):
                b = ps[2 + q % 2]
                for k in range(4):
                    S.op('pe', lambda e, q=q, k=k, b=b: e.transpose(b[:, k * 128:(k + 1) * 128], yacc[:, (4 * q + k) * 128:(4 * q + k + 1) * 128], C['ident']),
                         reads=['ss_yacc', 'cst'], writes=['ps%d' % (2 + q % 2)], inc=(k == 3))
                for k in range(4):
                    fc = 4 * q + k
                    S.op('act', lambda e, fc=fc, k=k, b=b: e.activation(out=ynT[:, fc, :], in_=b[:, k * 128:(k + 1) * 128], func=AF.Copy, scale=nwc[:, fc:fc + 1]),
                         reads=['ps%d' % (2 + q % 2), 'ss_nwc'], writes=['ss_ynT'])
            self.outproj_residual(l, 3, t0, N, wnext, lambda k: ynT[:, k, :], 16, 'ss_ynT', ysb, hn, rstd, 'ss_')

        for bi, (t0, n) in enumerate(BLOCKS128):
            do_block(bi, t0)

    def gdn(self, l):
        S, I, O, ps, C = self.S, self.I, self.O, self.ps, self.C
        jl = l // 3
        N = 128
        o = [0]

        def take(name, shape, dt):
            v = self.scr(name, shape, dt, o[0])
            o[0] += (int(np.prod(shape[1:])) * (2 if dt == BF16 else 4) + 63) // 64 * 64
            return v
        hn = take("hn", [128, DC, N], BF16)
        p0 = o[0]
        P = take("P", [128, 24, 176], F32)
        p1 = o[0]
        xc = take("xc", [128, 24, N], F32)
        p2 = o[0]
        gt = take("gt", [128, 8, N], F32)
        qT = take("qT", [128, 8, N], BF16)
        kT = take("kT", [128, 8, N], BF16)
        kdec = take("kdec", [128, 8, N], BF16)
        bV = take("bV", [128, 8, N], F32)
        gB = take("gB", [128, 8, N], F32)
        gam = take("gam", [128, 8, N], F32)
        attnT = take("attnT", [128, 8, N], BF16)
        mM = take("mM", [128, 8, N], F32)
        nM = take("nM", [128, 8, N], F32)
        Rm = take("Rm", [128, 8, N], F32)
        vnb = take("vnb", [128, 8, N], BF16)
        Sst = take("Sst", [128, 8, N], F32)
        Sbf = take("Sbf", [128, 8, N], BF16)
        onT = take("onT", [128, 8, N], BF16)
        ysb = take("ysb", [128, DC, N], F32)
        cols = take("cols", [128, 24, 4], F32)
        rows = take("rows", [128, 2, 8], F32)
        nwcol = take("nwcol", [128, 1], F32)
        negA = take("negA", [128, 8], F32)
        gsc = take("gsc", [128, 8], F32)
        beta = take("beta", [128, 8], F32)
        Gt = take("Gt", [128, 8], F32)
        Et = take("Et", [128, 8], F32)
        dec = take("dec", [128, 8], F32)
        cdbc = take("cdbc", [128, 8], F32)
        bE = take("bE", [128, 8], F32)
        ss8 = take("ss8", [128, 8], F32)
        cdall = take("cdall", [128, 16, 8], F32)
        gsel = take("gsel", [128, 16, 8], F32)
        carry = take("carry", [128, 24, 3], F32)
        rstd = take("rstd", [128, N], F32)
        sq2 = take("sq2", [128, 2, N], BF16)
        kdm = take("kdm", [128, 2, N], BF16)
        o[0] = p0
        Pa = take("Pa", [128, 8, N], F32)
        PTa = take("PTa", [128, 8, N], F32)
        Pb = take("Pb", [128, 8, N], F32)
        PTb = take("PTb", [128, 8, N], F32)
        assert o[0] <= p1
        o[0] = p0
        S0h = take("S0h", [128, 16, N], F32)
        assert o[0] <= p1
        o[0] = p1
        rr = take("rr", [128, 8, N], F32)
        oo = take("oo", [128, 8, N], F32)
        tmp4 = take("tmp4", [128, 8, N], F32)
        assert o[0] <= p2
        S.barrier()
        S.dma('sp', cols[:], I['gd_cols'][jl].rearrange("p (c k) -> p c k", k=4), writes=['gd_cols'], fence=True)
        S.dma('sp', rows[:], I['gd_rows'][jl].rearrange("p (a b) -> p a b", b=8), writes=['gd_rows'], fence=True)
        S.dma('sp', nwcol[:], I['gd_nwc'][jl], writes=['gd_nwc'], fence=True)
        S.op('act', lambda e: e.activation(out=negA[:], in_=rows[:, 0, :], func=AF.Exp), reads=['gd_rows'], writes=['gd_negA'])
        S.op('dve', lambda e: e.tensor_scalar(out=negA[:], in0=negA[:], scalar1=-1.0, scalar2=None, op0=ALU.mult), reads=['gd_negA'], writes=['gd_negA'])
        S.op('dve', lambda e: e.memset(Sst[:], 0.0), writes=['gd_S'])
        S.op('dve', lambda e: e.memset(Sbf[:], 0.0), writes=['gd_Sbf'])
        S.op('dve', lambda e: e.memset(carry[:], 0.0), writes=['gd_carry'])
        srcs = []
        for _ in BLOCKS128:
            srcs += [(I['gd_wx'][jl, u], 2048, ('gx', jl, u)) for u in range(12)]
            srcs += [(I['gd_wab'][jl], 128, ('ga', jl))]
            srcs += [(I['gd_wg'][jl, q], 2048, ('gg', jl, q)) for q in range(4)]
            srcs += [(I['gd_wo'][jl, d], 1024, ('go', jl, d)) for d in range(8)]
        wnext = self.wstream(srcs)
        cn = [0]
        fl = lambda t: t[:].rearrange("p a b -> p (a b)")

        def do_block(bi, t0):
            samp = t0 >= SEQ
            last_p = (t0 == SEQ - N)
            Am, Bmk, Sel = (C['Ablk'], C['Mblk'], C['SelLastS']) if samp else (C['A'], C['B'], C['SelLast'])
            nsq = 2 if samp else 6
            S.barrier()
            self.prenorm(l, 2, t0, N, hn, sq2, rstd, 'gd_')
            tP = self.scr("tP", [128, 24, 48], F32, self._off(gB))
            if samp:
                S.dma('sp', tP, I['gd_cs'][jl].rearrange("p (c k) -> p c k", k=48), writes=['gd_gB', 'gd_gam'], fence=True)
                for c in range(24):
                    S.op('dve', lambda e, c=c: e.tensor_copy(out=P[:, c, :].rearrange("p (s k) -> p s k", k=11)[:, :, 0:3],
                                                            in_=tP[:, c, :].rearrange("p (s k) -> p s k", k=3)),
                         reads=['gd_gB', 'gd_gam'], writes=['gd_P'])
            else:
                S.op('dve', lambda e: e.tensor_copy(out=P[:, :, 0:3], in_=carry[:]), reads=['gd_carry'], writes=['gd_P'])
            self.proj_conv('gd', hn, P, xc, cols, wnext, samp, False, cn)
            if samp:
                for c in range(24):
                    S.op('dve', lambda e, c=c: e.tensor_copy(out=tP[:, c, :].rearrange("p (s k) -> p s k", k=3),
                                                            in_=P[:, c, :].rearrange("p (s k) -> p s k", k=11)[:, :, 8:11]),
                         reads=['gd_P'] + ['gd_P%d' % c_ for c_ in range(24)], writes=['gd_gB', 'gd_gam'])
                S.dma('sp', O['gdn_conv_s'][jl].rearrange("p (c k) -> p c k", k=48), tP, reads=['gd_gB', 'gd_gam'])
            else:
                S.op('dve', lambda e: e.tensor_copy(out=carry[:], in_=P[:, :, N:N + 3]), reads=['gd_P'] + ['gd_P%d' % c_ for c_ in range(24)], writes=['gd_carry'])
                if last_p:
                    S.dma('sp', O['gdn_conv_p'][jl].rearrange("p (c k) -> p c k", k=3), carry[:], reads=['gd_carry'])
            xck = ['gd_xc%d' % c for c in range(24)]
            wb, wkey = wnext()
            wv = wb[:, 0:128].rearrange("p (c n) -> p c n", n=16)
            for c in range(DC):
                S.op('pe', lambda e, c=c, wv=wv: e.matmul(ps[6][:, 0:16], lhsT=hn[:, c, :], rhs=wv[:, c, :], start=(c == 0), stop=(c == DC - 1)),
                     reads=[wkey, 'gd_hn'], writes=['ps6'], inc=(c == DC - 1))
            S.op('dve', lambda e: e.tensor_tensor(out=gsc[:], in0=ps[6][:, 0:8], in1=rows[:, 1, :], op=ALU.add), reads=['ps6', 'gd_rows'], writes=['gd_gsc'])
            S.op('act', lambda e: e.activation(out=beta[:], in_=ps[6][:, 8:16], func=AF.Sigmoid), reads=['ps6', 'gd_gsc'], writes=['gd_beta'])
            S.op('act', lambda e: e.activation(out=gsc[:], in_=gsc[:], func=AF.Exp), reads=['gd_gsc'], writes=['gd_gsc'])
            S.op('act', lambda e: e.activation(out=gsc[:], in_=gsc[:], func=AF.Ln, bias=C['ones'][:, 0:1], scale=1.0), reads=['gd_gsc', 'cst'], writes=['gd_gsc'])
            S.op('dve', lambda e: e.tensor_tensor(out=gsc[:], in0=gsc[:], in1=negA[:], op=ALU.mult), reads=['gd_gsc', 'gd_negA'], writes=['gd_gsc'])
            S.op('pe', lambda e: e.matmul(ps[6][:, 32:40], lhsT=Bmk, rhs=gsc[:], start=True, stop=True), reads=['cst', 'gd_gsc'], writes=['ps6'])
            S.op('dve', lambda e: e.tensor_copy(out=Gt[:], in_=ps[6][:, 32:40]), reads=['ps6'], writes=['gd_Gt'])
            S.op('act', lambda e: e.activation(out=Et[:], in_=Gt[:], func=AF.Exp), reads=['gd_Gt'], writes=['gd_Et'])
            S.op('pe', lambda e: e.matmul(ps[6][:, 64:72], lhsT=Sel, rhs=Gt[:], start=True, stop=True), reads=['cst', 'gd_Gt'], writes=['ps6'])
            S.op('act', lambda e: e.activation(out=cdbc[:], in_=ps[6][:, 64:72], func=AF.Exp), reads=['ps6'], writes=['gd_cdbc'])
            S.op('dve', lambda e: e.tensor_tensor(out=dec[:], in0=ps[6][:, 64:72], in1=Gt[:], op=ALU.subtract), reads=['ps6', 'gd_Gt', 'gd_cdbc'], writes=['gd_dec'])
            S.op('act', lambda e: e.activation(out=dec[:], in_=dec[:], func=AF.Exp), reads=['gd_dec'], writes=['gd_dec'])
            S.op('dve', lambda e: e.tensor_tensor(out=bE[:], in0=beta[:], in1=Et[:], op=ALU.mult), reads=['gd_beta', 'gd_Et'], writes=['gd_bE'])
            for q in range(4):
                wb, wkey = wnext()
                wv = wb[:, 0:2048].rearrange("p (c n) -> p c n", n=256)
                cn[0] += 1
                bi_ = cn[0] % 2
                b = ps[bi_]
                for c in range(DC):
                    S.op('pe', lambda e, c=c, b=b, wv=wv: e.matmul(b[:, 0:256], lhsT=hn[:, c, :], rhs=wv[:, c, :], start=(c == 0), stop=(c == DC - 1)),
                         reads=[wkey, 'gd_hn'], writes=['ps%d' % bi_], inc=(c == DC - 1))
                S.op('act', lambda e, q=q, b=b: e.activation(out=fl(gt)[:, q * 256:(q + 1) * 256], in_=b[:, 0:256], func=AF.Silu), reads=['ps%d' % bi_], writes=['gd_gt'])
            sqv = fl(gam)[:, 0:256].bitcast(BF16)
            rtmp = fl(gam)[:, 512:1024]
            sq4 = fl(gam).bitcast(BF16).rearrange("p (g n) -> p g n", n=512)
            xgs = [xc[:, grp * 4:grp * 4 + 4, :].rearrange("p a b -> p (a b)") for grp in range(4)]
            for grp in range(4):
                S.op('act', lambda e, grp=grp: e.activation(out=sq4[:, grp, :], in_=xgs[grp], func=AF.Square),
                     reads=xck[grp * 4:grp * 4 + 4] + ['gd_gam'], writes=['gd_sq%d' % grp])
            for grp in range(4):
                b = ps[2 + grp]
                for k in range(4):
                    S.op('pe', lambda e, k=k, b=b, grp=grp: e.matmul(b[:, k * 128:(k + 1) * 128], lhsT=self.ones_bf[:], rhs=sq4[:, grp, k * 128:(k + 1) * 128], start=True, stop=True),
                         reads=['gd_sq%d' % grp, 'ones_bf'], writes=['ps%d' % (2 + grp)], inc=(k == 3))
            for grp in range(4):
                b = ps[2 + grp]
                S.op('act', lambda e, b=b: e.activation(out=b[:], in_=b[:], func=AF.Ln, scale=1.0, bias=self.epsc[:]),
                     reads=['ps%d' % (2 + grp), 'epsc'], writes=['ps%d' % (2 + grp)])
                S.op('act', lambda e, b=b: e.activation(out=b[:], in_=b[:], func=AF.Exp, scale=-0.5),
                     reads=['ps%d' % (2 + grp)], writes=['ps%d' % (2 + grp)])
            for grp in range(4):
                b = ps[2 + grp]
                dstT = qT if grp < 2 else kT
                hs = slice((grp % 2) * 4, (grp % 2) * 4 + 4)
                sc = (128.0 ** -0.5) if grp < 2 else 1.0
                S.op('dve', lambda e, grp=grp, dstT=dstT, hs=hs, sc=sc, b=b: e.scalar_tensor_tensor(out=dstT[:, hs, :].rearrange("p a b -> p (a b)"), in0=xgs[grp], scalar=sc, in1=b[:],
                                                                                          op0=ALU.mult, op1=ALU.mult),
                     reads=xck[grp * 4:grp * 4 + 4] + ['ps%d' % (2 + grp)], writes=['gd_qT' if grp < 2 else 'gd_kT'])
            S.op('dve', lambda e: e.memset(rtmp[:, 0:2], 0.0), writes=['gd_gam', 'gd_sq0', 'gd_sq1', 'gd_sq2', 'gd_sq3'])
            for half in range(2):
                b = ps[4 + half]
                for k in range(4):
                    h = half * 4 + k
                    S.op('pe', lambda e, k=k, h=h, b=b: e.matmul(b[:, k * 128:(k + 1) * 128], lhsT=kT[:, h, :], rhs=self.ident_bf[:], start=True, stop=True),
                         reads=['gd_kT', 'ident_bf'], writes=['ps%d' % (4 + half)], inc=(k == 3))
                S.op('dve', lambda e, half=half, b=b: e.tensor_tensor(out=kdec[:, half * 4:half * 4 + 4, :], in0=b[:].rearrange("p (a b) -> p a b", b=N),
                                                                    in1=dec[:, half * 4:half * 4 + 4].unsqueeze(2).broadcast_to([128, 4, N]), op=ALU.mult),
                     reads=['ps%d' % (4 + half), 'gd_dec'], writes=['gd_kdec'])
            for half in range(2):
                b = ps[2 + half]
                for k in range(4):
                    h = half * 4 + k
                    S.op('pe', lambda e, k=k, h=h, b=b: e.transpose(b[:, k * 128:(k + 1) * 128], xc[:, 16 + h, :], C['ident']),
                         reads=xck[16:24] + ['cst'], writes=['ps%d' % (2 + half)], inc=(k == 3))
                S.op('dve', lambda e, half=half, b=b: e.tensor_tensor(out=bV[:, half * 4:half * 4 + 4, :], in0=b[:].rearrange("p (a b) -> p a b", b=N),
                                                                    in1=beta[:, half * 4:half * 4 + 4].unsqueeze(2).broadcast_to([128, 4, N]), op=ALU.mult),
                     reads=['ps%d' % (2 + half), 'gd_beta'], writes=['gd_bV'])
            S.op('pool', lambda e: e.tensor_tensor(out=gB[:], in0=gsc[:].unsqueeze(2).broadcast_to([128, 8, N]), in1=Bmk.unsqueeze(1).broadcast_to([128, 8, N]), op=ALU.mult),
                 reads=['gd_gsc', 'cst', 'gd_gB'], writes=['gd_gB'])
            for hh in range(2):
                S.op('pe', lambda e, hh=hh: e.matmul(ps[6 + hh][:], lhsT=Am, rhs=gB[:, hh * 4:(hh + 1) * 4, :].rearrange("p a b -> p (a b)"), start=True, stop=True),
                     reads=['cst', 'gd_gB'], writes=['ps%d' % (6 + hh)])
                S.op('act', lambda e, hh=hh: e.activation(out=gam[:, hh * 4:(hh + 1) * 4, :].rearrange("p a b -> p (a b)"), in_=ps[6 + hh][:], func=AF.Exp),
                     reads=['ps%d' % (6 + hh), 'gd_gam'], writes=['gd_gam'])
            S.op('dve', lambda e: e.tensor_tensor(out=gam[:], in0=gam[:], in1=Bmk.unsqueeze(1).broadcast_to([128, 8, N]), op=ALU.mult), reads=['gd_gam', 'cst'], writes=['gd_gam'])
            for half in range(2):
                b = ps[4 + half]
                for k in range(4):
                    h = half * 4 + k
                    S.op('pe', lambda e, k=k, h=h, b=b: e.matmul(b[:, k * 128:(k + 1) * 128], lhsT=kT[:, h, :], rhs=qT[:, h, :], start=True, stop=True),
                         reads=['gd_kT', 'gd_qT'], writes=['ps%d' % (4 + half)], inc=(k == 3))
                S.op('dve', lambda e, half=half, b=b: e.tensor_tensor(out=attnT[:, half * 4:half * 4 + 4, :].rearrange("p a b -> p (a b)"), in0=b[:], in1=gam[:, half * 4:half * 4 + 4, :].rearrange("p a b -> p (a b)"), op=ALU.mult),
                     reads=['ps%d' % (4 + half), 'gd_gam'], writes=['gd_attnT'])
            S.op('pool', lambda e: e.tensor_tensor(out=gB[:], in0=gsc[:].unsqueeze(2).broadcast_to([128, 8, N]), in1=Am.unsqueeze(1).broadcast_to([128, 8, N]), op=ALU.mult),
                 reads=['gd_gsc', 'cst', 'gd_gB'], writes=['gd_gB'])
            for hh in range(2):
                S.op('pe', lambda e, hh=hh: e.matmul(ps[6 + hh][:], lhsT=Bmk, rhs=gB[:, hh * 4:(hh + 1) * 4, :].rearrange("p a b -> p (a b)"), start=True, stop=True),
                     reads=['cst', 'gd_gB'], writes=['ps%d' % (6 + hh)])
                S.op('act', lambda e, hh=hh: e.activation(out=gam[:, hh * 4:(hh + 1) * 4, :].rearrange("p a b -> p (a b)"), in_=ps[6 + hh][:], func=AF.Exp),
                     reads=['ps%d' % (6 + hh), 'gd_gam', 'gd_attnT'], writes=['gd_gam'])
            S.op('dve', lambda e: e.tensor_tensor(out=gam[:], in0=gam[:], in1=Am.unsqueeze(1).broadcast_to([128, 8, N]), op=ALU.mult), reads=['gd_gam', 'cst'], writes=['gd_gam'])
            S.op('dve', lambda e: e.tensor_tensor(out=gam[:], in0=gam[:], in1=beta[:].unsqueeze(2).broadcast_to([128, 8, N]), op=ALU.mult), reads=['gd_gam', 'gd_beta'], writes=['gd_gam'])
            for half in range(2):
                b = ps[4 + half]
                for k in range(4):
                    h = half * 4 + k
                    S.op('pe', lambda e, k=k, h=h, b=b: e.matmul(b[:, k * 128:(k + 1) * 128], lhsT=kT[:, h, :], rhs=kT[:, h, :], start=True, stop=True),
                         reads=['gd_kT'], writes=['ps%d' % (4 + half)], inc=(k == 3))
                S.op('dve', lambda e, half=half, b=b: e.tensor_tensor(out=mM[:, half * 4:half * 4 + 4, :].rearrange("p a b -> p (a b)"), in0=b[:], in1=gam[:, half * 4:half * 4 + 4, :].rearrange("p a b -> p (a b)"), op=ALU.mult),
                     reads=['ps%d' % (4 + half), 'gd_gam'], writes=['gd_mM'])
            for half in range(2):
                b = ps[2 + half]
                for k in range(4):
                    h = half * 4 + k
                    S.op('pe', lambda e, k=k, h=h, b=b: e.transpose(b[:, k * 128:(k + 1) * 128], mM[:, h, :], C['ident']),
                         reads=['gd_mM', 'cst'], writes=['ps%d' % (2 + half)], inc=(k == 3))
                S.op('act', lambda e, half=half, b=b: e.activation(out=nM[:, half * 4:half * 4 + 4, :].rearrange("p a b -> p (a b)"), in_=b[:], func=AF.Copy),
                     reads=['ps%d' % (2 + half)], writes=['gd_nM'])
            S.op('pool', lambda e: e.tensor_tensor(out=Rm[:], in0=C['ident'].unsqueeze(1).broadcast_to([128, 8, N]), in1=nM[:], op=ALU.subtract),
                 reads=['cst', 'gd_nM', 'gd_Rm'], writes=['gd_Rm'])
            Pc, PTc, kP, kPT = nM, mM, 'gd_nM', 'gd_mM'
            bufs = [(Pa, PTa, 'gd_Pa', 'gd_PTa'), (Pb, PTb, 'gd_Pb', 'gd_PTb')]
            for it in range(nsq):
                Pn, PTn, kPn, kPTn = bufs[it % 2]
                lastit = (it == nsq - 1)
                for half in range(2):
                    if not lastit:
                        b = ps[2 + half]
                        for k in range(4):
                            h = half * 4 + k
                            S.op('pe', lambda e, k=k, h=h, b=b, Pc=Pc, PTc=PTc: e.matmul(b[:, k * 128:(k + 1) * 128], lhsT=PTc[:, h, :], rhs=Pc[:, h, :], start=True, stop=True),
                                 reads=[kP, kPT], writes=['ps%d' % (2 + half)], inc=(k == 3))
                        S.op('act', lambda e, half=half, b=b, Pn=Pn: e.activation(out=Pn[:, half * 4:half * 4 + 4, :].rearrange("p a b -> p (a b)"), in_=b[:], func=AF.Copy),
                             reads=['ps%d' % (2 + half), kPn], writes=[kPn])
                    b = ps[4 + half]
                    for k in range(4):
                        h = half * 4 + k
                        S.op('pe', lambda e, k=k, h=h, b=b, Pc=Pc, PTc=PTc: e.matmul(b[:, k * 128:(k + 1) * 128], lhsT=Pc[:, h, :], rhs=PTc[:, h, :], start=True, stop=True),
                             reads=[kP, kPT], writes=['ps%d' % (4 + half)], inc=(k == 3))
                    S.op('act', lambda e, half=half, b=b, PTn=PTn: e.activation(out=PTn[:, half * 4:half * 4 + 4, :].rearrange("p a b -> p (a b)"), in_=b[:], func=AF.Copy),
                         reads=['ps%d' % (4 + half), kPTn], writes=[kPTn])
                for half in range(2):
                    b = ps[6 + half]
                    for k in range(4):
                        h = half * 4 + k
                        S.op('pe', lambda e, k=k, h=h, b=b, PTn=PTn: e.matmul(b[:, k * 128:(k + 1) * 128], lhsT=PTn[:, h, :], rhs=Rm[:, h, :], start=True, stop=True),
                             reads=[kPTn, 'gd_Rm'], writes=['ps%d' % (6 + half)], inc=(k == 3))
                for half in range(2):
                    b = ps[6 + half]
                    S.op('dve', lambda e, half=half, b=b: e.tensor_tensor(out=Rm[:, half * 4:half * 4 + 4, :].rearrange("p a b -> p (a b)"), in0=b[:], in1=Rm[:, half * 4:half * 4 + 4, :].rearrange("p a b -> p (a b)"), op=ALU.add),
                         reads=['ps%d' % (6 + half), 'gd_Rm'], writes=['gd_Rm'])
                Pc, PTc, kP, kPT = Pn, PTn, kPn, kPTn
            if not samp:
                for half in range(2):
                    for k in range(4):
                        h = half * 4 + k
                        S.op('pe', lambda e, k=k, h=h, half=half: e.matmul(ps[2 + half][:, k * 128:(k + 1) * 128], lhsT=kT[:, h, :], rhs=Sbf[:, h, :], start=True, stop=True),
                             reads=['gd_kT', 'gd_Sbf'], writes=['ps%d' % (2 + half)], inc=(k == 3))
                        S.op('pe', lambda e, k=k, h=h, half=half: e.matmul(ps[4 + half][:, k * 128:(k + 1) * 128], lhsT=qT[:, h, :], rhs=Sbf[:, h, :], start=True, stop=True),
                             reads=['gd_qT', 'gd_Sbf'], writes=['ps%d' % (4 + half)], inc=(k == 3))
            else:
                S.barrier()
                kTm = self.scr("kTm", [128, 16, N], BF16, self._off(mM))
                qTm = self.scr("qTm", [128, 16, N], BF16, self._off(nM))
                S0b = self.scr("S0b", [128, 16, N], BF16, self._off(gB))
                S.op('dve', lambda e: e.tensor_tensor(out=gsel[:], in0=Gt[:].unsqueeze(1).broadcast_to([128, 16, 8]),
                                                      in1=C['SeqSel'][:, 16:32].unsqueeze(2).broadcast_to([128, 16, 8]), op=ALU.mult), reads=['gd_Gt', 'cst'], writes=['gd_gsel'])
                S.op('pe', lambda e: e.matmul(ps[6][:, 0:128], lhsT=C['ones'], rhs=gsel[:].rearrange("p a b -> p (a b)"), start=True, stop=True), reads=['cst', 'gd_gsel'], writes=['ps6'])
                S.op('act', lambda e: e.activation(out=cdall[:].rearrange("p a b -> p (a b)"), in_=ps[6][:, 0:128], func=AF.Exp), reads=['ps6'], writes=['gd_cdall'])
                for h in range(8):
                    half, k = h // 4, h % 4
                    S.dma('sp', S0h[:], I['gd_s0'][jl, :, h].rearrange("s d e -> d s e"), writes=['gd_S0h'], fence=True)
                    S.op('act', lambda e: e.activation(out=fl(S0b), in_=fl(S0h), func=AF.Copy), reads=['gd_S0h', 'gd_S0b'], writes=['gd_S0b'])
                    S.op('dve', lambda e: e.memset(kTm[:], 0.0), reads=['gd_kTm'], writes=['gd_kTm'])
                    S.op('dve', lambda e: e.memset(qTm[:], 0.0), reads=['gd_qTm'], writes=['gd_qTm'])
                    for s_ in range(NS):
                        S.op('dve', lambda e, h=h, s_=s_: e.tensor_copy(out=kTm[:, s_, s_ * 8:(s_ + 1) * 8], in_=kT[:, h, s_ * 8:(s_ + 1) * 8]), reads=['gd_kT', 'gd_kTm'], writes=['gd_kTm'])
                        S.op('dve', lambda e, h=h, s_=s_: e.tensor_copy(out=qTm[:, s_, s_ * 8:(s_ + 1) * 8], in_=qT[:, h, s_ * 8:(s_ + 1) * 8]), reads=['gd_qT', 'gd_qTm'], writes=['gd_qTm'])
                    for s_ in range(NS):
                        S.op('pe', lambda e, k=k, half=half, s_=s_: e.matmul(ps[2 + half][:, k * 128:(k + 1) * 128], lhsT=kTm[:, s_, :], rhs=S0b[:, s_, :], start=(s_ == 0), stop=(s_ == NS - 1)),
                             reads=['gd_kTm', 'gd_S0b'], writes=['ps%d' % (2 + half)], inc=(s_ == NS - 1))
                    for s_ in range(NS):
                        S.op('pe', lambda e, k=k, half=half, s_=s_: e.matmul(ps[4 + half][:, k * 128:(k + 1) * 128], lhsT=qTm[:, s_, :], rhs=S0b[:, s_, :], start=(s_ == 0), stop=(s_ == NS - 1)),
                             reads=['gd_qTm', 'gd_S0b'], writes=['ps%d' % (4 + half)], inc=(s_ == NS - 1))
            for half in range(2):
                S.op('dve', lambda e, half=half: e.tensor_tensor(out=tmp4[:, half * 4:half * 4 + 4, :], in0=ps[2 + half][:].rearrange("p (a b) -> p a b", b=N),
                                                                 in1=bE[:, half * 4:half * 4 + 4].unsqueeze(2).broadcast_to([128, 4, N]), op=ALU.mult),
                     reads=['ps%d' % (2 + half), 'gd_bE', 'gd_tmp4'], writes=['gd_tmp4'])
                S.op('dve', lambda e, half=half: e.tensor_tensor(out=oo[:, half * 4:half * 4 + 4, :], in0=ps[4 + half][:].rearrange("p (a b) -> p a b", b=N),
                                                                 in1=Et[:, half * 4:half * 4 + 4].unsqueeze(2).broadcast_to([128, 4, N]), op=ALU.mult),
                     reads=['ps%d' % (4 + half), 'gd_Et', 'gd_oo'], writes=['gd_oo'])
            S.op('dve', lambda e: e.tensor_tensor(out=rr[:], in0=bV[:], in1=tmp4[:], op=ALU.subtract), reads=['gd_bV', 'gd_tmp4'] + xck, writes=['gd_rr'])
            for half in range(2):
                for k in range(4):
                    h = half * 4 + k
                    S.op('pe', lambda e, k=k, h=h, half=half: e.matmul(ps[2 + half][:, k * 128:(k + 1) * 128], lhsT=Rm[:, h, :], rhs=rr[:, h, :], start=True, stop=True),
                         reads=['gd_Rm', 'gd_rr'], writes=['ps%d' % (2 + half)], inc=(k == 3))
                S.op('act', lambda e, half=half: e.activation(out=vnb[:, half * 4:half * 4 + 4, :].rearrange("p a b -> p (a b)"), in_=ps[2 + half][:], func=AF.Copy),
                     reads=['ps%d' % (2 + half)], writes=['gd_vnb'])
            for half in range(2):
                for k in range(4):
                    h = half * 4 + k
                    S.op('pe', lambda e, k=k, h=h, half=half: e.matmul(ps[4 + half][:, k * 128:(k + 1) * 128], lhsT=attnT[:, h, :], rhs=vnb[:, h, :], start=True, stop=True),
                         reads=['gd_attnT', 'gd_vnb'], writes=['ps%d' % (4 + half)], inc=(k == 3))
                S.op('dve', lambda e, half=half: e.tensor_tensor(out=oo[:, half * 4:half * 4 + 4, :].rearrange("p a b -> p (a b)"), in0=ps[4 + half][:], in1=oo[:, half * 4:half * 4 + 4, :].rearrange("p a b -> p (a b)"), op=ALU.add),
                     reads=['ps%d' % (4 + half), 'gd_oo'], writes=['gd_oo'])
            if not samp:
                for half in range(2):
                    for k in range(4):
                        h = half * 4 + k
                        S.op('pe', lambda e, k=k, h=h, half=half: e.matmul(ps[6 + half][:, k * 128:(k + 1) * 128], lhsT=kdec[:, h, :], rhs=vnb[:, h, :], start=True, stop=True),
                             reads=['gd_kdec', 'gd_vnb'], writes=['ps%d' % (6 + half)], inc=(k == 3))
                    S.op('pool', lambda e, half=half: e.tensor_tensor(out=Sst[:, half * 4:half * 4 + 4, :], in0=Sst[:, half * 4:half * 4 + 4, :],
                                                                     in1=cdbc[:, half * 4:half * 4 + 4].unsqueeze(2).broadcast_to([128, 4, N]), op=ALU.mult),
                         reads=['gd_S', 'gd_cdbc', 'gd_Sbf'], writes=['gd_S'])
                    S.op('dve', lambda e, half=half: e.tensor_tensor(out=Sst[:, half * 4:half * 4 + 4, :].rearrange("p a b -> p (a b)"), in0=ps[6 + half][:], in1=Sst[:, half * 4:half * 4 + 4, :].rearrange("p a b -> p (a b)"), op=ALU.add),
                         reads=['ps%d' % (6 + half), 'gd_S'], writes=['gd_S'])
                S.op('act', lambda e: e.activation(out=fl(Sbf), in_=fl(Sst), func=AF.Copy), reads=['gd_S'], writes=['gd_Sbf'])
                if last_p:
                    S.dma('sp', O['gdn_state_p'][jl].rearrange("h d e -> d h e"), Sst[:], reads=['gd_S'])
            else:
                for h in range(8):
                    S.dma('sp', S0h[:], I['gd_s0'][jl, :, h].rearrange("s d e -> d s e"), writes=['gd_S0h'])
                    for s_ in range(NS):
                        k2 = s_ % 2
                        S.op('dve', lambda e, h=h, s_=s_, k2=k2: e.tensor_scalar(out=kdm[:, k2, :], in0=kdec[:, h, :], scalar1=C['SeqSel'][:, s_:s_ + 1], scalar2=None, op0=ALU.mult),
                             reads=['gd_kdec', 'cst'], writes=['gd_kdm%d' % k2])
                        S.op('pe', lambda e, h=h, k2=k2: e.matmul(ps[6 + k2][:, 0:N], lhsT=kdm[:, k2, :], rhs=vnb[:, h, :], start=True, stop=True),
                             reads=['gd_kdm%d' % k2, 'gd_vnb'], writes=['ps%d' % (6 + k2)])
                        S.op('dve', lambda e, h=h, s_=s_, k2=k2: e.scalar_tensor_tensor(out=S0h[:, s_, :], in0=S0h[:, s_, :], scalar=cdall[:, s_, h:h + 1], in1=ps[6 + k2][:, 0:N],
                                                                                 op0=ALU.mult, op1=ALU.add),
                             reads=['gd_S0h', 'gd_cdall', 'ps%d' % (6 + k2)], writes=['gd_S0h'])
                    S.dma('sp', O['gdn_state_s'][jl, :, h].rearrange("s d e -> d s e"), S0h[:], reads=['gd_S0h'])
            S.op('act', lambda e: e.activation(out=fl(tmp4), in_=fl(oo), func=AF.Square), reads=['gd_oo', 'gd_tmp4', 'gd_rr'], writes=['gd_tmp4'])
            S.op('dve', lambda e: e.tensor_reduce(out=ss8[:], in_=tmp4[:], axis=AX.X, op=ALU.add), reads=['gd_tmp4'], writes=['gd_ss8'])
            S.op('act', lambda e: e.activation(out=ss8[:], in_=ss8[:], func=AF.Sqrt, scale=1.0 / 128, bias=self.epsc[:]), reads=['gd_ss8', 'epsc'], writes=['gd_ss8'])
            S.op('dve', lambda e: e.reciprocal(out=ss8[:], in_=ss8[:]), reads=['gd_ss8'], writes=['gd_ss8'])
            S.op('dve', lambda e: e.tensor_tensor(out=oo[:], in0=oo[:], in1=ss8[:].unsqueeze(2).broadcast_to([128, 8, N]), op=ALU.mult), reads=['gd_oo', 'gd_ss8'], writes=['gd_oo'])
            S.op('dve', lambda e: e.tensor_tensor(out=oo[:], in0=oo[:], in1=gt[:], op=ALU.mult), reads=['gd_oo', 'gd_gt'], writes=['gd_oo'])
            for half in range(2):
                b = ps[2 + half]
                for k in range(4):
                    h = half * 4 + k
                    S.op('pe', lambda e, k=k, h=h, b=b: e.transpose(b[:, k * 128:(k + 1) * 128], oo[:, h, :], C['ident']),
                         reads=['gd_oo', 'cst'], writes=['ps%d' % (2 + half)], inc=(k == 3))
                S.op('act', lambda e, half=half, b=b: e.activation(out=onT[:, half * 4:half * 4 + 4, :].rearrange("p a b -> p (a b)"), in_=b[:], func=AF.Copy, scale=nwcol[:]),
                     reads=['ps%d' % (2 + half), 'gd_nwc'], writes=['gd_onT'])
            self.outproj_residual(l, 3, t0, N, wnext, lambda k: onT[:, k, :], 8, 'gd_onT', ysb, hn, rstd, 'gd_')

        for bi, (t0, n) in enumerate(BLOCKS128):
            do_block(bi, t0)

    def _sqv(self, tmp4):
        return tmp4[:].rearrange("p a b -> p (a b)")[:, 0:256].bitcast(BF16)

    def _off(self, v):
        return (v.offset - self.scratch[:, 0:1].offset) * (2 if v.dtype == BF16 else 4)

    def ffn(self, l, f):
        S = self.S
        X = self.X
        ipre, ipost = (0, 1) if f == 0 else (4, 5)
        o = 0
        xn = self.scr("xn", [128, DC, PMAX], BF16, o); o += DC * PMAX * 2
        a = self.scr("a", [128, FC, PMAX], BF16, o); o += FC * PMAX * 2
        ysb = self.scr("ysb", [128, DC, PMAX], F32, o); o += DC * PMAX * 4
        sg = [self.scr("sg%d" % i, [128, 512], F32, o + i * 2048) for i in range(2)]; o += 4096
        sq = [self.scr("sq%d" % i, [128, 512], BF16, o + i * 1024) for i in range(2)]; o += 2048
        rstd = self.scr("rstd", [128, PMAX], F32, o); o += PMAX * 4
        assert o <= self.scr_size
        ps = self.ps
        cnt = 0
        import os
        PH = int(os.environ.get('FFN_PH', '4'))
        srcs = []
        for _ in FFN_PASSES:
            srcs += [(self.I['ffn_w_in'][l, f, k], 2048, ('fi', l, f, k)) for k in range(FC)]
            srcs += [(self.I['ffn_w_out'][l, f, d, h], 1408, ('fo', l, f, d, h)) for d in range(DC) for h in range(2)]
        wnext = self.wstream(srcs)
        for subt in FFN_PASSES:
            S.barrier()
            p0 = subt[0][0]
            for si, (t0, n) in enumerate(subt):
                lo = t0 - p0
                st_ps = ps[6 + si % 2]
                kst = 'ps%d' % (6 + si % 2)
                S.op('act', lambda e, lo=lo, t0=t0, n=n: e.activation(out=xn[:, :, lo:lo + n], in_=X[:, :, t0:t0 + n], func=AF.Square),
                     reads=['X'], writes=['xn%d' % si])
                for c in range(DC):
                    S.op('pe', lambda e, c=c, lo=lo, n=n, st_ps=st_ps: e.matmul(st_ps[:, 0:n], lhsT=self.ones_bf[:], rhs=xn[:, c, lo:lo + n], start=(c == 0), stop=(c == DC - 1)),
                         reads=['xn%d' % si, 'ones_bf'], writes=[kst], inc=(c == DC - 1))
                S.op('act', lambda e, lo=lo, n=n, st_ps=st_ps: e.activation(out=rstd[:, lo:lo + n], in_=st_ps[:, 0:n], func=AF.Ln, scale=1.0 / D, bias=self.epsc[:]),
                     reads=[kst, 'epsc'], writes=['rstd%d' % si])
                S.op('act', lambda e, lo=lo, n=n: e.activation(out=rstd[:, lo:lo + n], in_=rstd[:, lo:lo + n], func=AF.Exp, scale=-0.5),
                     reads=['rstd%d' % si], writes=['rstd%d' % si])
                for c in range(DC):
                    S.op('dve', lambda e, c=c, lo=lo, n=n, t0=t0: e.scalar_tensor_tensor(
                        out=xn[:, c, lo:lo + n], in0=X[:, c, t0:t0 + n], scalar=self.nwcol(l, ipre, c), in1=rstd[:, lo:lo + n],
                        op0=ALU.mult, op1=ALU.mult), reads=['X', 'nw', 'rstd%d' % si], writes=['xn%d' % si])
            if PH < 2:
                continue
            for k in range(FC):
                wb, wkey = wnext()
                wv = wb[:, 0:2048].rearrange("p (c n) -> p c n", n=256)
                for si, (t0, n) in enumerate(subt):
                    lo = t0 - p0
                    for jj in range(1):
                        j = k
                        gi = cnt % 2
                        cnt += 1
                        gps, ups = ps[gi], ps[2 + gi]
                        for c in range(DC):
                            S.op('pe', lambda e, c=c, jj=jj, lo=lo, n=n, gps=gps, wv=wv: e.matmul(
                                gps[:, 0:n], lhsT=wv[:, c, 0:128], rhs=xn[:, c, lo:lo + n], start=(c == 0), stop=(c == DC - 1)),
                                reads=[wkey, 'xn%d' % si], writes=['ps%d' % gi], inc=(c == DC - 1))
                        for c in range(DC):
                            S.op('pe', lambda e, c=c, jj=jj, lo=lo, n=n, ups=ups, wv=wv: e.matmul(
                                ups[:, 0:n], lhsT=wv[:, c, 128:256], rhs=xn[:, c, lo:lo + n], start=(c == 0), stop=(c == DC - 1)),
                                reads=[wkey, 'xn%d' % si], writes=['ps%d' % (2 + gi)], inc=(c == DC - 1))
                        S.op('act', lambda e, gi=gi, n=n, gps=gps: e.activation(out=sg[gi][:, 0:n], in_=gps[:, 0:n], func=AF.Silu),
                             reads=['ps%d' % gi], writes=['sg%d' % gi])
                        S.op('dve', lambda e, gi=gi, n=n, ups=ups, j=j, lo=lo: e.tensor_tensor(
                            out=a[:, j, lo:lo + n], in0=sg[gi][:, 0:n], in1=ups[:, 0:n], op=ALU.mult),
                            reads=['sg%d' % gi, 'ps%d' % (2 + gi)], writes=['a%d' % si])
            if PH < 3:
                continue
            for d in range(DC):
                wb0, wkey0 = wnext()
                wb1, wkey1 = wnext()
                for si, (t0, n) in enumerate(subt):
                    lo = t0 - p0
                    yi = cnt % 2
                    cnt += 1
                    yps = ps[4 + yi]
                    st_ps = ps[si] if si < 2 else ps[6]
                    kst = 'ps%d' % (si if si < 2 else 6)
                    for c in range(FC):
                        wb, wkey, cc = (wb0, wkey0, c) if c < 11 else (wb1, wkey1, c - 11)
                        S.op('pe', lambda e, c=c, cc=cc, lo=lo, n=n, yps=yps, wb=wb: e.matmul(
                            yps[:, 0:n], lhsT=wb[:, cc * 128:(cc + 1) * 128], rhs=a[:, c, lo:lo + n], start=(c == 0), stop=(c == FC - 1)),
                            reads=[wkey, 'a%d' % si], writes=['ps%d' % (4 + yi)], inc=(c == FC - 1))
                    S.op('dve', lambda e, d=d, lo=lo, n=n, yps=yps: e.tensor_copy(out=ysb[:, d, lo:lo + n], in_=yps[:, 0:n]),
                         reads=['ps%d' % (4 + yi)], writes=['ysb%d' % si])
                    S.op('act', lambda e, d=d, lo=lo, n=n: e.activation(out=xn[:, d, lo:lo + n], in_=ysb[:, d, lo:lo + n], func=AF.Square),
                         reads=['ysb%d' % si], writes=['xn%d' % si])
            if PH < 4:
                continue
            for si, (t0, n) in enumerate(subt):
                lo = t0 - p0
                st_ps = ps[6 + si % 2]
                kst = 'ps%d' % (6 + si % 2)
                for d in range(DC):
                    S.op('pe', lambda e, d=d, lo=lo, n=n, st_ps=st_ps: e.matmul(st_ps[:, 0:n], lhsT=self.ones_bf[:], rhs=xn[:, d, lo:lo + n], start=(d == 0), stop=(d == DC - 1)),
                         reads=['xn%d' % si, 'ones_bf'], writes=[kst], inc=(d == DC - 1))
                S.op('act', lambda e, lo=lo, n=n, st_ps=st_ps: e.activation(out=rstd[:, lo:lo + n], in_=st_ps[:, 0:n], func=AF.Ln, scale=1.0 / D, bias=self.epsc[:]),
                     reads=[kst, 'epsc'], writes=['rstd%d' % si])
                S.op('act', lambda e, lo=lo, n=n: e.activation(out=rstd[:, lo:lo + n], in_=rstd[:, lo:lo + n], func=AF.Exp, scale=-0.5),
                     reads=['rstd%d' % si], writes=['rstd%d' % si])
                for d in range(DC):
                    S.op('dve', lambda e, d=d, lo=lo, n=n: e.tensor_tensor(out=ysb[:, d, lo:lo + n], in0=ysb[:, d, lo:lo + n], in1=rstd[:, lo:lo + n], op=ALU.mult),
                         reads=['ysb%d' % si, 'rstd%d' % si], writes=['ysb%d' % si])
                    S.op('dve', lambda e, d=d, lo=lo, n=n, t0=t0: e.scalar_tensor_tensor(
                        out=X[:, d, t0:t0 + n], in0=ysb[:, d, lo:lo + n], scalar=self.nwcol(l, ipost, d, half=True), in1=X[:, d, t0:t0 + n],
                        op0=ALU.mult, op1=ALU.add), reads=['ysb%d' % si, 'nwh', 'X'], writes=['X'])


_CACHE = {}


def prep_shared(inp, nl=DEPTH):
    sh = {}
    f32 = lambda k: np.asarray(inp[k], np.float32)
    rep = lambda v: np.ascontiguousarray(np.broadcast_to(v, (128,) + v.shape))
    w = f32('cmlp_w_in')[0].reshape(DC, 128, 2, 8, 256)
    sh['cm_wu'] = np.ascontiguousarray(w[:, :, 0].transpose(2, 1, 0, 3)).reshape(8, 128, 2048)
    sh['cm_wv'] = np.ascontiguousarray(w[:, :, 1].transpose(2, 1, 0, 3)).reshape(8, 128, 2048)
    w = f32('cmlp_w_out')[0].reshape(16, 128, DC, 128)
    sh['cm_wo'] = np.ascontiguousarray(w.transpose(2, 1, 0, 3)).reshape(8, 128, 2048)
    b = f32('cmlp_b_in')[0]
    sh['cm_bv'] = rep(b[2048:])
    sh['cm_lnw'] = rep(f32('cmlp_ln_w')[0])
    sh['cm_lnb'] = rep(f32('cmlp_ln_b')[0])
    sh['cm_cols'] = np.ascontiguousarray(np.concatenate([b[:2048].reshape(16, 128).T, f32('cmlp_ln_w')[0].reshape(16, 128).T,
                                                         f32('cmlp_ln_b')[0].reshape(16, 128).T], axis=1))
    w = f32('gdn_w_in')
    sh['gd_wx'] = np.ascontiguousarray(w[:, :, :3072].reshape(2, DC, 128, 12, 256).transpose(0, 3, 2, 1, 4)).reshape(2, 12, 128, 2048)
    sh['gd_wg'] = np.ascontiguousarray(w[:, :, 3072:4096].reshape(2, DC, 128, 4, 256).transpose(0, 3, 2, 1, 4)).reshape(2, 4, 128, 2048)
    sh['gd_wab'] = np.ascontiguousarray(w[:, :, 4096:].reshape(2, DC, 128, 16).transpose(0, 2, 1, 3)).reshape(2, 128, 128)
    w = f32('gdn_w_out').reshape(2, 8, 128, DC, 128)
    sh['gd_wo'] = np.ascontiguousarray(w.transpose(0, 3, 2, 1, 4)).reshape(2, 8, 128, 1024)
    sh['gd_cols'] = np.ascontiguousarray(f32('gdn_conv_w').reshape(2, 4, 24, 128).transpose(0, 3, 2, 1)).reshape(2, 128, 96)
    sh['gd_rows'] = np.ascontiguousarray(np.broadcast_to(np.concatenate([f32('gdn_a_log'), f32('gdn_dt_bias')], axis=1)[:, None, :], (2, 128, 16)))
    sh['gd_nwc'] = np.ascontiguousarray(f32('gdn_norm_w').reshape(2, 128, 1))
    w = f32('ssd_w_in')[0]
    sh['ss_wz'] = np.ascontiguousarray(w[:, :2048].reshape(DC, 128, 8, 256).transpose(2, 1, 0, 3)).reshape(8, 128, 2048)
    sh['ss_wx'] = np.ascontiguousarray(w[:, 2048:5120].reshape(DC, 128, 12, 256).transpose(2, 1, 0, 3)).reshape(12, 128, 2048)
    sh['ss_wdt'] = np.ascontiguousarray(w[:, 5120:].reshape(DC, 128, 32).transpose(1, 0, 2)).reshape(128, 256)
    w = f32('ssd_w_out')[0].reshape(16, 128, DC, 128)
    sh['ss_wo'] = np.ascontiguousarray(w.transpose(2, 1, 0, 3)).reshape(8, 128, 2048)
    cw = f32('ssd_conv_w')[0].reshape(4, 24, 128)
    cb = f32('ssd_conv_b')[0].reshape(1, 24, 128)
    sh['ss_cols'] = np.ascontiguousarray(np.concatenate([cw, cb], axis=0).transpose(2, 1, 0)).reshape(128, 120)
    sh['ss_nwc'] = np.ascontiguousarray(f32('ssd_norm_w')[0].reshape(16, 128).T)
    sh['ss_rows'] = rep(np.concatenate([f32('ssd_a_log')[0], f32('ssd_dt_bias')[0], f32('ssd_d')[0]]))
    ws = f32('cmlp_w_s')[0]
    sh['cm_wsT'] = np.ascontiguousarray(ws.transpose(2, 0, 1))
    blk = np.zeros((128, 8, 128), np.float32)
    for q in range(NS):
        blk[q * 8:(q + 1) * 8, :, q * 8:(q + 1) * 8] = ws[:, :8, :8].transpose(2, 0, 1)
    sh['cm_wsblk'] = blk
    bs = f32('cmlp_b_s')[0]
    sh['cm_bs'] = rep(bs)
    sh['cm_bss'] = rep(np.ascontiguousarray(np.tile(bs[:, :8], (1, NS))))
    sh['consts'] = CONST_ARR
    nw = np.asarray(inp['norm_w'], np.float32).reshape(DEPTH, 6, DC, 128)
    sh['nw'] = np.ascontiguousarray(nw.transpose(3, 0, 1, 2).reshape(128, DEPTH * 6 * DC))
    w = np.asarray(inp['ffn_w_in'], np.float32).reshape(DEPTH, 2, DC, 128, 2, FC, 128)
    sh['ffn_w_in'] = np.ascontiguousarray(w[:nl].transpose(0, 1, 5, 3, 2, 4, 6)).reshape(nl, 2, FC, 128, 2048)
    w = np.asarray(inp['ffn_w_out'], np.float32).reshape(DEPTH, 2, 2, 11, 128, DC, 128)
    sh['ffn_w_out'] = np.ascontiguousarray(w[:nl].transpose(0, 1, 5, 2, 4, 3, 6)).reshape(nl, 2, DC, 2, 128, 11 * 128)
    return sh


def prep_core(inp, c):
    xp = np.asarray(inp['x_prompt'], np.float32)[c]
    xs = np.asarray(inp['x_sample'], np.float32)[c * NS:(c + 1) * NS].reshape(NS * LS, D)
    m = {}
    m['xT'] = np.ascontiguousarray(np.concatenate([xp, xs], axis=0).T)
    m['gd_s0'] = np.ascontiguousarray(np.asarray(inp['state_gdn'], np.float32)[:, c * NS:(c + 1) * NS])
    cs = np.asarray(inp['state_gdn_conv'], np.float32)[:, c * NS:(c + 1) * NS].reshape(2, NS, 3, 24, 128)
    m['gd_cs'] = np.ascontiguousarray(cs.transpose(0, 4, 3, 1, 2)).reshape(2, 128, 24 * 48)
    st = np.asarray(inp['state_ssd'], np.float32)[0, c * NS:(c + 1) * NS].reshape(NS, 2048, 128)
    m['ss_s0T'] = np.ascontiguousarray(st.transpose(0, 2, 1))
    cs = np.asarray(inp['state_ssd_conv'], np.float32)[0, c * NS:(c + 1) * NS].reshape(NS, 3, 24, 128)
    m['ss_cs'] = np.ascontiguousarray(cs.transpose(3, 2, 0, 1)).reshape(128, 24 * 48)
    return m


def run(inp, stages=999, plan=None):
    key = str(plan)
    if key not in _CACHE:
        b = Builder(stages, plan=plan)
        _CACHE[key] = (b.build(), b.nl)
    nc, nl = _CACHE[key]
    sh = prep_shared(inp, nl)
    in_maps = []
    for c in range(NCORES):
        m = dict(sh)
        m.update(prep_core(inp, c))
        in_maps.append(m)
    res = run_bass_kernel_spmd(nc, in_maps, core_ids=list(range(NCORES)))
    return res.results


def assemble(results):
    yT = [r['yT'] for r in results]
    y_prompt = np.stack([y[:, :SEQ].T for y in yT]).astype(np.float32)
    y_sample = np.concatenate([y[:, SEQ:].T.reshape(NS, LS, D) for y in yT]).astype(np.float32)
    return y_prompt, y_sample


def kernel(**inputs):
    results = run(inputs)
    y_prompt, y_sample = assemble(results)
    g = [gdn_outputs(r) for r in results]
    d = [ssd_outputs(r) for r in results]
    f = np.float32
    gdn_state_p = np.stack([x[0] for x in g], axis=1).astype(f)
    gdn_conv_p = np.stack([x[2] for x in g], axis=1).astype(f)
    ssd_state_p = np.stack([x[0] for x in d], axis=0)[None].astype(f)
    ssd_conv_p = np.stack([x[2] for x in d], axis=0)[None].astype(f)
    gdn_state_s = np.concatenate([x[1] for x in g], axis=1).astype(f)
    gdn_conv_s = np.concatenate([x[3] for x in g], axis=1).astype(f)
    ssd_state_s = np.concatenate([x[1] for x in d], axis=0)[None].astype(f)
    ssd_conv_s = np.concatenate([x[3] for x in d], axis=0)[None].astype(f)
    cmlp_v_s = np.concatenate([r['cmlp_v'].reshape(NS, LS, 2048) for r in results], axis=0)[None].astype(f)
    return (y_prompt, y_sample, gdn_state_p, gdn_conv_p, ssd_state_p, ssd_conv_p,
            gdn_state_s, gdn_conv_s, ssd_state_s, ssd_conv_s, cmlp_v_s)


def ssd_outputs(r):
    sp = r['ssd_pT'].T.reshape(32, 64, 128)
    ssn = r['ssd_sT'].transpose(0, 2, 1).reshape(NS, 32, 64, 128)
    cp = r['ssd_conv_p'].reshape(128, 24, 3).transpose(2, 1, 0).reshape(3, 3072)
    csn = r['ssd_conv_s'].reshape(128, 24, NS, 3).transpose(2, 3, 1, 0).reshape(NS, 3, 3072)
    return sp, ssn, cp, csn


def gdn_outputs(r):
    cp = r['gdn_conv_p'].reshape(2, 128, 24, 3).transpose(0, 3, 2, 1).reshape(2, 3, 3072)
    csn = r['gdn_conv_s'].reshape(2, 128, 24, NS, 3).transpose(0, 3, 4, 2, 1).reshape(2, NS, 3, 3072)
    return r['gdn_state_p'], r['gdn_state_s'], cp, csn


def check_extra(res, core, extra, rv):
    for i, (nsp, nss) in extra.items():
        if i % 3 == 0:
            sp, ssn, cp, csn = gdn_outputs(res[core])
            j = i // 3
            print("  layer", i, "gdn state_p", rv(sp[j], np.asarray(nsp[1])[0]), "state_s", rv(ssn[j], np.asarray(nss[1])),
                  "conv_p", rv(cp[j], np.asarray(nsp[0])[0]), "conv_s", rv(csn[j], np.asarray(nss[0])))
        if i % 3 == 2:
            sp, ssn, cp, csn = ssd_outputs(res[core])
            print("  layer", i, "ssd state_p", rv(sp, np.asarray(nsp[1])[0]), "state_s", rv(ssn, np.asarray(nss[1])),
                  "conv_p", rv(cp, np.asarray(nsp[0])[0]), "conv_s", rv(csn, np.asarray(nss[0])))
        if i % 3 == 1:
            ref = np.asarray(nss[0]).reshape(NS * LS, 2048)
            print("  layer", i, "cmlp_v resvar", rv(res[core]['cmlp_v'], ref))
```

```python
import numpy as np
from contextlib import ExitStack
import concourse.bass as bass
import concourse.mybir as mybir
from concourse.alu_op_type import AluOpType as ALU
from concourse.bass_utils import run_bass_kernel_spmd

F32 = mybir.dt.float32
BF16 = mybir.dt.bfloat16
U8 = mybir.dt.uint8
AF = mybir.ActivationFunctionType
AX = mybir.AxisListType

NCORES = 8
D = 1024
DC = 8
SEQ = 2048
NS = 16
LS = 8
T = SEQ + NS * LS
DEPTH = 4
DFF = 2816
FC = 22
EPS = 1e-6
WBUF = 2048
NRING = 5
NSTG = 3
WC_UNITS = 440
LOOKAHEAD = 3
CAST_PATTERN = ['act', 'dve']

FFN_PASSES = [[(0, 512), (512, 256)], [(768, 512), (1280, 256)], [(1536, 512), (2048, 128)]]
PMAX = 768
import os
BLOCKS128 = [(t, 128) for t in range(0, T, 128)][int(os.environ.get('B128_0', '0')):int(os.environ.get('B128_1', '17'))]
BLOCKS = [(0, 512), (512, 512), (1024, 512), (1536, 512), (2048, 128)][int(os.environ.get("BLK0", "0")):int(os.environ.get("NBLK", "5"))]


class Sched:
    NDMA = 28

    def __init__(self, nc, stack):
        self.nc = nc
        self.names = ['pe', 'act', 'dve', 'pool', 'sp']
        self.prog = {e: [] for e in self.names}
        self.sem = {e: stack.enter_context(nc.semaphore('s_' + e)) for e in self.names}
        self.cnt = {e: 0 for e in self.names}
        self.pend = {e: False for e in self.names}
        self.seen = {e: {} for e in self.names}
        self.dsem = [stack.enter_context(nc.semaphore('d%d' % i)) for i in range(self.NDMA)]
        self.dcnt = 0
        self.lastw = {}
        self.reads = {}
        self.sp_events = []
        self.all_dma = {}
        self.last_barrier = []
        self.sw_last = {}

    def _deps(self, e, reads, writes):
        deps = {}

        def add(ev):
            s, v = ev
            k = id(s)
            if k not in deps or deps[k][1] < v:
                deps[k] = (s, v)
        for k in reads:
            ev = self.lastw.get(k)
            if ev is not None:
                add(ev)
        for k in writes:
            ev = self.lastw.get(k)
            if ev is not None:
                add(ev)
            for ev in self.reads.get(k, {}).values():
                add(ev)
        return self._filter(e, deps.values())

    def _filter(self, e, evs):
        out = []
        for s, v in evs:
            if s is self.sem[e] and (v > self.cnt[e] or e == 'pe'):
                continue
            if self.seen[e].get(id(s), 0) < v:
                self.seen[e][id(s)] = v
                out.append((s, v))
        return out

    def _commit(self, ev, reads, writes):
        for k in writes:
            self.lastw[k] = ev
            self.reads[k] = {}
        s, v = ev
        for k in reads:
            d = self.reads.setdefault(k, {})
            old = d.get(id(s))
            if old is None or old[1] < v:
                d[id(s)] = ev

    def op(self, e, fn, reads=(), writes=(), inc=True):
        waits = self._deps(e, reads, writes)
        sem = self.sem[e]
        ev = (sem, self.cnt[e] + 1)
        if inc:
            self.cnt[e] += 1
            self.pend[e] = False
        else:
            self.pend[e] = True

        def emit(eng, fn=fn, waits=waits, inc=inc, sem=sem):
            for s, v in waits:
                eng.wait_ge(s, v)
            ins = fn(eng)
            if inc:
                ins.then_inc(sem, 1)
        self.prog[e].append(emit)
        self._commit(ev, reads, writes)

    def dma(self, e, out, in_, reads=(), writes=(), fence=False, **kw):
        i = self.dcnt % self.NDMA
        gen = self.dcnt // self.NDMA
        self.dcnt += 1
        ds = self.dsem[i]
        waits = self._deps(e, reads, writes)
        if fence:
            waits += self._filter(e, self.last_barrier)
        if gen > 0:
            waits += self._filter(e, [(ds, 16 * gen)])
        ev = (ds, 16 * (gen + 1))
        self.all_dma[i] = ev

        def emit(eng, waits=waits, ds=ds):
            for s, v in waits:
                eng.wait_ge(s, v)
            eng.dma_start(out=out, in_=in_, **kw).then_inc(ds, 16)
        self.prog[e].append(emit)
        self._commit(ev, reads, writes)
        if e == 'sp':
            self.sp_events.append(ev)
        return ev

    def swdma(self, sem, ev, out, in_, reads=(), writes=(), **kw):
        e = 'pool'
        waits = self._deps(e, reads, writes)

        def emit(eng, waits=waits, sem=sem):
            for s, v in waits:
                eng.wait_ge(s, v)
            eng.sem_clear(sem)
            eng.dma_start(out=out, in_=in_, **kw).then_inc(sem, 16)
        self.prog[e].append(emit)
        self._commit(ev, reads, writes)

    def publish(self, sem, ready_sem):
        def emit(eng, sem=sem, ready_sem=ready_sem):
            eng.wait_ge(sem, 16)
            eng.sem_inc(ready_sem, 1)
        self.prog['pool'].append(emit)

    def barrier(self, engines=('pe', 'act', 'dve', 'pool')):
        evs = [(self.sem[w], self.cnt[w]) for w in engines if self.cnt[w] > 0] + list(self.sp_events)
        self.last_barrier = [(self.sem[w], self.cnt[w]) for w in engines if self.cnt[w] > 0]
        self.sp_events = []
        for e in engines:
            waits = self._filter(e, evs)

            def emit(eng, waits=waits):
                for s, v in waits:
                    eng.wait_ge(s, v)
            self.prog[e].append(emit)

    def final_wait(self, e='sp'):
        evs = list(self.all_dma.values()) + [(self.sem[w], self.cnt[w]) for w in self.names if self.cnt[w] > 0 and w != e]
        waits = self._filter(e, evs)

        def emit(eng, waits=waits):
            for s, v in waits:
                eng.wait_ge(s, v)
        self.prog[e].append(emit)

    def finish(self, block, nc):
        for e in self.names:
            assert not self.pend[e], e
        decos = {'pe': block.tensor, 'act': block.scalar, 'dve': block.vector, 'pool': block.gpsimd, 'sp': block.sync}
        for name in self.names:
            prog = self.prog[name]

            def body(eng, prog=prog):
                for f in prog:
                    f(eng)
            decos[name](body)


def make_consts():
    i = np.arange(128)
    c = {}
    c['ident'] = np.eye(128, dtype=np.float32)
    c['ones'] = np.ones((128, 128), np.float32)
    c['A'] = (i[:, None] > i[None, :]).astype(np.float32)
    c['B'] = (i[:, None] <= i[None, :]).astype(np.float32)
    c['Ms'] = (i[:, None] < i[None, :]).astype(np.float32)
    blk = (i[:, None] // 8 == i[None, :] // 8)
    c['Mblk'] = (blk & ((i[:, None] % 8) <= (i[None, :] % 8))).astype(np.float32)
    rep = np.zeros((128, 128), np.float32)
    for r in range(8):
        rep[r, (i % 8) == r] = 1.0
    c['Rep8'] = rep
    sel = np.zeros((128, 128), np.float32)
    sel[127, :] = 1.0
    c['SelLast'] = sel
    c['Ablk'] = (blk & ((i[:, None] % 8) > (i[None, :] % 8))).astype(np.float32)
    c['SelLastS'] = (blk & ((i[:, None] % 8) == 7)).astype(np.float32)
    ss = np.zeros((128, 128), np.float32)
    for q in range(16):
        ss[q * 8:(q + 1) * 8, q] = 1.0
        ss[q * 8 + 7, 16 + q] = 1.0
    c['SeqSel'] = ss
    names = ['ident', 'ones', 'A', 'B', 'Ms', 'Mblk', 'Rep8', 'SelLast', 'Ablk', 'SelLastS', 'SeqSel']
    return names, np.concatenate([c[n] for n in names], axis=1)


CONST_NAMES, CONST_ARR = make_consts()


class Builder:
    def __init__(self, stages=999, dbg=False, plan=None):
        self.stages = stages
        if plan is None:
            plan = [(l, p) for l in range(DEPTH) for p in range(3)][:min(stages, 12)]
        self.plan = plan
        self.nl = DEPTH
        self.nc = bass.Bass("TRN2", target_bir_lowering=False)
        self.dbg = dbg

    def sb(self, name, shape, dt, off):
        return self.nc.alloc_sbuf_tensor_at(name, shape, dt, offset=self.arena0 + off)

    def din(self, name, shape, dt=F32):
        return self.nc.dram_tensor(name, list(shape), dt, kind="ExternalInput").ap()

    def dout(self, name, shape, dt=F32):
        return self.nc.dram_tensor(name, list(shape), dt, kind="ExternalOutput").ap()

    def wstream(self, sources):
        state = {'issued': 0, 'taken': 0, 'slots': []}

        def issue():
            k = state['issued']
            src, nelem, cid = sources[k]
            gi = self.wcnt
            self.wcnt += 1
            ri = gi % NRING
            ring = self.ring[ri]
            if cid in self.wc_idx:
                idx = self.wc_idx[cid]
                self.S.dma('sp', ring[:, 0:nelem], self.wcache[idx, :, 0:nelem], reads=['wc%d' % idx], writes=['ring%d' % ri])
            else:
                si = self.scnt % NSTG
                self.scnt += 1
                stg = self.stg[si]
                self.S.dma('sp', stg[:, 0:nelem], src, writes=['stg%d' % si])
                ce = CAST_PATTERN[self.scnt % len(CAST_PATTERN)]
                if ce == 'act':
                    self.S.op('act', lambda e: e.activation(out=ring[:, 0:nelem], in_=stg[:, 0:nelem], func=AF.Copy),
                              reads=['stg%d' % si], writes=['ring%d' % ri])
                else:
                    self.S.op(ce, lambda e: e.tensor_copy(out=ring[:, 0:nelem], in_=stg[:, 0:nelem]),
                              reads=['stg%d' % si], writes=['ring%d' % ri])
                if cid is not None and len(self.wc_idx) < WC_UNITS:
                    idx = len(self.wc_idx)
                    self.wc_idx[cid] = idx
                    self.S.dma('sp', self.wcache[idx, :, 0:nelem], ring[:, 0:nelem], reads=['ring%d' % ri], writes=['wc%d' % idx])
            state['slots'].append((ring, 'ring%d' % ri))
            state['issued'] += 1

        def nxt():
            while state['issued'] < len(sources) and state['issued'] <= state['taken'] + LOOKAHEAD:
                issue()
            r = state['slots'][state['taken']]
            state['taken'] += 1
            return r
        return nxt

    def flush_w(self, upto=None):
        if upto is None:
            upto = self.wcnt - 1
        while self.wpub <= upto:
            self.S.publish(self.rsem[self.wpub % NRING], self.wready)
            self.wpub += 1

    def build(self):
        nc = self.nc
        with ExitStack() as st:
            self.st = st
            S = self.S = Sched(nc, st)
            self.wcnt = 0
            self.scnt = 0
            self.wc_idx = {}
            self.wcache = nc.dram_tensor('wcache', [WC_UNITS, 128, WBUF], BF16, kind='Internal').ap()
            self.wpub = 0
            self.wready = st.enter_context(nc.semaphore('wready'))
            self.rsem = [st.enter_context(nc.semaphore('r%d' % i)) for i in range(NRING)]
            I = self.I = {}
            I['xT'] = self.din('xT', [D, T])
            I['consts'] = self.din('consts', [128, CONST_ARR.shape[1]])
            I['nw'] = self.din('nw', [128, DEPTH * 6 * DC])
            I['ffn_w_in'] = self.din('ffn_w_in', [self.nl, 2, FC, 128, 2048])
            I['ffn_w_out'] = self.din('ffn_w_out', [self.nl, 2, 8, 2, 128, 11 * 128])
            I['cm_wv'] = self.din('cm_wv', [8, 128, 2048])
            I['cm_wu'] = self.din('cm_wu', [8, 128, 2048])
            I['cm_wo'] = self.din('cm_wo', [8, 128, 2048])
            I['cm_bv'] = self.din('cm_bv', [128, 2048])
            I['cm_lnw'] = self.din('cm_lnw', [128, 2048])
            I['cm_lnb'] = self.din('cm_lnb', [128, 2048])
            I['cm_cols'] = self.din('cm_cols', [128, 48])
            I['cm_wsT'] = self.din('cm_wsT', [128, 8, 128])
            I['cm_wsblk'] = self.din('cm_wsblk', [128, 8, 128])
            I['cm_bs'] = self.din('cm_bs', [128, 8, 128])
            I['cm_bss'] = self.din('cm_bss', [128, 8, 128])
            I['ss_wz'] = self.din('ss_wz', [8, 128, 2048])
            I['ss_wx'] = self.din('ss_wx', [12, 128, 2048])
            I['ss_wdt'] = self.din('ss_wdt', [128, 256])
            I['ss_wo'] = self.din('ss_wo', [8, 128, 2048])
            I['ss_cols'] = self.din('ss_cols', [128, 120])
            I['ss_nwc'] = self.din('ss_nwc', [128, 16])
            I['ss_rows'] = self.din('ss_rows', [128, 96])
            I['ss_s0T'] = self.din('ss_s0T', [NS, 128, 2048])
            I['ss_cs'] = self.din('ss_cs', [128, 24 * 48])
            I['gd_wx'] = self.din('gd_wx', [2, 12, 128, 2048])
            I['gd_wg'] = self.din('gd_wg', [2, 4, 128, 2048])
            I['gd_wab'] = self.din('gd_wab', [2, 128, 128])
            I['gd_wo'] = self.din('gd_wo', [2, 8, 128, 1024])
            I['gd_cols'] = self.din('gd_cols', [2, 128, 96])
            I['gd_rows'] = self.din('gd_rows', [2, 128, 16])
            I['gd_nwc'] = self.din('gd_nwc', [2, 128, 1])
            I['gd_s0'] = self.din('gd_s0', [2, NS, 8, 128, 128])
            I['gd_cs'] = self.din('gd_cs', [2, 128, 24 * 48])
            O = self.O = {}
            O['yT'] = self.dout('yT', [D, T])
            O['gdn_state_p'] = self.dout('gdn_state_p', [2, 8, 128, 128])
            O['gdn_state_s'] = self.dout('gdn_state_s', [2, NS, 8, 128, 128])
            O['gdn_conv_p'] = self.dout('gdn_conv_p', [2, 128, 72])
            O['gdn_conv_s'] = self.dout('gdn_conv_s', [2, 128, 24 * 48])
            O['ssd_pT'] = self.dout('ssd_pT', [128, 2048])
            O['ssd_sT'] = self.dout('ssd_sT', [NS, 128, 2048])
            O['ssd_conv_p'] = self.dout('ssd_conv_p', [128, 72])
            O['ssd_conv_s'] = self.dout('ssd_conv_s', [128, 24 * 48])
            O['cmlp_v'] = self.dout('cmlp_v', [NS * LS, 2048])
            base0 = nc.sbuf_base
            self.arena0 = (base0 + 31) // 32 * 32
            total = nc.sbuf_top - self.arena0 - 64
            self.arena = st.enter_context(nc.sbuf_tensor("arena", [128, total], U8))
            off = 0
            self.X = self.sb("X", [128, DC, T], F32, off); off += DC * T * 4
            self.ring = []
            for i in range(NRING):
                self.ring.append(self.sb("ring%d" % i, [128, WBUF], BF16, off)); off += WBUF * 2
            self.stg = []
            for i in range(NSTG):
                self.stg.append(self.sb("stg%d" % i, [128, WBUF], F32, off)); off += WBUF * 4
            ncst = CONST_ARR.shape[1]
            self.cst = self.sb("cst", [128, ncst], F32, off); off += ncst * 4
            self.C = {n: self.cst[:, k * 128:(k + 1) * 128] for k, n in enumerate(CONST_NAMES)}
            self.ones_bf = self.sb("ones_bf", [128, 128], BF16, off); off += 256
            self.ident_bf = self.sb("ident_bf", [128, 128], BF16, off); off += 256
            self.nw = self.sb("nw", [128, DEPTH * 6 * DC], F32, off); off += DEPTH * 6 * DC * 4
            self.nwh = self.sb("nwh", [128, DEPTH * 6 * DC], F32, off); off += DEPTH * 6 * DC * 4
            self.epsc = self.sb("epsc", [128, 1], F32, off); off += 32
            self.scr0 = off
            self.scr_size = (total - off) // 64 * 64
            self.scratch = self.sb("scratch", [128, self.scr_size // 4], F32, off)
            self.ps = [st.enter_context(nc.psum_tensor("ps%d" % i, [128, 512], F32)) for i in range(8)]
            block = st.enter_context(nc.Block())
            S.dma('sp', self.cst[:], I['consts'], writes=['cst'])
            S.dma('sp', self.nw[:], I['nw'], writes=['nw'])
            for c in range(DC):
                S.dma('sp', self.X[:, c, :], I['xT'][c * 128:(c + 1) * 128, :], writes=['X'])
            S.op('dve', lambda e: e.tensor_copy(out=self.ones_bf[:], in_=self.C['ones']), reads=['cst'], writes=['ones_bf'])
            S.op('dve', lambda e: e.tensor_copy(out=self.ident_bf[:], in_=self.C['ident']), reads=['cst'], writes=['ident_bf'])
            S.op('dve', lambda e: e.tensor_scalar(out=self.nwh[:], in0=self.nw[:], scalar1=0.5, scalar2=None, op0=ALU.mult),
                 reads=['nw'], writes=['nwh'])
            S.op('dve', lambda e: e.memset(self.epsc[:], EPS), writes=['epsc'])
            S.barrier()
            for (l, part) in self.plan:
                if part == 0:
                    self.ffn(l, 0)
                elif part == 2:
                    self.ffn(l, 1)
                elif l % 3 == 1:
                    self.cmlp(l)
                elif l % 3 == 2:
                    self.ssd(l)
                else:
                    self.gdn(l)
                S.barrier()
            for c in range(DC):
                S.dma('sp', O['yT'][c * 128:(c + 1) * 128, :], self.X[:, c, :], reads=['X', 'Xf0', 'Xf1', 'Xf2'], fence=True)
            S.final_wait('sp')
            S.finish(block, nc)
        return nc

    def nwcol(self, l, i, c, half=False):
        k = (l * 6 + i) * DC + c
        return (self.nwh if half else self.nw)[:, k:k + 1]

    def scr(self, name, shape, dt, off):
        nb = int(np.prod(shape[1:])) * (2 if dt == BF16 else 4)
        assert off + nb <= self.scr_size, (name, off + nb, self.scr_size)
        assert off % 4 == 0 and nb % 4 == 0
        v = self.scratch[:, off // 4:(off + nb) // 4]
        if dt == BF16:
            v = v.bitcast(BF16)
        if len(shape) == 3:
            v = v.rearrange("p (a b) -> p a b", b=shape[2])
        return v

    def prenorm(self, l, idx, t0, n, hn, sq2, rstd, tag):
        S, X = self.S, self.X
        self.pn = getattr(self, 'pn', 0) + 1
        st_ps = self.ps[6 + self.pn % 2]
        kst = 'ps%d' % (6 + self.pn % 2)
        S.op('act', lambda e: e.activation(out=hn[:, :, 0:n], in_=X[:, :, t0:t0 + n], func=AF.Square), reads=['X'], writes=[tag + 'hn'])
        for c in range(DC):
            S.op('pe', lambda e, c=c: e.matmul(st_ps[:, 0:n], lhsT=self.ones_bf[:], rhs=hn[:, c, 0:n], start=(c == 0), stop=(c == DC - 1)),
                 reads=[tag + 'hn', 'ones_bf'], writes=[kst], inc=(c == DC - 1))
        S.op('act', lambda e: e.activation(out=rstd[:, 0:n], in_=st_ps[:, 0:n], func=AF.Ln, scale=1.0 / D, bias=self.epsc[:]),
             reads=[kst, 'epsc'], writes=[tag + 'rstd'])
        S.op('act', lambda e: e.activation(out=rstd[:, 0:n], in_=rstd[:, 0:n], func=AF.Exp, scale=-0.5),
             reads=[tag + 'rstd'], writes=[tag + 'rstd'])
        for c in range(DC):
            S.op('dve', lambda e, c=c: e.scalar_tensor_tensor(
                out=hn[:, c, 0:n], in0=X[:, c, t0:t0 + n], scalar=self.nwcol(l, idx, c), in1=rstd[:, 0:n],
                op0=ALU.mult, op1=ALU.mult), reads=['X', 'nw', tag + 'rstd'], writes=[tag + 'hn'])

    def outproj_residual(self, l, idx, t0, n, wnext, rhs_fn, nk, rhs_key, ysb, sqb, rstd, tag, half=False):
        S, X = self.S, self.X
        for d in range(DC):
            wb, wkey = wnext()
            self.yc = getattr(self, 'yc', 0) + 1
            yi = self.yc % 2
            yps = self.ps[yi]
            for k in range(nk):
                S.op('pe', lambda e, k=k, wb=wb, yps=yps: e.matmul(yps[:, 0:n], lhsT=wb[:, k * 128:(k + 1) * 128], rhs=rhs_fn(k), start=(k == 0), stop=(k == nk - 1)),
                     reads=[wkey, rhs_key], writes=['ps%d' % yi], inc=(k == nk - 1))
            S.op('dve', lambda e, d=d, yps=yps: e.tensor_copy(out=ysb[:, d, 0:n], in_=yps[:, 0:n]), reads=['ps%d' % yi], writes=[tag + 'ysb'])
            S.op('act', lambda e, d=d: e.activation(out=sqb[:, d, 0:n], in_=ysb[:, d, 0:n], func=AF.Square), reads=[tag + 'ysb'], writes=[tag + 'sqb'])
        self.pn = getattr(self, 'pn', 0) + 1
        st_ps = self.ps[6 + self.pn % 2]
        kst = 'ps%d' % (6 + self.pn % 2)
        for d in range(DC):
            S.op('pe', lambda e, d=d: e.matmul(st_ps[:, 0:n], lhsT=self.ones_bf[:], rhs=sqb[:, d, 0:n], start=(d == 0), stop=(d == DC - 1)),
                 reads=[tag + 'sqb', 'ones_bf'], writes=[kst], inc=(d == DC - 1))
        S.op('act', lambda e: e.activation(out=rstd[:, 0:n], in_=st_ps[:, 0:n], func=AF.Ln, scale=1.0 / D, bias=self.epsc[:]),
             reads=[kst, 'epsc'], writes=[tag + 'rstd'])
        S.op('act', lambda e: e.activation(out=rstd[:, 0:n], in_=rstd[:, 0:n], func=AF.Exp, scale=-0.5),
             reads=[tag + 'rstd'], writes=[tag + 'rstd'])
        for d in range(DC):
            S.op('dve', lambda e, d=d: e.tensor_tensor(out=ysb[:, d, 0:n], in0=ysb[:, d, 0:n], in1=rstd[:, 0:n], op=ALU.mult),
                 reads=[tag + 'ysb', tag + 'rstd'], writes=[tag + 'ysb'])
            S.op('dve', lambda e, d=d: e.scalar_tensor_tensor(
                out=X[:, d, t0:t0 + n], in0=ysb[:, d, 0:n], scalar=self.nwcol(l, idx, d, half=half), in1=X[:, d, t0:t0 + n],
                op0=ALU.mult, op1=ALU.add), reads=[tag + 'ysb', 'nw', 'nwh', 'X'], writes=['X'])

    def proj_conv(self, tag, hn, P, xc, cols, wnext, samp, has_bias, cn):
        S, ps = self.S, self.ps
        N = 128
        pend = []

        def silu(c):
            if has_bias:
                S.op('act', lambda e, c=c: e.activation(out=xc[:, c, :], in_=xc[:, c, :], func=AF.Silu, bias=cols[:, c, 4:5], scale=1.0),
                     reads=[tag + '_xc%d' % c, tag + '_cols'], writes=[tag + '_xc%d' % c])
            else:
                S.op('act', lambda e, c=c: e.activation(out=xc[:, c, :], in_=xc[:, c, :], func=AF.Silu), reads=[tag + '_xc%d' % c], writes=[tag + '_xc%d' % c])
        for u in range(12):
            wb, wkey = wnext()
            wv = wb[:, 0:2048].rearrange("p (c n) -> p c n", n=256)
            for jj in range(2):
                ch = 2 * u + jj
                cn[0] += 1
                bi_ = cn[0] % 2
                b = ps[bi_]
                for c in range(DC):
                    S.op('pe', lambda e, c=c, jj=jj, b=b, wv=wv: e.matmul(b[:, 0:N], lhsT=wv[:, c, jj * 128:(jj + 1) * 128], rhs=hn[:, c, :], start=(c == 0), stop=(c == DC - 1)),
                         reads=[wkey, tag + '_hn'], writes=['ps%d' % bi_], inc=(c == DC - 1))
                if samp:
                    S.op('act', lambda e, ch=ch, b=b: e.activation(out=P[:, ch, :].rearrange("p (s k) -> p s k", k=11)[:, :, 3:11],
                                                                 in_=b[:, 0:N].rearrange("p (s k) -> p s k", k=8), func=AF.Copy),
                         reads=['ps%d' % bi_], writes=[tag + '_P%d' % ch])
                else:
                    S.op('act', lambda e, ch=ch, b=b: e.activation(out=P[:, ch, 3:3 + N], in_=b[:, 0:N], func=AF.Copy),
                         reads=['ps%d' % bi_], writes=[tag + '_P%d' % ch])
            for jj in range(2):
                c = 2 * u + jj
                if samp:
                    pv = lambda j, c=c: P[:, c, :].rearrange("p (s k) -> p s k", k=11)[:, :, j:j + 8]
                    ov = xc[:, c, :].rearrange("p (s k) -> p s k", k=8)
                else:
                    pv = lambda j, c=c: P[:, c, j:j + N]
                    ov = xc[:, c, :]
                S.op('act', lambda e, c=c, pv=pv, ov=ov: e.activation(out=ov, in_=pv(0), func=AF.Copy, scale=cols[:, c, 0:1]),
                     reads=[tag + '_P', tag + '_P%d' % c, tag + '_cols'], writes=[tag + '_xc%d' % c])
                for j in range(1, 4):
                    S.op('dve', lambda e, c=c, j=j, pv=pv, ov=ov: e.scalar_tensor_tensor(out=ov, in0=pv(j), scalar=cols[:, c, j:j + 1], in1=ov, op0=ALU.mult, op1=ALU.add),
                         reads=[tag + '_P', tag + '_P%d' % c, tag + '_cols', tag + '_xc%d' % c], writes=[tag + '_xc%d' % c])
            for c in pend:
                silu(c)
            pend = [2 * u, 2 * u + 1]
        for c in pend:
            silu(c)

    def cmlp(self, l):
        S, I, ps = self.S, self.I, self.ps
        KB = 1024
        hn = self.scr("hn", [128, DC, 512], BF16, 0)
        vt = self.scr("vt", [128, 4, 2048], F32, 8 * KB)
        um_p = self.scr("um", [128, 16, 512], BF16, 8 * KB)
        ysb_p = self.scr("ysb", [128, DC, 512], F32, 24 * KB)
        vbf = self.scr("vbf", [128, 4, 2048], BF16, 40 * KB)
        um_s = self.scr("ums", [128, 16, 128], BF16, 44 * KB)
        ysb_s = self.scr("ysbs", [128, DC, 128], F32, 48 * KB)
        lnw_t = self.scr("lnwt", [128, 2048], F32, 16 * KB)
        lnb_t = self.scr("lnbt", [128, 2048], F32, 24 * KB)
        vout = self.scr("vout", [128, 2048], F32, 32 * KB)
        bv = self.scr("bv", [128, 2048], F32, 56 * KB)
        R = self.scr("R", [128, 16, 128], F32, 64 * KB)
        ws_bf = self.scr("wsbf", [128, 8, 128], BF16, 72 * KB)
        wsb_bf = self.scr("wsbbf", [128, 8, 128], BF16, 74 * KB)
        cols = self.scr("cols", [128, 48], F32, 76 * KB)
        ug = self.scr("ug", [128, 512], F32, 76 * KB + 256)
        rstd = self.scr("rstd", [128, 512], F32, 78 * KB + 256)
        mb = self.scr("mb", [128, 512], F32, 80 * KB + 256)
        sq2 = self.scr("sq2", [128, 2, 512], BF16, 82 * KB + 256)
        bst = self.scr("bst", [128, 4, 6], F32, 84 * KB + 256)
        mv = self.scr("mv", [128, 2], F32, 84 * KB + 384)
        rs1 = self.scr("rs1", [128, 1], F32, 84 * KB + 416)
        ws_st = self.scr("wsst", [128, 8, 128], F32, 8 * KB)
        bs_st = self.scr("bsst", [128, 8, 128], F32, 12 * KB)
        S.barrier()
        S.dma('sp', bv[:], I['cm_bv'], writes=['cm_bv'], fence=True)
        S.dma('sp', cols[:], I['cm_cols'], writes=['cm_cols'], fence=True)

        def setup_R(ws_src, bs_src, mask, wdst, tagk):
            S.dma('sp', ws_st[:], ws_src, writes=['cm_wsst'], fence=True)
            S.dma('sp', bs_st[:], bs_src, writes=['cm_bsst'], fence=True)
            S.op('dve', lambda e: e.tensor_tensor(out=wdst[:], in0=ws_st[:], in1=mask.unsqueeze(1).broadcast_to([128, 8, 128]), op=ALU.mult),
                 reads=['cm_wsst', 'cst'], writes=[tagk])
            for h in range(8):
                b = ps[2 + h // 4]
                S.op('pe', lambda e, h=h, b=b: e.matmul(b[:, (h % 4) * 128:(h % 4 + 1) * 128], lhsT=self.ones_bf[:], rhs=wdst[:, h, :], start=True, stop=True),
                     reads=[tagk, 'ones_bf'], writes=['ps%d' % (2 + h // 4)], inc=True)
            for fc in range(16):
                h = fc // 2
                b = ps[2 + h // 4]
                S.op('dve', lambda e, fc=fc, h=h, b=b: e.scalar_tensor_tensor(
                    out=R[:, fc, :], in0=b[:, (h % 4) * 128:(h % 4 + 1) * 128], scalar=cols[:, 32 + fc:33 + fc], in1=bs_st[:, h, :],
                    op0=ALU.mult, op1=ALU.add), reads=['ps%d' % (2 + h // 4), 'cm_cols', 'cm_bsst'], writes=['cm_R'])

        setup_R(I['cm_wsT'], I['cm_bs'], self.C['B'], ws_bf, 'cm_ws')
        srcs = []
        for _ in BLOCKS:
            srcs += [(I['cm_wv'][q], 2048, ('cv', q)) for q in range(8)]
            srcs += [(I['cm_wu'][q], 2048, ('cu', q)) for q in range(8)]
            srcs += [(I['cm_wo'][d], 2048, ('co', d)) for d in range(8)]
        wnext = self.wstream(srcs)
        vcl = [0]

        def do_block(bi, t0, n):
            samp = (t0 >= SEQ)
            NB = n // 128
            S.barrier()
            if samp:
                setup_R(I['cm_wsblk'], I['cm_bss'], self.C['Mblk'], wsb_bf, 'cm_wsb')
                S.barrier()
                S.dma('sp', lnw_t[:], I['cm_lnw'], writes=['cm_lnwt'], fence=True)
                S.dma('sp', lnb_t[:], I['cm_lnb'], writes=['cm_lnbt'], fence=True)
            wmix = wsb_bf if samp else ws_bf
            wmk = 'cm_wsb' if samp else 'cm_ws'
            um = um_s if samp else um_p
            ysb = ysb_s if samp else ysb_p
            self.prenorm(l, 2, t0, n, hn, sq2, rstd, 'cm_')
            for q in range(8):
                wb, wkey = wnext()
                wv = wb[:, 0:2048].rearrange("p (c n) -> p c n", n=256)
                for tt in range(NB):
                    vi = vcl[0] % 2
                    vcl[0] += 1
                    vps = ps[vi]
                    for c in range(DC):
                        S.op('pe', lambda e, c=c, tt=tt, vps=vps, wv=wv: e.matmul(vps[:, 0:256], lhsT=hn[:, c, tt * 128:(tt + 1) * 128], rhs=wv[:, c, :], start=(c == 0), stop=(c == DC - 1)),
                             reads=[wkey, 'cm_hn'], writes=['ps%d' % vi], inc=(c == DC - 1))
                    S.op('dve', lambda e, q=q, tt=tt, vps=vps: e.tensor_tensor(out=vt[:, tt, q * 256:(q + 1) * 256], in0=vps[:, 0:256], in1=bv[:, q * 256:(q + 1) * 256], op=ALU.add),
                         reads=['ps%d' % vi, 'cm_bv'], writes=['cm_vt%d' % tt])
                    S.op('act', lambda e, q=q, tt=tt: e.activation(out=vt[:, tt, q * 256:(q + 1) * 256], in_=vt[:, tt, q * 256:(q + 1) * 256], func=AF.Gelu),
                         reads=['cm_vt%d' % tt], writes=['cm_vt%d' % tt])
            for tt in range(NB):
                for g in range(4):
                    S.op('dve', lambda e, tt=tt, g=g: e.bn_stats(out=bst[:, g, :], in_=vt[:, tt, g * 512:(g + 1) * 512]),
                         reads=['cm_vt%d' % tt], writes=['cm_bst'])
                S.op('dve', lambda e: e.bn_aggr(out=mv[:], in_=bst[:].rearrange("p a b -> p (a b)")), reads=['cm_bst'], writes=['cm_mv'])
                S.op('act', lambda e: e.activation(out=rs1[:], in_=mv[:, 1:2], func=AF.Ln, scale=1.0, bias=self.epsc[:]),
                     reads=['cm_mv', 'epsc'], writes=['cm_rs1'])
                S.op('act', lambda e: e.activation(out=rs1[:], in_=rs1[:], func=AF.Exp, scale=-0.5),
                     reads=['cm_rs1'], writes=['cm_rs1'])
                S.op('dve', lambda e, tt=tt: e.tensor_scalar(out=vbf[:, tt, :], in0=vt[:, tt, :], scalar1=mv[:, 0:1], scalar2=rs1[:], op0=ALU.subtract, op1=ALU.mult),
                     reads=['cm_vt%d' % tt, 'cm_mv', 'cm_rs1'], writes=['cm_vbf'])
                if samp:
                    S.op('dve', lambda e, tt=tt: e.tensor_scalar(out=vout[:], in0=vt[:, tt, :], scalar1=mv[:, 0:1], scalar2=rs1[:], op0=ALU.subtract, op1=ALU.mult),
                         reads=['cm_vt%d' % tt, 'cm_mv', 'cm_rs1'], writes=['cm_vout'])
                    S.op('dve', lambda e: e.tensor_tensor(out=vout[:], in0=vout[:], in1=lnw_t[:], op=ALU.mult), reads=['cm_vout', 'cm_lnwt'], writes=['cm_vout'])
                    S.op('dve', lambda e: e.tensor_tensor(out=vout[:], in0=vout[:], in1=lnb_t[:], op=ALU.add), reads=['cm_vout', 'cm_lnbt'], writes=['cm_vout'])
                    S.dma('sp', self.O['cmlp_v'], vout[:], reads=['cm_vout'])
            for q in range(8):
                wb, wkey = wnext()
                wv = wb[:, 0:2048].rearrange("p (c n) -> p c n", n=256)
                for jj in range(2):
                    fc = 2 * q + jj
                    mi = fc % 2
                    mps, ups = ps[2 + mi], ps[4 + mi]
                    for tt in range(NB):
                        S.op('pe', lambda e, tt=tt, fc=fc, q=q, mps=mps: e.matmul(mps[:, tt * 128:(tt + 1) * 128], lhsT=vbf[:, tt, fc * 128:(fc + 1) * 128], rhs=wmix[:, q, :], start=True, stop=True),
                             reads=['cm_vbf', wmk], writes=['ps%d' % (2 + mi)], inc=(tt == NB - 1))
                    for c in range(DC):
                        S.op('pe', lambda e, c=c, jj=jj, ups=ups, wv=wv: e.matmul(ups[:, 0:n], lhsT=wv[:, c, jj * 128:(jj + 1) * 128], rhs=hn[:, c, 0:n], start=(c == 0), stop=(c == DC - 1)),
                             reads=[wkey, 'cm_hn'], writes=['ps%d' % (4 + mi)], inc=(c == DC - 1))
                    S.op('act', lambda e, fc=fc, ups=ups: e.activation(out=ug[:, 0:n], in_=ups[:, 0:n], func=AF.Gelu, bias=cols[:, fc:fc + 1], scale=1.0),
                         reads=['ps%d' % (4 + mi), 'cm_cols'], writes=['cm_ug'])
                    S.op('dve', lambda e, fc=fc, mps=mps: e.scalar_tensor_tensor(
                        out=mb[:, 0:n].rearrange("p (a b) -> p a b", b=128), in0=mps[:, 0:n].rearrange("p (a b) -> p a b", b=128),
                        scalar=cols[:, 16 + fc:17 + fc], in1=R[:, fc:fc + 1, :].broadcast_to([128, NB, 128]),
                        op0=ALU.mult, op1=ALU.add), reads=['ps%d' % (2 + mi), 'cm_cols', 'cm_R'], writes=['cm_mb'])
                    S.op('dve', lambda e, fc=fc: e.tensor_tensor(out=um[:, fc, 0:n], in0=ug[:, 0:n], in1=mb[:, 0:n], op=ALU.mult),
                         reads=['cm_ug', 'cm_mb'], writes=['cm_um'])
            self.outproj_residual(l, 3, t0, n, wnext, lambda k: um[:, k, 0:n], 16, 'cm_um', ysb, hn, rstd, 'cm_')

        for bi, (t0, n) in enumerate(BLOCKS):
            do_block(bi, t0, n)

    def ssd(self, l):
        S, I, O, ps, C = self.S, self.I, self.O, self.ps, self.C
        N = 128
        o = [0]

        def take(name, shape, dt):
            v = self.scr(name, shape, dt, o[0])
            o[0] += (int(np.prod(shape[1:])) * (2 if dt == BF16 else 4) + 63) // 64 * 64
            return v
        hn = take("hn", [128, DC, N], BF16)
        p0 = o[0]
        P = take("P", [128, 24, 176], F32)
        p1 = o[0]
        xc = take("xc", [128, 24, N], F32)
        xtok = take("xtok", [128, 2048], F32)
        zt = take("zt", [128, 2048], F32)
        yacc = take("yacc", [128, 2048], F32)
        tmpf = take("tmpf", [128, 2048], F32)
        ST = take("ST", [128, 2048], F32)
        ST_bf = take("STbf", [128, 2048], BF16)
        cols = take("cols", [128, 24, 5], F32)
        nwc = take("nwc", [128, 16], F32)
        rows = take("rows", [128, 3, 32], F32)
        dtt = take("dtt", [128, 32], F32)
        ga = take("ga", [128, 32], F32)
        Gt = take("Gt", [128, 32], F32)
        Et = take("Et", [128, 32], F32)
        dec = take("dec", [128, 32], F32)
        cdbc = take("cdbc", [128, 32], F32)
        arow = take("arow", [128, 32], F32)
        ss4 = take("ss4", [128, 4], F32)
        cdall = take("cdall", [128, 16, 32], F32)
        carry = take("carry", [128, 24, 3], F32)
        rstd = take("rstd", [128, N], F32)
        sq2 = take("sq2", [128, 2, N], BF16)
        stg2 = take("stg2", [128, 2, 512], F32)
        sbf2 = take("sbf2", [128, 2, 512], BF16)
        o[0] = p0
        ynT = take("ynT", [128, 16, N], BF16)
        ysb = take("ysb", [128, DC, N], F32)
        MT = take("MT", [128, 8, N], BF16)
        BT_bf = take("BTbf", [128, 4, N], BF16)
        CT_bf = take("CTbf", [128, 4, N], BF16)
        Btok = take("Btok", [128, 4, N], BF16)
        cbTm = take("cbTm", [128, N], F32)
        gsel = take("gsel", [128, 16, 32], F32)
        Bm2 = take("Bm2", [128, 2, N], BF16)
        assert o[0] <= p1
        o[0] = p1
        xdt = take("xdt", [128, 2048], BF16)
        xdtd = take("xdtd", [128, 2048], BF16)
        CTm = take("CTm", [128, 16, N], BF16)
        assert o[0] <= p1 + 24 * N * 4
        tP = tmpf[:, 0:1152].rearrange("p (c k) -> p c k", k=48)
        gB = tmpf[:, 0:1024].rearrange("p (r i) -> p r i", i=N)
        tY = tmpf[:, 1024:1536]
        outb = tmpf[:, 1536:2048]

        S.barrier()
        S.dma('sp', cols[:], I['ss_cols'].rearrange("p (c k) -> p c k", k=5), writes=['ss_cols'], fence=True)
        S.dma('sp', nwc[:], I['ss_nwc'], writes=['ss_nwc'], fence=True)
        S.dma('sp', rows[:], I['ss_rows'].rearrange("p (a b) -> p a b", b=32), writes=['ss_rows'], fence=True)
        S.op('act', lambda e: e.activation(out=arow[:], in_=rows[:, 0, :], func=AF.Exp), reads=['ss_rows'], writes=['ss_arow'])
        S.op('dve', lambda e: e.tensor_scalar(out=arow[:], in0=arow[:], scalar1=-1.0, scalar2=None, op0=ALU.mult), reads=['ss_arow'], writes=['ss_arow'])
        S.op('dve', lambda e: e.memset(ST[:], 0.0), writes=['ss_ST'])
        S.op('dve', lambda e: e.memset(ST_bf[:], 0.0), writes=['ss_STbf'])
        S.op('dve', lambda e: e.memset(carry[:], 0.0), writes=['ss_carry'])
        srcs = []
        for _ in BLOCKS128:
            srcs += [(I['ss_wx'][u], 2048, ('sx', u)) for u in range(12)]
            srcs += [(I['ss_wdt'], 256, ('sd',))]
            srcs += [(I['ss_wz'][q], 2048, ('sz', q)) for q in range(8)]
            srcs += [(I['ss_wo'][d], 2048, ('so', d)) for d in range(8)]
        wnext = self.wstream(srcs)
        cn = [0]

        def do_block(bi, t0):
            samp = t0 >= SEQ
            last_p = (t0 == SEQ - N)
            Am, Bmk, Sel = (C['Ablk'], C['Mblk'], C['SelLastS']) if samp else (C['A'], C['B'], C['SelLast'])
            S.barrier()
            self.prenorm(l, 2, t0, N, hn, sq2, rstd, 'ss_')
            if samp:
                S.dma('sp', tP, I['ss_cs'].rearrange("p (c k) -> p c k", k=48), writes=['ss_tmpf'], fence=True)
                for c in range(24):
                    S.op('dve', lambda e, c=c: e.tensor_copy(out=P[:, c, :].rearrange("p (s k) -> p s k", k=11)[:, :, 0:3],
                                                            in_=tP[:, c, :].rearrange("p (s k) -> p s k", k=3)),
                         reads=['ss_tmpf'], writes=['ss_P'])
            else:
                S.op('dve', lambda e: e.tensor_copy(out=P[:, :, 0:3], in_=carry[:]), reads=['ss_carry'], writes=['ss_P'])
            self.proj_conv('ss', hn, P, xc, cols, wnext, samp, True, cn)
            if samp:
                for c in range(24):
                    S.op('dve', lambda e, c=c: e.tensor_copy(out=tP[:, c, :].rearrange("p (s k) -> p s k", k=3),
                                                            in_=P[:, c, :].rearrange("p (s k) -> p s k", k=11)[:, :, 8:11]),
                         reads=['ss_P'] + ['ss_P%d' % c_ for c_ in range(24)], writes=['ss_tmpf'])
                S.dma('sp', O['ssd_conv_s'].rearrange("p (c k) -> p c k", k=48), tP, reads=['ss_tmpf'])
                S.barrier()
            else:
                S.op('dve', lambda e: e.tensor_copy(out=carry[:], in_=P[:, :, N:N + 3]), reads=['ss_P'] + ['ss_P%d' % c_ for c_ in range(24)], writes=['ss_carry'])
                if last_p:
                    S.dma('sp', O['ssd_conv_p'].rearrange("p (c k) -> p c k", k=3), carry[:], reads=['ss_carry'])
            xck = ['ss_xc%d' % c for c in range(24)]
            for q in range(4):
                b = ps[2 + q % 2]
                for k in range(4):
                    S.op('pe', lambda e, q=q, k=k, b=b: e.transpose(b[:, k * 128:(k + 1) * 128], xc[:, 4 * q + k, :], C['ident']),
                         reads=xck[4 * q:4 * q + 4] + ['cst'], writes=['ps%d' % (2 + q % 2)], inc=(k == 3))
                S.op('act', lambda e, q=q, b=b: e.activation(out=xtok[:, q * 512:(q + 1) * 512], in_=b[:], func=AF.Copy),
                     reads=['ps%d' % (2 + q % 2)], writes=['ss_xtok'])
            b = ps[2]
            for k in range(4):
                S.op('pe', lambda e, k=k, b=b: e.transpose(b[:, k * 128:(k + 1) * 128], xc[:, 16 + k, :], C['ident']),
                     reads=xck[16:20] + ['cst'], writes=['ps2'], inc=(k == 3))
            S.op('act', lambda e, b=b: e.activation(out=Btok[:].rearrange("p a b -> p (a b)"), in_=b[:], func=AF.Copy), reads=['ps2'], writes=['ss_Btok'])
            S.op('dve', lambda e: e.tensor_copy(out=BT_bf[:], in_=xc[:, 16:20, :]), reads=xck[16:20], writes=['ss_BT'])
            S.op('dve', lambda e: e.tensor_copy(out=CT_bf[:], in_=xc[:, 20:24, :]), reads=xck[20:24], writes=['ss_CT'])
            wb, wkey = wnext()
            wv = wb[:, 0:256].rearrange("p (c n) -> p c n", n=32)
            b = ps[4]
            for c in range(DC):
                S.op('pe', lambda e, c=c, b=b, wv=wv: e.matmul(b[:, 0:32], lhsT=hn[:, c, :], rhs=wv[:, c, :], start=(c == 0), stop=(c == DC - 1)),
                     reads=[wkey, 'ss_hn'], writes=['ps4'], inc=(c == DC - 1))
            S.op('dve', lambda e, b=b: e.tensor_tensor(out=dtt[:], in0=b[:, 0:32], in1=rows[:, 1, :], op=ALU.add), reads=['ps4', 'ss_rows'], writes=['ss_dtt'])
            S.op('act', lambda e: e.activation(out=dtt[:], in_=dtt[:], func=AF.Exp), reads=['ss_dtt'], writes=['ss_dtt'])
            S.op('act', lambda e: e.activation(out=dtt[:], in_=dtt[:], func=AF.Ln, bias=C['ones'][:, 0:1], scale=1.0), reads=['ss_dtt', 'cst'], writes=['ss_dtt'])
            S.op('dve', lambda e: e.tensor_tensor(out=ga[:], in0=dtt[:], in1=arow[:], op=ALU.mult), reads=['ss_dtt', 'ss_arow'], writes=['ss_ga'])
            S.op('pe', lambda e, b=b: e.matmul(b[:, 32:64], lhsT=Bmk, rhs=ga[:], start=True, stop=True), reads=['cst', 'ss_ga'], writes=['ps4'])
            S.op('dve', lambda e, b=b: e.tensor_copy(out=Gt[:], in_=b[:, 32:64]), reads=['ps4'], writes=['ss_Gt'])
            S.op('act', lambda e: e.activation(out=Et[:], in_=Gt[:], func=AF.Exp), reads=['ss_Gt'], writes=['ss_Et'])
            S.op('pe', lambda e, b=b: e.matmul(b[:, 64:96], lhsT=Sel, rhs=Gt[:], start=True, stop=True), reads=['cst', 'ss_Gt'], writes=['ps4'])
            S.op('act', lambda e, b=b: e.activation(out=cdbc[:], in_=b[:, 64:96], func=AF.Exp), reads=['ps4'], writes=['ss_cdbc'])
            S.op('dve', lambda e, b=b: e.tensor_tensor(out=dec[:], in0=b[:, 64:96], in1=Gt[:], op=ALU.subtract), reads=['ps4', 'ss_Gt', 'ss_cdbc'], writes=['ss_dec'])
            S.op('act', lambda e: e.activation(out=dec[:], in_=dec[:], func=AF.Exp), reads=['ss_dec'], writes=['ss_dec'])
            S.op('dve', lambda e: e.tensor_tensor(out=tmpf[:].rearrange("p (h q) -> p h q", q=64), in0=xtok[:].rearrange("p (h q) -> p h q", q=64),
                                                  in1=dtt[:].unsqueeze(2).broadcast_to([128, 32, 64]), op=ALU.mult),
                 reads=['ss_xtok', 'ss_dtt', 'ss_tmpf'], writes=['ss_tmpf'])
            S.op('act', lambda e: e.activation(out=xdt[:], in_=tmpf[:], func=AF.Copy), reads=['ss_tmpf'] + xck, writes=['ss_xdt'])
            S.op('pool', lambda e: e.tensor_tensor(out=xdtd[:].rearrange("p (h q) -> p h q", q=64), in0=tmpf[:].rearrange("p (h q) -> p h q", q=64),
                                                  in1=dec[:].unsqueeze(2).broadcast_to([128, 32, 64]), op=ALU.mult),
                 reads=['ss_tmpf', 'ss_dec'] + xck, writes=['ss_xdtd'])
            for q in range(8):
                wb, wkey = wnext()
                wv = wb[:, 0:2048].rearrange("p (c n) -> p c n", n=256)
                cn[0] += 1
                bi_ = cn[0] % 2
                b = ps[bi_]
                for c in range(DC):
                    S.op('pe', lambda e, c=c, b=b, wv=wv: e.matmul(b[:, 0:256], lhsT=hn[:, c, :], rhs=wv[:, c, :], start=(c == 0), stop=(c == DC - 1)),
                         reads=[wkey, 'ss_hn'], writes=['ps%d' % bi_], inc=(c == DC - 1))
                S.op('act', lambda e, q=q, b=b: e.activation(out=zt[:, q * 256:(q + 1) * 256], in_=b[:, 0:256], func=AF.Silu), reads=['ps%d' % bi_], writes=['ss_zt'])
            S.op('pool', lambda e: e.tensor_tensor(out=yacc[:].rearrange("p (h q) -> p h q", q=64), in0=xtok[:].rearrange("p (h q) -> p h q", q=64),
                                                  in1=rows[:, 2, :].unsqueeze(2).broadcast_to([128, 32, 64]), op=ALU.mult),
                 reads=['ss_xtok', 'ss_rows'], writes=['ss_yacc'])
            if samp:
                S.op('dve', lambda e: e.tensor_tensor(out=gsel[:], in0=Gt[:].unsqueeze(1).broadcast_to([128, 16, 32]),
                                                      in1=C['SeqSel'][:, 16:32].unsqueeze(2).broadcast_to([128, 16, 32]), op=ALU.mult),
                     reads=['ss_Gt', 'cst'], writes=['ss_gsel'])
                S.op('pe', lambda e: e.matmul(ps[4][:], lhsT=C['ones'], rhs=gsel[:].rearrange("p a b -> p (a b)"), start=True, stop=True),
                     reads=['cst', 'ss_gsel'], writes=['ps4'])
                S.op('act', lambda e: e.activation(out=cdall[:].rearrange("p a b -> p (a b)"), in_=ps[4][:], func=AF.Exp), reads=['ps4'], writes=['ss_cdall'])
            for g in range(4):
                S.op('pe', lambda e, g=g: e.matmul(ps[2][:, 0:N], lhsT=BT_bf[:, g, :], rhs=CT_bf[:, g, :], start=True, stop=True),
                     reads=['ss_BT', 'ss_CT'], writes=['ps2'])
                S.op('dve', lambda e: e.tensor_tensor(out=cbTm[:], in0=ps[2][:, 0:N], in1=Bmk, op=ALU.mult), reads=['ps2', 'cst'], writes=['ss_cbTm'])
                S.op('pool', lambda e, g=g: e.tensor_tensor(out=gB, in0=ga[:, g * 8:(g + 1) * 8].unsqueeze(2).broadcast_to([128, 8, N]),
                                                           in1=Bmk.unsqueeze(1).broadcast_to([128, 8, N]), op=ALU.mult),
                     reads=['ss_ga', 'cst', 'ss_tmpf'], writes=['ss_tmpf'])
                for hh in range(2):
                    S.op('pe', lambda e, hh=hh: e.matmul(ps[3 + hh][:], lhsT=Am,
                                                         rhs=gB[:, hh * 4:(hh + 1) * 4, :].rearrange("p a b -> p (a b)"), start=True, stop=True),
                         reads=['cst', 'ss_tmpf'], writes=['ps%d' % (3 + hh)])
                    S.op('act', lambda e, hh=hh: e.activation(out=gB[:, hh * 4:(hh + 1) * 4, :].rearrange("p a b -> p (a b)"), in_=ps[3 + hh][:], func=AF.Exp),
                         reads=['ps%d' % (3 + hh), 'ss_tmpf'], writes=['ss_tmpf'])
                S.op('dve', lambda e: e.tensor_tensor(out=MT[:], in0=gB, in1=cbTm[:].unsqueeze(1).broadcast_to([128, 8, N]), op=ALU.mult),
                     reads=['ss_tmpf', 'ss_cbTm'], writes=['ss_MT'])
                for r in range(8):
                    h = g * 8 + r
                    S.op('pe', lambda e, r=r, h=h: e.matmul(ps[5][:, r * 64:(r + 1) * 64], lhsT=MT[:, r, :], rhs=xdt[:, h * 64:(h + 1) * 64], start=True, stop=True),
                         reads=['ss_MT', 'ss_xdt'], writes=['ps5'], inc=(r == 7))
                S.op('dve', lambda e, g=g: e.tensor_tensor(out=yacc[:, g * 512:(g + 1) * 512], in0=ps[5][:], in1=yacc[:, g * 512:(g + 1) * 512], op=ALU.add),
                     reads=['ps5', 'ss_yacc'], writes=['ss_yacc'])
                if not samp:
                    S.op('pe', lambda e, g=g: e.matmul(ps[6][:], lhsT=CT_bf[:, g, :], rhs=ST_bf[:, g * 512:(g + 1) * 512], start=True, stop=True),
                         reads=['ss_CT', 'ss_STbf'], writes=['ps6'])
                else:
                    S.op('dve', lambda e: e.memset(CTm[:], 0.0), reads=['ss_CTm'], writes=['ss_CTm'])
                    for s_ in range(NS):
                        S.op('dve', lambda e, g=g, s_=s_: e.tensor_copy(out=CTm[:, s_, s_ * 8:(s_ + 1) * 8], in_=CT_bf[:, g, s_ * 8:(s_ + 1) * 8]),
                             reads=['ss_CT', 'ss_CTm'], writes=['ss_CTm'])
                    for s_ in range(NS):
                        k2 = (g * NS + s_) % 2
                        S.dma('sp', stg2[:, k2, :], I['ss_s0T'][s_, :, g * 512:(g + 1) * 512], writes=['ss_stg%d' % k2])
                        S.op('act', lambda e, k2=k2: e.activation(out=sbf2[:, k2, :], in_=stg2[:, k2, :], func=AF.Copy), reads=['ss_stg%d' % k2], writes=['ss_sbf%d' % k2])
                        S.op('pe', lambda e, s_=s_, k2=k2: e.matmul(ps[6][:], lhsT=CTm[:, s_, :], rhs=sbf2[:, k2, :], start=(s_ == 0), stop=(s_ == NS - 1)),
                             reads=['ss_CTm', 'ss_sbf%d' % k2], writes=['ps6'], inc=True)
                        S.op('dve', lambda e, g=g, s_=s_, k2=k2: e.tensor_scalar(out=Bm2[:, k2, :], in0=Btok[:, g, :], scalar1=C['SeqSel'][:, s_:s_ + 1], scalar2=None, op0=ALU.mult),
                             reads=['ss_Btok', 'cst'], writes=['ss_Bm%d' % k2])
                        S.op('pe', lambda e, g=g, k2=k2: e.matmul(ps[7][:], lhsT=Bm2[:, k2, :], rhs=xdtd[:, g * 512:(g + 1) * 512], start=True, stop=True),
                             reads=['ss_Bm%d' % k2, 'ss_xdtd'], writes=['ps7'])
                        S.op('dve', lambda e, g=g, s_=s_, k2=k2: e.tensor_tensor(out=outb.rearrange("p (r q) -> p r q", q=64), in0=stg2[:, k2, :].rearrange("p (r q) -> p r q", q=64),
                                                                       in1=cdall[:, s_, g * 8:(g + 1) * 8].unsqueeze(2).broadcast_to([128, 8, 64]), op=ALU.mult),
                             reads=['ss_stg%d' % k2, 'ss_cdall', 'ss_outb'], writes=['ss_outb'])
                        S.op('dve', lambda e: e.tensor_tensor(out=outb, in0=ps[7][:], in1=outb, op=ALU.add), reads=['ps7', 'ss_outb'], writes=['ss_outb'])
                        S.dma('sp', O['ssd_sT'][s_, :, g * 512:(g + 1) * 512], outb, reads=['ss_outb'])
                S.op('dve', lambda e, g=g: e.tensor_tensor(out=tY.rearrange("p (r q) -> p r q", q=64), in0=ps[6][:].rearrange("p (r q) -> p r q", q=64),
                                                           in1=Et[:, g * 8:(g + 1) * 8].unsqueeze(2).broadcast_to([128, 8, 64]), op=ALU.mult),
                     reads=['ps6', 'ss_Et', 'ss_tY'], writes=['ss_tY'])
                S.op('dve', lambda e, g=g: e.tensor_tensor(out=yacc[:, g * 512:(g + 1) * 512], in0=tY, in1=yacc[:, g * 512:(g + 1) * 512], op=ALU.add),
                     reads=['ss_tY', 'ss_yacc'], writes=['ss_yacc'])
                if not samp:
                    S.op('pe', lambda e, g=g: e.matmul(ps[7][:], lhsT=Btok[:, g, :], rhs=xdtd[:, g * 512:(g + 1) * 512], start=True, stop=True),
                         reads=['ss_Btok', 'ss_xdtd'], writes=['ps7'])
                    S.op('pool', lambda e, g=g: e.tensor_tensor(out=ST[:, g * 512:(g + 1) * 512].rearrange("p (r q) -> p r q", q=64), in0=ST[:, g * 512:(g + 1) * 512].rearrange("p (r q) -> p r q", q=64),
                                                               in1=cdbc[:, g * 8:(g + 1) * 8].unsqueeze(2).broadcast_to([128, 8, 64]), op=ALU.mult),
                         reads=['ss_ST', 'ss_cdbc', 'ss_STbf'], writes=['ss_ST'])
                    S.op('dve', lambda e, g=g: e.tensor_tensor(out=ST[:, g * 512:(g + 1) * 512], in0=ps[7][:], in1=ST[:, g * 512:(g + 1) * 512], op=ALU.add),
                         reads=['ps7', 'ss_ST'], writes=['ss_ST'])
            if not samp:
                S.op('act', lambda e: e.activation(out=ST_bf[:], in_=ST[:], func=AF.Copy), reads=['ss_ST', 'ps6'], writes=['ss_STbf'])
                if last_p:
                    S.dma('sp', O['ssd_pT'], ST[:], reads=['ss_ST'])
            S.op('dve', lambda e: e.tensor_tensor(out=yacc[:], in0=yacc[:], in1=zt[:], op=ALU.mult), reads=['ss_yacc', 'ss_zt'], writes=['ss_yacc'])
            S.op('act', lambda e: e.activation(out=tmpf[:], in_=yacc[:], func=AF.Square), reads=['ss_yacc', 'ss_tmpf', 'ss_tY', 'ss_outb'], writes=['ss_tmpf'])
            S.op('dve', lambda e: e.tensor_reduce(out=ss4[:], in_=tmpf[:].rearrange("p (g q) -> p g q", q=512), axis=AX.X, op=ALU.add), reads=['ss_tmpf'], writes=['ss_ss4'])
            S.op('act', lambda e: e.activation(out=ss4[:], in_=ss4[:], func=AF.Sqrt, scale=1.0 / 512, bias=self.epsc[:]), reads=['ss_ss4', 'epsc'], writes=['ss_ss4'])
            S.op('dve', lambda e: e.reciprocal(out=ss4[:], in_=ss4[:]), reads=['ss_ss4'], writes=['ss_ss4'])
            S.op('dve', lambda e: e.tensor_tensor(out=yacc[:].rearrange("p (g q) -> p g q", q=512), in0=yacc[:].rearrange("p (g q) -> p g q", q=512),
                                                  in1=ss4[:].unsqueeze(2).broadcast_to([128, 4, 512]), op=ALU.mult), reads=['ss_yacc', 'ss_ss4'], writes=['ss_yacc'])
            for q in range(4):
                b = ps[2 + q % 2]
                for k in range(4):
                    S.op('pe', lambda e, q=q, k=k, b=b: e.transpose(b[:, k * 128:(k + 1) * 128], yacc[:, (4 * q + k) * 128:(4 * q + k + 1) * 128], C['ident']),
                         reads=['ss_yacc', 'cst'], writes=['ps%d' % (2 + q % 2)], inc=(k == 3))
                for k in range(4):
                    fc = 4 * q + k
                    S.op('act', lambda e, fc=fc, k=k, b=b: e.activation(out=ynT[:, fc, :], in_=b[:, k * 128:(k + 1) * 128], func=AF.Copy, scale=nwc[:, fc:fc + 1]),
                         reads=['ps%d' % (2 + q % 2), 'ss_nwc'], writes=['ss_ynT'])
            self.outproj_residual(l, 3, t0, N, wnext, lambda k: ynT[:, k, :], 16, 'ss_ynT', ysb, hn, rstd, 'ss_')

        for bi, (t0, n) in enumerate(BLOCKS128):
            do_block(bi, t0)

    def gdn(self, l):
        S, I, O, ps, C = self.S, self.I, self.O, self.ps, self.C
        jl = l // 3
        N = 128
        o = [0]

        def take(name, shape, dt):
            v = self.scr(name, shape, dt, o[0])
            o[0] += (int(np.prod(shape[1:])) * (2 if dt == BF16 else 4) + 63) // 64 * 64
            return v
        hn = take("hn", [128, DC, N], BF16)
        p0 = o[0]
        P = take("P", [128, 24, 176], F32)
        p1 = o[0]
        xc = take("xc", [128, 24, N], F32)
        p2 = o[0]
        gt = take("gt", [128, 8, N], F32)
        qT = take("qT", [128, 8, N], BF16)
        kT = take("kT", [128, 8, N], BF16)
        kdec = take("kdec", [128, 8, N], BF16)
        bV = take("bV", [128, 8, N], F32)
        gB = take("gB", [128, 8, N], F32)
        gam = take("gam", [128, 8, N], F32)
        attnT = take("attnT", [128, 8, N], BF16)
        mM = take("mM", [128, 8, N], F32)
        nM = take("nM", [128, 8, N], F32)
        Rm = take("Rm", [128, 8, N], F32)
        vnb = take("vnb", [128, 8, N], BF16)
        Sst = take("Sst", [128, 8, N], F32)
        Sbf = take("Sbf", [128, 8, N], BF16)
        onT = take("onT", [128, 8, N], BF16)
        ysb = take("ysb", [128, DC, N], F32)
        cols = take("cols", [128, 24, 4], F32)
        rows = take("rows", [128, 2, 8], F32)
        nwcol = take("nwcol", [128, 1], F32)
        negA = take("negA", [128, 8], F32)
        gsc = take("gsc", [128, 8], F32)
        beta = take("beta", [128, 8], F32)
        Gt = take("Gt", [128, 8], F32)
        Et = take("Et", [128, 8], F32)
        dec = take("dec", [128, 8], F32)
        cdbc = take("cdbc", [128, 8], F32)
        bE = take("bE", [128, 8], F32)
        ss8 = take("ss8", [128, 8], F32)
        cdall = take("cdall", [128, 16, 8], F32)
        gsel = take("gsel", [128, 16, 8], F32)
        carry = take("carry", [128, 24, 3], F32)
        rstd = take("rstd", [128, N], F32)
        sq2 = take("sq2", [128, 2, N], BF16)
        kdm = take("kdm", [128, 2, N], BF16)
        o[0] = p0
        Pa = take("Pa", [128, 8, N], F32)
        PTa = take("PTa", [128, 8, N], F32)
        Pb = take("Pb", [128, 8, N], F32)
        PTb = take("PTb", [128, 8, N], F32)
        assert o[0] <= p1
        o[0] = p0
        S0h = take("S0h", [128, 16, N], F32)
        assert o[0] <= p1
        o[0] = p1
        rr = take("rr", [128, 8, N], F32)
        oo = take("oo", [128, 8, N], F32)
        tmp4 = take("tmp4", [128, 8, N], F32)
        assert o[0] <= p2
        S.barrier()
        S.dma('sp', cols[:], I['gd_cols'][jl].rearrange("p (c k) -> p c k", k=4), writes=['gd_cols'], fence=True)
        S.dma('sp', rows[:], I['gd_rows'][jl].rearrange("p (a b) -> p a b", b=8), writes=['gd_rows'], fence=True)
        S.dma('sp', nwcol[:], I['gd_nwc'][jl], writes=['gd_nwc'], fence=True)
        S.op('act', lambda e: e.activation(out=negA[:], in_=rows[:, 0, :], func=AF.Exp), reads=['gd_rows'], writes=['gd_negA'])
        S.op('dve', lambda e: e.tensor_scalar(out=negA[:], in0=negA[:], scalar1=-1.0, scalar2=None, op0=ALU.mult), reads=['gd_negA'], writes=['gd_negA'])
        S.op('dve', lambda e: e.memset(Sst[:], 0.0), writes=['gd_S'])
        S.op('dve', lambda e: e.memset(Sbf[:], 0.0), writes=['gd_Sbf'])
        S.op('dve', lambda e: e.memset(carry[:], 0.0), writes=['gd_carry'])
        srcs = []
        for _ in BLOCKS128:
            srcs += [(I['gd_wx'][jl, u], 2048, ('gx', jl, u)) for u in range(12)]
            srcs += [(I['gd_wab'][jl], 128, ('ga', jl))]
            srcs += [(I['gd_wg'][jl, q], 2048, ('gg', jl, q)) for q in range(4)]
            srcs += [(I['gd_wo'][jl, d], 1024, ('go', jl, d)) for d in range(8)]
        wnext = self.wstream(srcs)
        cn = [0]
        fl = lambda t: t[:].rearrange("p a b -> p (a b)")

        def do_block(bi, t0):
            samp = t0 >= SEQ
            last_p = (t0 == SEQ - N)
            Am, Bmk, Sel = (C['Ablk'], C['Mblk'], C['SelLastS']) if samp else (C['A'], C['B'], C['SelLast'])
            nsq = 2 if samp else 6
            S.barrier()
            self.prenorm(l, 2, t0, N, hn, sq2, rstd, 'gd_')
            tP = self.scr("tP", [128, 24, 48], F32, self._off(gB))
            if samp:
                S.dma('sp', tP, I['gd_cs'][jl].rearrange("p (c k) -> p c k", k=48), writes=['gd_gB', 'gd_gam'], fence=True)
                for c in range(24):
                    S.op('dve', lambda e, c=c: e.tensor_copy(out=P[:, c, :].rearrange("p (s k) -> p s k", k=11)[:, :, 0:3],
                                                            in_=tP[:, c, :].rearrange("p (s k) -> p s k", k=3)),
                         reads=['gd_gB', 'gd_gam'], writes=['gd_P'])
            else:
                S.op('dve', lambda e: e.tensor_copy(out=P[:, :, 0:3], in_=carry[:]), reads=['gd_carry'], writes=['gd_P'])
            self.proj_conv('gd', hn, P, xc, cols, wnext, samp, False, cn)
            if samp:
                for c in range(24):
                    S.op('dve', lambda e, c=c: e.tensor_copy(out=tP[:, c, :].rearrange("p (s k) -> p s k", k=3),
                                                            in_=P[:, c, :].rearrange("p (s k) -> p s k", k=11)[:, :, 8:11]),
                         reads=['gd_P'] + ['gd_P%d' % c_ for c_ in range(24)], writes=['gd_gB', 'gd_gam'])
                S.dma('sp', O['gdn_conv_s'][jl].rearrange("p (c k) -> p c k", k=48), tP, reads=['gd_gB', 'gd_gam'])
            else:
                S.op('dve', lambda e: e.tensor_copy(out=carry[:], in_=P[:, :, N:N + 3]), reads=['gd_P'] + ['gd_P%d' % c_ for c_ in range(24)], writes=['gd_carry'])
                if last_p:
                    S.dma('sp', O['gdn_conv_p'][jl].rearrange("p (c k) -> p c k", k=3), carry[:], reads=['gd_carry'])
            xck = ['gd_xc%d' % c for c in range(24)]
            wb, wkey = wnext()
            wv = wb[:, 0:128].rearrange("p (c n) -> p c n", n=16)
            for c in range(DC):
                S.op('pe', lambda e, c=c, wv=wv: e.matmul(ps[6][:, 0:16], lhsT=hn[:, c, :], rhs=wv[:, c, :], start=(c == 0), stop=(c == DC - 1)),
                     reads=[wkey, 'gd_hn'], writes=['ps6'], inc=(c == DC - 1))
            S.op('dve', lambda e: e.tensor_tensor(out=gsc[:], in0=ps[6][:, 0:8], in1=rows[:, 1, :], op=ALU.add), reads=['ps6', 'gd_rows'], writes=['gd_gsc'])
            S.op('act', lambda e: e.activation(out=beta[:], in_=ps[6][:, 8:16], func=AF.Sigmoid), reads=['ps6', 'gd_gsc'], writes=['gd_beta'])
            S.op('act', lambda e: e.activation(out=gsc[:], in_=gsc[:], func=AF.Exp), reads=['gd_gsc'], writes=['gd_gsc'])
            S.op('act', lambda e: e.activation(out=gsc[:], in_=gsc[:], func=AF.Ln, bias=C['ones'][:, 0:1], scale=1.0), reads=['gd_gsc', 'cst'], writes=['gd_gsc'])
            S.op('dve', lambda e: e.tensor_tensor(out=gsc[:], in0=gsc[:], in1=negA[:], op=ALU.mult), reads=['gd_gsc', 'gd_negA'], writes=['gd_gsc'])
            S.op('pe', lambda e: e.matmul(ps[6][:, 32:40], lhsT=Bmk, rhs=gsc[:], start=True, stop=True), reads=['cst', 'gd_gsc'], writes=['ps6'])
            S.op('dve', lambda e: e.tensor_copy(out=Gt[:], in_=ps[6][:, 32:40]), reads=['ps6'], writes=['gd_Gt'])
            S.op('act', lambda e: e.activation(out=Et[:], in_=Gt[:], func=AF.Exp), reads=['gd_Gt'], writes=['gd_Et'])
            S.op('pe', lambda e: e.matmul(ps[6][:, 64:72], lhsT=Sel, rhs=Gt[:], start=True, stop=True), reads=['cst', 'gd_Gt'], writes=['ps6'])
            S.op('act', lambda e: e.activation(out=cdbc[:], in_=ps[6][:, 64:72], func=AF.Exp), reads=['ps6'], writes=['gd_cdbc'])
            S.op('dve', lambda e: e.tensor_tensor(out=dec[:], in0=ps[6][:, 64:72], in1=Gt[:], op=ALU.subtract), reads=['ps6', 'gd_Gt', 'gd_cdbc'], writes=['gd_dec'])
            S.op('act', lambda e: e.activation(out=dec[:], in_=dec[:], func=AF.Exp), reads=['gd_dec'], writes=['gd_dec'])
            S.op('dve', lambda e: e.tensor_tensor(out=bE[:], in0=beta[:], in1=Et[:], op=ALU.mult), reads=['gd_beta', 'gd_Et'], writes=['gd_bE'])
            for q in range(4):
                wb, wkey = wnext()
                wv = wb[:, 0:2048].rearrange("p (c n) -> p c n", n=256)
                cn[0] += 1
                bi_ = cn[0] % 2
                b = ps[bi_]
                for c in range(DC):
                    S.op('pe', lambda e, c=c, b=b, wv=wv: e.matmul(b[:, 0:256], lhsT=hn[:, c, :], rhs=wv[:, c, :], start=(c == 0), stop=(c == DC - 1)),
                         reads=[wkey, 'gd_hn'], writes=['ps%d' % bi_], inc=(c == DC - 1))
                S.op('act', lambda e, q=q, b=b: e.activation(out=fl(gt)[:, q * 256:(q + 1) * 256], in_=b[:, 0:256], func=AF.Silu), reads=['ps%d' % bi_], writes=['gd_gt'])
            sqv = fl(gam)[:, 0:256].bitcast(BF16)
            rtmp = fl(gam)[:, 512:1024]
            sq4 = fl(gam).bitcast(BF16).rearrange("p (g n) -> p g n", n=512)
            xgs = [xc[:, grp * 4:grp * 4 + 4, :].rearrange("p a b -> p (a b)") for grp in range(4)]
            for grp in range(4):
                S.op('act', lambda e, grp=grp: e.activation(out=sq4[:, grp, :], in_=xgs[grp], func=AF.Square),
                     reads=xck[grp * 4:grp * 4 + 4] + ['gd_gam'], writes=['gd_sq%d' % grp])
            for grp in range(4):
                b = ps[2 + grp]
                for k in range(4):
                    S.op('pe', lambda e, k=k, b=b, grp=grp: e.matmul(b[:, k * 128:(k + 1) * 128], lhsT=self.ones_bf[:], rhs=sq4[:, grp, k * 128:(k + 1) * 128], start=True, stop=True),
                         reads=['gd_sq%d' % grp, 'ones_bf'], writes=['ps%d' % (2 + grp)], inc=(k == 3))
            for grp in range(4):
                b = ps[2 + grp]
                S.op('act', lambda e, b=b: e.activation(out=b[:], in_=b[:], func=AF.Ln, scale=1.0, bias=self.epsc[:]),
                     reads=['ps%d' % (2 + grp), 'epsc'], writes=['ps%d' % (2 + grp)])
                S.op('act', lambda e, b=b: e.activation(out=b[:], in_=b[:], func=AF.Exp, scale=-0.5),
                     reads=['ps%d' % (2 + grp)], writes=['ps%d' % (2 + grp)])
            for grp in range(4):
                b = ps[2 + grp]
                dstT = qT if grp < 2 else kT
                hs = slice((grp % 2) * 4, (grp % 2) * 4 + 4)
                sc = (128.0 ** -0.5) if grp < 2 else 1.0
                S.op('dve', lambda e, grp=grp, dstT=dstT, hs=hs, sc=sc, b=b: e.scalar_tensor_tensor(out=dstT[:, hs, :].rearrange("p a b -> p (a b)"), in0=xgs[grp], scalar=sc, in1=b[:],
                                                                                          op0=ALU.mult, op1=ALU.mult),
                     reads=xck[grp * 4:grp * 4 + 4] + ['ps%d' % (2 + grp)], writes=['gd_qT' if grp < 2 else 'gd_kT'])
            S.op('dve', lambda e: e.memset(rtmp[:, 0:2], 0.0), writes=['gd_gam', 'gd_sq0', 'gd_sq1', 'gd_sq2', 'gd_sq3'])
            for half in range(2):
                b = ps[4 + half]
                for k in range(4):
                    h = half * 4 + k
                    S.op('pe', lambda e, k=k, h=h, b=b: e.matmul(b[:, k * 128:(k + 1) * 128], lhsT=kT[:, h, :], rhs=self.ident_bf[:], start=True, stop=True),
                         reads=['gd_kT', 'ident_bf'], writes=['ps%d' % (4 + half)], inc=(k == 3))
                S.op('dve', lambda e, half=half, b=b: e.tensor_tensor(out=kdec[:, half * 4:half * 4 + 4, :], in0=b[:].rearrange("p (a b) -> p a b", b=N),
                                                                    in1=dec[:, half * 4:half * 4 + 4].unsqueeze(2).broadcast_to([128, 4, N]), op=ALU.mult),
                     reads=['ps%d' % (4 + half), 'gd_dec'], writes=['gd_kdec'])
            for half in range(2):
                b = ps[2 + half]
                for k in range(4):
                    h = half * 4 + k
                    S.op('pe', lambda e, k=k, h=h, b=b: e.transpose(b[:, k * 128:(k + 1) * 128], xc[:, 16 + h, :], C['ident']),
                         reads=xck[16:24] + ['cst'], writes=['ps%d' % (2 + half)], inc=(k == 3))
                S.op('dve', lambda e, half=half, b=b: e.tensor_tensor(out=bV[:, half * 4:half * 4 + 4, :], in0=b[:].rearrange("p (a b) -> p a b", b=N),
                                                                    in1=beta[:, half * 4:half * 4 + 4].unsqueeze(2).broadcast_to([128, 4, N]), op=ALU.mult),
                     reads=['ps%d' % (2 + half), 'gd_beta'], writes=['gd_bV'])
            S.op('pool', lambda e: e.tensor_tensor(out=gB[:], in0=gsc[:].unsqueeze(2).broadcast_to([128, 8, N]), in1=Bmk.unsqueeze(1).broadcast_to([128, 8, N]), op=ALU.mult),
                 reads=['gd_gsc', 'cst', 'gd_gB'], writes=['gd_gB'])
            for hh in range(2):
                S.op('pe', lambda e, hh=hh: e.matmul(ps[6 + hh][:], lhsT=Am, rhs=gB[:, hh * 4:(hh + 1) * 4, :].rearrange("p a b -> p (a b)"), start=True, stop=True),
                     reads=['cst', 'gd_gB'], writes=['ps%d' % (6 + hh)])
                S.op('act', lambda e, hh=hh: e.activation(out=gam[:, hh * 4:(hh + 1) * 4, :].rearrange("p a b -> p (a b)"), in_=ps[6 + hh][:], func=AF.Exp),
                     reads=['ps%d' % (6 + hh), 'gd_gam'], writes=['gd_gam'])
            S.op('dve', lambda e: e.tensor_tensor(out=gam[:], in0=gam[:], in1=Bmk.unsqueeze(1).broadcast_to([128, 8, N]), op=ALU.mult), reads=['gd_gam', 'cst'], writes=['gd_gam'])
            for half in range(2):
                b = ps[4 + half]
                for k in range(4):
                    h = half * 4 + k
                    S.op('pe', lambda e, k=k, h=h, b=b: e.matmul(b[:, k * 128:(k + 1) * 128], lhsT=kT[:, h, :], rhs=qT[:, h, :], start=True, stop=True),
                         reads=['gd_kT', 'gd_qT'], writes=['ps%d' % (4 + half)], inc=(k == 3))
                S.op('dve', lambda e, half=half, b=b: e.tensor_tensor(out=attnT[:, half * 4:half * 4 + 4, :].rearrange("p a b -> p (a b)"), in0=b[:], in1=gam[:, half * 4:half * 4 + 4, :].rearrange("p a b -> p (a b)"), op=ALU.mult),
                     reads=['ps%d' % (4 + half), 'gd_gam'], writes=['gd_attnT'])
            S.op('pool', lambda e: e.tensor_tensor(out=gB[:], in0=gsc[:].unsqueeze(2).broadcast_to([128, 8, N]), in1=Am.unsqueeze(1).broadcast_to([128, 8, N]), op=ALU.mult),
                 reads=['gd_gsc', 'cst', 'gd_gB'], writes=['gd_gB'])
            for hh in range(2):
                S.op('pe', lambda e, hh=hh: e.matmul(ps[6 + hh][:], lhsT=Bmk, rhs=gB[:, hh * 4:(hh + 1) * 4, :].rearrange("p a b -> p (a b)"), start=True, stop=True),
                     reads=['cst', 'gd_gB'], writes=['ps%d' % (6 + hh)])
                S.op('act', lambda e, hh=hh: e.activation(out=gam[:, hh * 4:(hh + 1) * 4, :].rearrange("p a b -> p (a b)"), in_=ps[6 + hh][:], func=AF.Exp),
                     reads=['ps%d' % (6 + hh), 'gd_gam', 'gd_attnT'], writes=['gd_gam'])
            S.op('dve', lambda e: e.tensor_tensor(out=gam[:], in0=gam[:], in1=Am.unsqueeze(1).broadcast_to([128, 8, N]), op=ALU.mult), reads=['gd_gam', 'cst'], writes=['gd_gam'])
            S.op('dve', lambda e: e.tensor_tensor(out=gam[:], in0=gam[:], in1=beta[:].unsqueeze(2).broadcast_to([128, 8, N]), op=ALU.mult), reads=['gd_gam', 'gd_beta'], writes=['gd_gam'])
            for half in range(2):
                b = ps[4 + half]
                for k in range(4):
                    h = half * 4 + k
                    S.op('pe', lambda e, k=k, h=h, b=b: e.matmul(b[:, k * 128:(k + 1) * 128], lhsT=kT[:, h, :], rhs=kT[:, h, :], start=True, stop=True),
                         reads=['gd_kT'], writes=['ps%d' % (4 + half)], inc=(k == 3))
                S.op('dve', lambda e, half=half, b=b: e.tensor_tensor(out=mM[:, half * 4:half * 4 + 4, :].rearrange("p a b -> p (a b)"), in0=b[:], in1=gam[:, half * 4:half * 4 + 4, :].rearrange("p a b -> p (a b)"), op=ALU.mult),
                     reads=['ps%d' % (4 + half), 'gd_gam'], writes=['gd_mM'])
            for half in range(2):
                b = ps[2 + half]
                for k in range(4):
                    h = half * 4 + k
                    S.op('pe', lambda e, k=k, h=h, b=b: e.transpose(b[:, k * 128:(k + 1) * 128], mM[:, h, :], C['ident']),
                         reads=['gd_mM', 'cst'], writes=['ps%d' % (2 + half)], inc=(k == 3))
                S.op('act', lambda e, half=half, b=b: e.activation(out=nM[:, half * 4:half * 4 + 4, :].rearrange("p a b -> p (a b)"), in_=b[:], func=AF.Copy),
                     reads=['ps%d' % (2 + half)], writes=['gd_nM'])
            S.op('pool', lambda e: e.tensor_tensor(out=Rm[:], in0=C['ident'].unsqueeze(1).broadcast_to([128, 8, N]), in1=nM[:], op=ALU.subtract),
                 reads=['cst', 'gd_nM', 'gd_Rm'], writes=['gd_Rm'])
            Pc, PTc, kP, kPT = nM, mM, 'gd_nM', 'gd_mM'
            bufs = [(Pa, PTa, 'gd_Pa', 'gd_PTa'), (Pb, PTb, 'gd_Pb', 'gd_PTb')]
            for it in range(nsq):
                Pn, PTn, kPn, kPTn = bufs[it % 2]
                lastit = (it == nsq - 1)
                for half in range(2):
                    if not lastit:
                        b = ps[2 + half]
                        for k in range(4):
                            h = half * 4 + k
                            S.op('pe', lambda e, k=k, h=h, b=b, Pc=Pc, PTc=PTc: e.matmul(b[:, k * 128:(k + 1) * 128], lhsT=PTc[:, h, :], rhs=Pc[:, h, :], start=True, stop=True),
                                 reads=[kP, kPT], writes=['ps%d' % (2 + half)], inc=(k == 3))
                        S.op('act', lambda e, half=half, b=b, Pn=Pn: e.activation(out=Pn[:, half * 4:half * 4 + 4, :].rearrange("p a b -> p (a b)"), in_=b[:], func=AF.Copy),
                             reads=['ps%d' % (2 + half), kPn], writes=[kPn])
                    b = ps[4 + half]
                    for k in range(4):
                        h = half * 4 + k
                        S.op('pe', lambda e, k=k, h=h, b=b, Pc=Pc, PTc=PTc: e.matmul(b[:, k * 128:(k + 1) * 128], lhsT=Pc[:, h, :], rhs=PTc[:, h, :], start=True, stop=True),
                             reads=[kP, kPT], writes=['ps%d' % (4 + half)], inc=(k == 3))
                    S.op('act', lambda e, half=half, b=b, PTn=PTn: e.activation(out=PTn[:, half * 4:half * 4 + 4, :].rearrange("p a b -> p (a b)"), in_=b[:], func=AF.Copy),
                         reads=['ps%d' % (4 + half), kPTn], writes=[kPTn])
                for half in range(2):
                    b = ps[6 + half]
                    for k in range(4):
                        h = half * 4 + k
                        S.op('pe', lambda e, k=k, h=h, b=b, PTn=PTn: e.matmul(b[:, k * 128:(k + 1) * 128], lhsT=PTn[:, h, :], rhs=Rm[:, h, :], start=True, stop=True),
                             reads=[kPTn, 'gd_Rm'], writes=['ps%d' % (6 + half)], inc=(k == 3))
                for half in range(2):
                    b = ps[6 + half]
                    S.op('dve', lambda e, half=half, b=b: e.tensor_tensor(out=Rm[:, half * 4:half * 4 + 4, :].rearrange("p a b -> p (a b)"), in0=b[:], in1=Rm[:, half * 4:half * 4 + 4, :].rearrange("p a b -> p (a b)"), op=ALU.add),
                         reads=['ps%d' % (6 + half), 'gd_Rm'], writes=['gd_Rm'])
                Pc, PTc, kP, kPT = Pn, PTn, kPn, kPTn
            if not samp:
                for half in range(2):
                    for k in range(4):
                        h = half * 4 + k
                        S.op('pe', lambda e, k=k, h=h, half=half: e.matmul(ps[2 + half][:, k * 128:(k + 1) * 128], lhsT=kT[:, h, :], rhs=Sbf[:, h, :], start=True, stop=True),
                             reads=['gd_kT', 'gd_Sbf'], writes=['ps%d' % (2 + half)], inc=(k == 3))
                        S.op('pe', lambda e, k=k, h=h, half=half: e.matmul(ps[4 + half][:, k * 128:(k + 1) * 128], lhsT=qT[:, h, :], rhs=Sbf[:, h, :], start=True, stop=True),
                             reads=['gd_qT', 'gd_Sbf'], writes=['ps%d' % (4 + half)], inc=(k == 3))
            else:
                S.barrier()
                kTm = self.scr("kTm", [128, 16, N], BF16, self._off(mM))
                qTm = self.scr("qTm", [128, 16, N], BF16, self._off(nM))
                S0b = self.scr("S0b", [128, 16, N], BF16, self._off(gB))
                S.op('dve', lambda e: e.tensor_tensor(out=gsel[:], in0=Gt[:].unsqueeze(1).broadcast_to([128, 16, 8]),
                                                      in1=C['SeqSel'][:, 16:32].unsqueeze(2).broadcast_to([128, 16, 8]), op=ALU.mult), reads=['gd_Gt', 'cst'], writes=['gd_gsel'])
                S.op('pe', lambda e: e.matmul(ps[6][:, 0:128], lhsT=C['ones'], rhs=gsel[:].rearrange("p a b -> p (a b)"), start=True, stop=True), reads=['cst', 'gd_gsel'], writes=['ps6'])
                S.op('act', lambda e: e.activation(out=cdall[:].rearrange("p a b -> p (a b)"), in_=ps[6][:, 0:128], func=AF.Exp), reads=['ps6'], writes=['gd_cdall'])
                for h in range(8):
                    half, k = h // 4, h % 4
                    S.dma('sp', S0h[:], I['gd_s0'][jl, :, h].rearrange("s d e -> d s e"), writes=['gd_S0h'], fence=True)
                    S.op('act', lambda e: e.activation(out=fl(S0b), in_=fl(S0h), func=AF.Copy), reads=['gd_S0h', 'gd_S0b'], writes=['gd_S0b'])
                    S.op('dve', lambda e: e.memset(kTm[:], 0.0), reads=['gd_kTm'], writes=['gd_kTm'])
                    S.op('dve', lambda e: e.memset(qTm[:], 0.0), reads=['gd_qTm'], writes=['gd_qTm'])
                    for s_ in range(NS):
                        S.op('dve', lambda e, h=h, s_=s_: e.tensor_copy(out=kTm[:, s_, s_ * 8:(s_ + 1) * 8], in_=kT[:, h, s_ * 8:(s_ + 1) * 8]), reads=['gd_kT', 'gd_kTm'], writes=['gd_kTm'])
                        S.op('dve', lambda e, h=h, s_=s_: e.tensor_copy(out=qTm[:, s_, s_ * 8:(s_ + 1) * 8], in_=qT[:, h, s_ * 8:(s_ + 1) * 8]), reads=['gd_qT', 'gd_qTm'], writes=['gd_qTm'])
                    for s_ in range(NS):
                        S.op('pe', lambda e, k=k, half=half, s_=s_: e.matmul(ps[2 + half][:, k * 128:(k + 1) * 128], lhsT=kTm[:, s_, :], rhs=S0b[:, s_, :], start=(s_ == 0), stop=(s_ == NS - 1)),
                             reads=['gd_kTm', 'gd_S0b'], writes=['ps%d' % (2 + half)], inc=(s_ == NS - 1))
                    for s_ in range(NS):
                        S.op('pe', lambda e, k=k, half=half, s_=s_: e.matmul(ps[4 + half][:, k * 128:(k + 1) * 128], lhsT=qTm[:, s_, :], rhs=S0b[:, s_, :], start=(s_ == 0), stop=(s_ == NS - 1)),
                             reads=['gd_qTm', 'gd_S0b'], writes=['ps%d' % (4 + half)], inc=(s_ == NS - 1))
            for half in range(2):
                S.op('dve', lambda e, half=half: e.tensor_tensor(out=tmp4[:, half * 4:half * 4 + 4, :], in0=ps[2 + half][:].rearrange("p (a b) -> p a b", b=N),
                                                                 in1=bE[:, half * 4:half * 4 + 4].unsqueeze(2).broadcast_to([128, 4, N]), op=ALU.mult),
                     reads=['ps%d' % (2 + half), 'gd_bE', 'gd_tmp4'], writes=['gd_tmp4'])
                S.op('dve', lambda e, half=half: e.tensor_tensor(out=oo[:, half * 4:half * 4 + 4, :], in0=ps[4 + half][:].rearrange("p (a b) -> p a b", b=N),
                                                                 in1=Et[:, half * 4:half * 4 + 4].unsqueeze(2).broadcast_to([128, 4, N]), op=ALU.mult),
                     reads=['ps%d' % (4 + half), 'gd_Et', 'gd_oo'], writes=['gd_oo'])
            S.op('dve', lambda e: e.tensor_tensor(out=rr[:], in0=bV[:], in1=tmp4[:], op=ALU.subtract), reads=['gd_bV', 'gd_tmp4'] + xck, writes=['gd_rr'])
            for half in range(2):
                for k in range(4):
                    h = half * 4 + k
                    S.op('pe', lambda e, k=k, h=h, half=half: e.matmul(ps[2 + half][:, k * 128:(k + 1) * 128], lhsT=Rm[:, h, :], rhs=rr[:, h, :], start=True, stop=True),
                         reads=['gd_Rm', 'gd_rr'], writes=['ps%d' % (2 + half)], inc=(k == 3))
                S.op('act', lambda e, half=half: e.activation(out=vnb[:, half * 4:half * 4 + 4, :].rearrange("p a b -> p (a b)"), in_=ps[2 + half][:], func=AF.Copy),
                     reads=['ps%d' % (2 + half)], writes=['gd_vnb'])
            for half in range(2):
                for k in range(4):
                    h = half * 4 + k
                    S.op('pe', lambda e, k=k, h=h, half=half: e.matmul(ps[4 + half][:, k * 128:(k + 1) * 128], lhsT=attnT[:, h, :], rhs=vnb[:, h, :], start=True, stop=True),
                         reads=['gd_attnT', 'gd_vnb'], writes=['ps%d' % (4 + half)], inc=(k == 3))
                S.op('dve', lambda e, half=half: e.tensor_tensor(out=oo[:, half * 4:half * 4 + 4, :].rearrange("p a b -> p (a b)"), in0=ps[4 + half][:], in1=oo[:, half * 4:half * 4 + 4, :].rearrange("p a b -> p (a b)"), op=ALU.add),
                     reads=['ps%d' % (4 + half), 'gd_oo'], writes=['gd_oo'])
            if not samp:
                for half in range(2):
                    for k in range(4):
                        h = half * 4 + k
                        S.op('pe', lambda e, k=k, h=h, half=half: e.matmul(ps[6 + half][:, k * 128:(k + 1) * 128], lhsT=kdec[:, h, :], rhs=vnb[:, h, :], start=True, stop=True),
                             reads=['gd_kdec', 'gd_vnb'], writes=['ps%d' % (6 + half)], inc=(k == 3))
                    S.op('pool', lambda e, half=half: e.tensor_tensor(out=Sst[:, half * 4:half * 4 + 4, :], in0=Sst[:, half * 4:half * 4 + 4, :],
                                                                     in1=cdbc[:, half * 4:half * 4 + 4].unsqueeze(2).broadcast_to([128, 4, N]), op=ALU.mult),
                         reads=['gd_S', 'gd_cdbc', 'gd_Sbf'], writes=['gd_S'])
                    S.op('dve', lambda e, half=half: e.tensor_tensor(out=Sst[:, half * 4:half * 4 + 4, :].rearrange("p a b -> p (a b)"), in0=ps[6 + half][:], in1=Sst[:, half * 4:half * 4 + 4, :].rearrange("p a b -> p (a b)"), op=ALU.add),
                         reads=['ps%d' % (6 + half), 'gd_S'], writes=['gd_S'])
                S.op('act', lambda e: e.activation(out=fl(Sbf), in_=fl(Sst), func=AF.Copy), reads=['gd_S'], writes=['gd_Sbf'])
                if last_p:
                    S.dma('sp', O['gdn_state_p'][jl].rearrange("h d e -> d h e"), Sst[:], reads=['gd_S'])
            else:
                for h in range(8):
                    S.dma('sp', S0h[:], I['gd_s0'][jl, :, h].rearrange("s d e -> d s e"), writes=['gd_S0h'])
                    for s_ in range(NS):
                        k2 = s_ % 2
                        S.op('dve', lambda e, h=h, s_=s_, k2=k2: e.tensor_scalar(out=kdm[:, k2, :], in0=kdec[:, h, :], scalar1=C['SeqSel'][:, s_:s_ + 1], scalar2=None, op0=ALU.mult),
                             reads=['gd_kdec', 'cst'], writes=['gd_kdm%d' % k2])
                        S.op('pe', lambda e, h=h, k2=k2: e.matmul(ps[6 + k2][:, 0:N], lhsT=kdm[:, k2, :], rhs=vnb[:, h, :], start=True, stop=True),
                             reads=['gd_kdm%d' % k2, 'gd_vnb'], writes=['ps%d' % (6 + k2)])
                        S.op('dve', lambda e, h=h, s_=s_, k2=k2: e.scalar_tensor_tensor(out=S0h[:, s_, :], in0=S0h[:, s_, :], scalar=cdall[:, s_, h:h + 1], in1=ps[6 + k2][:, 0:N],
                                                                                 op0=ALU.mult, op1=ALU.add),
                             reads=['gd_S0h', 'gd_cdall', 'ps%d' % (6 + k2)], writes=['gd_S0h'])
                    S.dma('sp', O['gdn_state_s'][jl, :, h].rearrange("s d e -> d s e"), S0h[:], reads=['gd_S0h'])
            S.op('act', lambda e: e.activation(out=fl(tmp4), in_=fl(oo), func=AF.Square), reads=['gd_oo', 'gd_tmp4', 'gd_rr'], writes=['gd_tmp4'])
            S.op('dve', lambda e: e.tensor_reduce(out=ss8[:], in_=tmp4[:], axis=AX.X, op=ALU.add), reads=['gd_tmp4'], writes=['gd_ss8'])
            S.op('act', lambda e: e.activation(out=ss8[:], in_=ss8[:], func=AF.Sqrt, scale=1.0 / 128, bias=self.epsc[:]), reads=['gd_ss8', 'epsc'], writes=['gd_ss8'])
            S.op('dve', lambda e: e.reciprocal(out=ss8[:], in_=ss8[:]), reads=['gd_ss8'], writes=['gd_ss8'])
            S.op('dve', lambda e: e.tensor_tensor(out=oo[:], in0=oo[:], in1=ss8[:].unsqueeze(2).broadcast_to([128, 8, N]), op=ALU.mult), reads=['gd_oo', 'gd_ss8'], writes=['gd_oo'])
            S.op('dve', lambda e: e.tensor_tensor(out=oo[:], in0=oo[:], in1=gt[:], op=ALU.mult), reads=['gd_oo', 'gd_gt'], writes=['gd_oo'])
            for half in range(2):
                b = ps[2 + half]
                for k in range(4):
                    h = half * 4 + k
                    S.op('pe', lambda e, k=k, h=h, b=b: e.transpose(b[:, k * 128:(k + 1) * 128], oo[:, h, :], C['ident']),
                         reads=['gd_oo', 'cst'], writes=['ps%d' % (2 + half)], inc=(k == 3))
                S.op('act', lambda e, half=half, b=b: e.activation(out=onT[:, half * 4:half * 4 + 4, :].rearrange("p a b -> p (a b)"), in_=b[:], func=AF.Copy, scale=nwcol[:]),
                     reads=['ps%d' % (2 + half), 'gd_nwc'], writes=['gd_onT'])
            self.outproj_residual(l, 3, t0, N, wnext, lambda k: onT[:, k, :], 8, 'gd_onT', ysb, hn, rstd, 'gd_')

        for bi, (t0, n) in enumerate(BLOCKS128):
            do_block(bi, t0)

    def _sqv(self, tmp4):
        return tmp4[:].rearrange("p a b -> p (a b)")[:, 0:256].bitcast(BF16)

    def _off(self, v):
        return (v.offset - self.scratch[:, 0:1].offset) * (2 if v.dtype == BF16 else 4)

    def ffn(self, l, f):
        S = self.S
        X = self.X
        ipre, ipost = (0, 1) if f == 0 else (4, 5)
        o = 0
        xn = self.scr("xn", [128, DC, PMAX], BF16, o); o += DC * PMAX * 2
        a = self.scr("a", [128, FC, PMAX], BF16, o); o += FC * PMAX * 2
        ysb = self.scr("ysb", [128, DC, PMAX], F32, o); o += DC * PMAX * 4
        sg = [self.scr("sg%d" % i, [128, 512], F32, o + i * 2048) for i in range(2)]; o += 4096
        sq = [self.scr("sq%d" % i, [128, 512], BF16, o + i * 1024) for i in range(2)]; o += 2048
        rstd = self.scr("rstd", [128, PMAX], F32, o); o += PMAX * 4
        assert o <= self.scr_size
        ps = self.ps
        cnt = 0
        import os
        PH = int(os.environ.get('FFN_PH', '4'))
        srcs = []
        for _ in FFN_PASSES:
            srcs += [(self.I['ffn_w_in'][l, f, k], 2048, ('fi', l, f, k)) for k in range(FC)]
            srcs += [(self.I['ffn_w_out'][l, f, d, h], 1408, ('fo', l, f, d, h)) for d in range(DC) for h in range(2)]
        wnext = self.wstream(srcs)
        for pi, subt in enumerate(FFN_PASSES):
            if pi == 0:
                S.barrier()
            xk = 'Xf%d' % pi
            p0 = subt[0][0]
            for si, (t0, n) in enumerate(subt):
                lo = t0 - p0
                st_ps = ps[6 + si % 2]
                kst = 'ps%d' % (6 + si % 2)
                S.op('act', lambda e, lo=lo, t0=t0, n=n: e.activation(out=xn[:, :, lo:lo + n], in_=X[:, :, t0:t0 + n], func=AF.Square),
                     reads=['X', xk], writes=['xn%d' % si])
                for c in range(DC):
                    S.op('pe', lambda e, c=c, lo=lo, n=n, st_ps=st_ps: e.matmul(st_ps[:, 0:n], lhsT=self.ones_bf[:], rhs=xn[:, c, lo:lo + n], start=(c == 0), stop=(c == DC - 1)),
                         reads=['xn%d' % si, 'ones_bf'], writes=[kst], inc=(c == DC - 1))
                S.op('act', lambda e, lo=lo, n=n, st_ps=st_ps: e.activation(out=rstd[:, lo:lo + n], in_=st_ps[:, 0:n], func=AF.Ln, scale=1.0 / D, bias=self.epsc[:]),
                     reads=[kst, 'epsc'], writes=['rstd%d' % si])
                S.op('act', lambda e, lo=lo, n=n: e.activation(out=rstd[:, lo:lo + n], in_=rstd[:, lo:lo + n], func=AF.Exp, scale=-0.5),
                     reads=['rstd%d' % si], writes=['rstd%d' % si])
                for c in range(DC):
                    S.op('dve', lambda e, c=c, lo=lo, n=n, t0=t0: e.scalar_tensor_tensor(
                        out=xn[:, c, lo:lo + n], in0=X[:, c, t0:t0 + n], scalar=self.nwcol(l, ipre, c), in1=rstd[:, lo:lo + n],
                        op0=ALU.mult, op1=ALU.mult), reads=['X', xk, 'nw', 'rstd%d' % si], writes=['xn%d' % si])
            if PH < 2:
                continue
            for k in range(FC):
                wb, wkey = wnext()
                wv = wb[:, 0:2048].rearrange("p (c n) -> p c n", n=256)
                for si, (t0, n) in enumerate(subt):
                    lo = t0 - p0
                    for jj in range(1):
                        j = k
                        gi = cnt % 2
                        cnt += 1
                        gps, ups = ps[gi], ps[2 + gi]
                        for c in range(DC):
                            S.op('pe', lambda e, c=c, jj=jj, lo=lo, n=n, gps=gps, wv=wv: e.matmul(
                                gps[:, 0:n], lhsT=wv[:, c, 0:128], rhs=xn[:, c, lo:lo + n], start=(c == 0), stop=(c == DC - 1)),
                                reads=[wkey, 'xn%d' % si], writes=['ps%d' % gi], inc=(c == DC - 1))
                        for c in range(DC):
                            S.op('pe', lambda e, c=c, jj=jj, lo=lo, n=n, ups=ups, wv=wv: e.matmul(
                                ups[:, 0:n], lhsT=wv[:, c, 128:256], rhs=xn[:, c, lo:lo + n], start=(c == 0), stop=(c == DC - 1)),
                                reads=[wkey, 'xn%d' % si], writes=['ps%d' % (2 + gi)], inc=(c == DC - 1))
                        S.op('act', lambda e, gi=gi, n=n, gps=gps: e.activation(out=sg[gi][:, 0:n], in_=gps[:, 0:n], func=AF.Silu),
                             reads=['ps%d' % gi], writes=['sg%d' % gi])
                        S.op('dve', lambda e, gi=gi, n=n, ups=ups, j=j, lo=lo: e.tensor_tensor(
                            out=a[:, j, lo:lo + n], in0=sg[gi][:, 0:n], in1=ups[:, 0:n], op=ALU.mult),
                            reads=['sg%d' % gi, 'ps%d' % (2 + gi)], writes=['a%d' % si])
            if PH < 3:
                continue
            for d in range(DC):
                wb0, wkey0 = wnext()
                wb1, wkey1 = wnext()
                for si, (t0, n) in enumerate(subt):
                    lo = t0 - p0
                    yi = cnt % 2
                    cnt += 1
                    yps = ps[4 + yi]
                    st_ps = ps[si] if si < 2 else ps[6]
                    kst = 'ps%d' % (si if si < 2 else 6)
                    for c in range(FC):
                        wb, wkey, cc = (wb0, wkey0, c) if c < 11 else (wb1, wkey1, c - 11)
                        S.op('pe', lambda e, c=c, cc=cc, lo=lo, n=n, yps=yps, wb=wb: e.matmul(
                            yps[:, 0:n], lhsT=wb[:, cc * 128:(cc + 1) * 128], rhs=a[:, c, lo:lo + n], start=(c == 0), stop=(c == FC - 1)),
                            reads=[wkey, 'a%d' % si], writes=['ps%d' % (4 + yi)], inc=(c == FC - 1))
                    S.op('dve', lambda e, d=d, lo=lo, n=n, yps=yps: e.tensor_copy(out=ysb[:, d, lo:lo + n], in_=yps[:, 0:n]),
                         reads=['ps%d' % (4 + yi)], writes=['ysb%d' % si])
                    S.op('act', lambda e, d=d, lo=lo, n=n: e.activation(out=xn[:, d, lo:lo + n], in_=ysb[:, d, lo:lo + n], func=AF.Square),
                         reads=['ysb%d' % si], writes=['xn%d' % si])
            if PH < 4:
                continue
            for si, (t0, n) in enumerate(subt):
                lo = t0 - p0
                st_ps = ps[6 + si % 2]
                kst = 'ps%d' % (6 + si % 2)
                for d in range(DC):
                    S.op('pe', lambda e, d=d, lo=lo, n=n, st_ps=st_ps: e.matmul(st_ps[:, 0:n], lhsT=self.ones_bf[:], rhs=xn[:, d, lo:lo + n], start=(d == 0), stop=(d == DC - 1)),
                         reads=['xn%d' % si, 'ones_bf'], writes=[kst], inc=(d == DC - 1))
                S.op('act', lambda e, lo=lo, n=n, st_ps=st_ps: e.activation(out=rstd[:, lo:lo + n], in_=st_ps[:, 0:n], func=AF.Ln, scale=1.0 / D, bias=self.epsc[:]),
                     reads=[kst, 'epsc'], writes=['rstd%d' % si])
                S.op('act', lambda e, lo=lo, n=n: e.activation(out=rstd[:, lo:lo + n], in_=rstd[:, lo:lo + n], func=AF.Exp, scale=-0.5),
                     reads=['rstd%d' % si], writes=['rstd%d' % si])
                for d in range(DC):
                    S.op('dve', lambda e, d=d, lo=lo, n=n: e.tensor_tensor(out=ysb[:, d, lo:lo + n], in0=ysb[:, d, lo:lo + n], in1=rstd[:, lo:lo + n], op=ALU.mult),
                         reads=['ysb%d' % si, 'rstd%d' % si], writes=['ysb%d' % si])
                    S.op('dve', lambda e, d=d, lo=lo, n=n, t0=t0: e.scalar_tensor_tensor(
                        out=X[:, d, t0:t0 + n], in0=ysb[:, d, lo:lo + n], scalar=self.nwcol(l, ipost, d, half=True), in1=X[:, d, t0:t0 + n],
                        op0=ALU.mult, op1=ALU.add), reads=['ysb%d' % si, 'nwh', 'X', xk], writes=[xk])


_CACHE = {}


def prep_shared(inp, nl=DEPTH):
    sh = {}
    f32 = lambda k: np.asarray(inp[k], np.float32)
    rep = lambda v: np.ascontiguousarray(np.broadcast_to(v, (128,) + v.shape))
    w = f32('cmlp_w_in')[0].reshape(DC, 128, 2, 8, 256)
    sh['cm_wu'] = np.ascontiguousarray(w[:, :, 0].transpose(2, 1, 0, 3)).reshape(8, 128, 2048)
    sh['cm_wv'] = np.ascontiguousarray(w[:, :, 1].transpose(2, 1, 0, 3)).reshape(8, 128, 2048)
    w = f32('cmlp_w_out')[0].reshape(16, 128, DC, 128)
    sh['cm_wo'] = np.ascontiguousarray(w.transpose(2, 1, 0, 3)).reshape(8, 128, 2048)
    b = f32('cmlp_b_in')[0]
    sh['cm_bv'] = rep(b[2048:])
    sh['cm_lnw'] = rep(f32('cmlp_ln_w')[0])
    sh['cm_lnb'] = rep(f32('cmlp_ln_b')[0])
    sh['cm_cols'] = np.ascontiguousarray(np.concatenate([b[:2048].reshape(16, 128).T, f32('cmlp_ln_w')[0].reshape(16, 128).T,
                                                         f32('cmlp_ln_b')[0].reshape(16, 128).T], axis=1))
    w = f32('gdn_w_in')
    sh['gd_wx'] = np.ascontiguousarray(w[:, :, :3072].reshape(2, DC, 128, 12, 256).transpose(0, 3, 2, 1, 4)).reshape(2, 12, 128, 2048)
    sh['gd_wg'] = np.ascontiguousarray(w[:, :, 3072:4096].reshape(2, DC, 128, 4, 256).transpose(0, 3, 2, 1, 4)).reshape(2, 4, 128, 2048)
    sh['gd_wab'] = np.ascontiguousarray(w[:, :, 4096:].reshape(2, DC, 128, 16).transpose(0, 2, 1, 3)).reshape(2, 128, 128)
    w = f32('gdn_w_out').reshape(2, 8, 128, DC, 128)
    sh['gd_wo'] = np.ascontiguousarray(w.transpose(0, 3, 2, 1, 4)).reshape(2, 8, 128, 1024)
    sh['gd_cols'] = np.ascontiguousarray(f32('gdn_conv_w').reshape(2, 4, 24, 128).transpose(0, 3, 2, 1)).reshape(2, 128, 96)
    sh['gd_rows'] = np.ascontiguousarray(np.broadcast_to(np.concatenate([f32('gdn_a_log'), f32('gdn_dt_bias')], axis=1)[:, None, :], (2, 128, 16)))
    sh['gd_nwc'] = np.ascontiguousarray(f32('gdn_norm_w').reshape(2, 128, 1))
    w = f32('ssd_w_in')[0]
    sh['ss_wz'] = np.ascontiguousarray(w[:, :2048].reshape(DC, 128, 8, 256).transpose(2, 1, 0, 3)).reshape(8, 128, 2048)
    sh['ss_wx'] = np.ascontiguousarray(w[:, 2048:5120].reshape(DC, 128, 12, 256).transpose(2, 1, 0, 3)).reshape(12, 128, 2048)
    sh['ss_wdt'] = np.ascontiguousarray(w[:, 5120:].reshape(DC, 128, 32).transpose(1, 0, 2)).reshape(128, 256)
    w = f32('ssd_w_out')[0].reshape(16, 128, DC, 128)
    sh['ss_wo'] = np.ascontiguousarray(w.transpose(2, 1, 0, 3)).reshape(8, 128, 2048)
    cw = f32('ssd_conv_w')[0].reshape(4, 24, 128)
    cb = f32('ssd_conv_b')[0].reshape(1, 24, 128)
    sh['ss_cols'] = np.ascontiguousarray(np.concatenate([cw, cb], axis=0).transpose(2, 1, 0)).reshape(128, 120)
    sh['ss_nwc'] = np.ascontiguousarray(f32('ssd_norm_w')[0].reshape(16, 128).T)
    sh['ss_rows'] = rep(np.concatenate([f32('ssd_a_log')[0], f32('ssd_dt_bias')[0], f32('ssd_d')[0]]))
    ws = f32('cmlp_w_s')[0]
    sh['cm_wsT'] = np.ascontiguousarray(ws.transpose(2, 0, 1))
    blk = np.zeros((128, 8, 128), np.float32)
    for q in range(NS):
        blk[q * 8:(q + 1) * 8, :, q * 8:(q + 1) * 8] = ws[:, :8, :8].transpose(2, 0, 1)
    sh['cm_wsblk'] = blk
    bs = f32('cmlp_b_s')[0]
    sh['cm_bs'] = rep(bs)
    sh['cm_bss'] = rep(np.ascontiguousarray(np.tile(bs[:, :8], (1, NS))))
    sh['consts'] = CONST_ARR
    nw = np.asarray(inp['norm_w'], np.float32).reshape(DEPTH, 6, DC, 128)
    sh['nw'] = np.ascontiguousarray(nw.transpose(3, 0, 1, 2).reshape(128, DEPTH * 6 * DC))
    w = np.asarray(inp['ffn_w_in'], np.float32).reshape(DEPTH, 2, DC, 128, 2, FC, 128)
    sh['ffn_w_in'] = np.ascontiguousarray(w[:nl].transpose(0, 1, 5, 3, 2, 4, 6)).reshape(nl, 2, FC, 128, 2048)
    w = np.asarray(inp['ffn_w_out'], np.float32).reshape(DEPTH, 2, 2, 11, 128, DC, 128)
    sh['ffn_w_out'] = np.ascontiguousarray(w[:nl].transpose(0, 1, 5, 2, 4, 3, 6)).reshape(nl, 2, DC, 2, 128, 11 * 128)
    return sh


def prep_core(inp, c):
    xp = np.asarray(inp['x_prompt'], np.float32)[c]
    xs = np.asarray(inp['x_sample'], np.float32)[c * NS:(c + 1) * NS].reshape(NS * LS, D)
    m = {}
    m['xT'] = np.ascontiguousarray(np.concatenate([xp, xs], axis=0).T)
    m['gd_s0'] = np.ascontiguousarray(np.asarray(inp['state_gdn'], np.float32)[:, c * NS:(c + 1) * NS])
    cs = np.asarray(inp['state_gdn_conv'], np.float32)[:, c * NS:(c + 1) * NS].reshape(2, NS, 3, 24, 128)
    m['gd_cs'] = np.ascontiguousarray(cs.transpose(0, 4, 3, 1, 2)).reshape(2, 128, 24 * 48)
    st = np.asarray(inp['state_ssd'], np.float32)[0, c * NS:(c + 1) * NS].reshape(NS, 2048, 128)
    m['ss_s0T'] = np.ascontiguousarray(st.transpose(0, 2, 1))
    cs = np.asarray(inp['state_ssd_conv'], np.float32)[0, c * NS:(c + 1) * NS].reshape(NS, 3, 24, 128)
    m['ss_cs'] = np.ascontiguousarray(cs.transpose(3, 2, 0, 1)).reshape(128, 24 * 48)
    return m


def run(inp, stages=999, plan=None):
    key = str(plan)
    if key not in _CACHE:
        b = Builder(stages, plan=plan)
        _CACHE[key] = (b.build(), b.nl)
    nc, nl = _CACHE[key]
    sh = prep_shared(inp, nl)
    in_maps = []
    for c in range(NCORES):
        m = dict(sh)
        m.update(prep_core(inp, c))
        in_maps.append(m)
    res = run_bass_kernel_spmd(nc, in_maps, core_ids=list(range(NCORES)))
    return res.results


def assemble(results):
    yT = [r['yT'] for r in results]
    y_prompt = np.stack([y[:, :SEQ].T for y in yT]).astype(np.float32)
    y_sample = np.concatenate([y[:, SEQ:].T.reshape(NS, LS, D) for y in yT]).astype(np.float32)
    return y_prompt, y_sample


def kernel(**inputs):
    results = run(inputs)
    y_prompt, y_sample = assemble(results)
    g = [gdn_outputs(r) for r in results]
    d = [ssd_outputs(r) for r in results]
    f = np.float32
    gdn_state_p = np.stack([x[0] for x in g], axis=1).astype(f)
    gdn_conv_p = np.stack([x[2] for x in g], axis=1).astype(f)
    ssd_state_p = np.stack([x[0] for x in d], axis=0)[None].astype(f)
    ssd_conv_p = np.stack([x[2] for x in d], axis=0)[None].astype(f)
    gdn_state_s = np.concatenate([x[1] for x in g], axis=1).astype(f)
    gdn_conv_s = np.concatenate([x[3] for x in g], axis=1).astype(f)
    ssd_state_s = np.concatenate([x[1] for x in d], axis=0)[None].astype(f)
    ssd_conv_s = np.concatenate([x[3] for x in d], axis=0)[None].astype(f)
    cmlp_v_s = np.concatenate([r['cmlp_v'].reshape(NS, LS, 2048) for r in results], axis=0)[None].astype(f)
    return (y_prompt, y_sample, gdn_state_p, gdn_conv_p, ssd_state_p, ssd_conv_p,
            gdn_state_s, gdn_conv_s, ssd_state_s, ssd_conv_s, cmlp_v_s)


def ssd_outputs(r):
    sp = r['ssd_pT'].T.reshape(32, 64, 128)
    ssn = r['ssd_sT'].transpose(0, 2, 1).reshape(NS, 32, 64, 128)
    cp = r['ssd_conv_p'].reshape(128, 24, 3).transpose(2, 1, 0).reshape(3, 3072)
    csn = r['ssd_conv_s'].reshape(128, 24, NS, 3).transpose(2, 3, 1, 0).reshape(NS, 3, 3072)
    return sp, ssn, cp, csn


def gdn_outputs(r):
    cp = r['gdn_conv_p'].reshape(2, 128, 24, 3).transpose(0, 3, 2, 1).reshape(2, 3, 3072)
    csn = r['gdn_conv_s'].reshape(2, 128, 24, NS, 3).transpose(0, 3, 4, 2, 1).reshape(2, NS, 3, 3072)
    return r['gdn_state_p'], r['gdn_state_s'], cp, csn


def check_extra(res, core, extra, rv):
    for i, (nsp, nss) in extra.items():
        if i % 3 == 0:
            sp, ssn, cp, csn = gdn_outputs(res[core])
            j = i // 3
            print("  layer", i, "gdn state_p", rv(sp[j], np.asarray(nsp[1])[0]), "state_s", rv(ssn[j], np.asarray(nss[1])),
                  "conv_p", rv(cp[j], np.asarray(nsp[0])[0]), "conv_s", rv(csn[j], np.asarray(nss[0])))
        if i % 3 == 2:
            sp, ssn, cp, csn = ssd_outputs(res[core])
            print("  layer", i, "ssd state_p", rv(sp, np.asarray(nsp[1])[0]), "state_s", rv(ssn, np.asarray(nss[1])),
                  "conv_p", rv(cp, np.asarray(nsp[0])[0]), "conv_s", rv(csn, np.asarray(nss[0])))
        if i % 3 == 1:
            ref = np.asarray(nss[0]).reshape(NS * LS, 2048)
            print("  layer", i, "cmlp_v resvar", rv(res[core]['cmlp_v'], ref))
```

```python
import numpy as np
from contextlib import ExitStack
import concourse.bass as bass
import concourse.mybir as mybir
from concourse.alu_op_type import AluOpType as ALU
from concourse.bass_utils import run_bass_kernel_spmd

F32 = mybir.dt.float32
BF16 = mybir.dt.bfloat16
U8 = mybir.dt.uint8
AF = mybir.ActivationFunctionType
AX = mybir.AxisListType

NCORES = 8
D = 1024
DC = 8
SEQ = 2048
NS = 16
LS = 8
T = SEQ + NS * LS
DEPTH = 4
DFF = 2816
FC = 22
EPS = 1e-6
WBUF = 2048
NRING = 5
NSTG = 3
WC_UNITS = 440
LOOKAHEAD = 3
CAST_PATTERN = ['act', 'dve']

FFN_PASSES = [[(0, 512), (512, 256)], [(768, 512), (1280, 256)], [(1536, 512), (2048, 128)]]
PMAX = 768
import os
BLOCKS128 = [(t, 128) for t in range(0, T, 128)][int(os.environ.get('B128_0', '0')):int(os.environ.get('B128_1', '17'))]
BLOCKS = [(0, 512), (512, 512), (1024, 512), (1536, 512), (2048, 128)][int(os.environ.get("BLK0", "0")):int(os.environ.get("NBLK", "5"))]


class Sched:
    NDMA = 28

    def __init__(self, nc, stack):
        self.nc = nc
        self.names = ['pe', 'act', 'dve', 'pool', 'sp']
        self.prog = {e: [] for e in self.names}
        self.sem = {e: stack.enter_context(nc.semaphore('s_' + e)) for e in self.names}
        self.cnt = {e: 0 for e in self.names}
        self.pend = {e: False for e in self.names}
        self.seen = {e: {} for e in self.names}
        self.dsem = [stack.enter_context(nc.semaphore('d%d' % i)) for i in range(self.NDMA)]
        self.dcnt = 0
        self.lastw = {}
        self.reads = {}
        self.sp_events = []
        self.all_dma = {}
        self.last_barrier = []
        self.sw_last = {}

    def _deps(self, e, reads, writes):
        deps = {}

        def add(ev):
            s, v = ev
            k = id(s)
            if k not in deps or deps[k][1] < v:
                deps[k] = (s, v)
        for k in reads:
            ev = self.lastw.get(k)
            if ev is not None:
                add(ev)
        for k in writes:
            ev = self.lastw.get(k)
            if ev is not None:
                add(ev)
            for ev in self.reads.get(k, {}).values():
                add(ev)
        return self._filter(e, deps.values())

    def _filter(self, e, evs):
        out = []
        for s, v in evs:
            if s is self.sem[e] and (v > self.cnt[e] or e == 'pe'):
                continue
            if self.seen[e].get(id(s), 0) < v:
                self.seen[e][id(s)] = v
                out.append((s, v))
        return out

    def _commit(self, ev, reads, writes):
        for k in writes:
            self.lastw[k] = ev
            self.reads[k] = {}
        s, v = ev
        for k in reads:
            d = self.reads.setdefault(k, {})
            old = d.get(id(s))
            if old is None or old[1] < v:
                d[id(s)] = ev

    def op(self, e, fn, reads=(), writes=(), inc=True):
        waits = self._deps(e, reads, writes)
        sem = self.sem[e]
        ev = (sem, self.cnt[e] + 1)
        if inc:
            self.cnt[e] += 1
            self.pend[e] = False
        else:
            self.pend[e] = True

        def emit(eng, fn=fn, waits=waits, inc=inc, sem=sem):
            for s, v in waits:
                eng.wait_ge(s, v)
            ins = fn(eng)
            if inc:
                ins.then_inc(sem, 1)
        self.prog[e].append(emit)
        self._commit(ev, reads, writes)

    def dma(self, e, out, in_, reads=(), writes=(), fence=False, **kw):
        i = self.dcnt % self.NDMA
        gen = self.dcnt // self.NDMA
        self.dcnt += 1
        ds = self.dsem[i]
        waits = self._deps(e, reads, writes)
        if fence:
            waits += self._filter(e, self.last_barrier)
        if gen > 0:
            waits += self._filter(e, [(ds, 16 * gen)])
        ev = (ds, 16 * (gen + 1))
        self.all_dma[i] = ev

        def emit(eng, waits=waits, ds=ds):
            for s, v in waits:
                eng.wait_ge(s, v)
            eng.dma_start(out=out, in_=in_, **kw).then_inc(ds, 16)
        self.prog[e].append(emit)
        self._commit(ev, reads, writes)
        if e == 'sp':
            self.sp_events.append(ev)
        return ev

    def swdma(self, sem, ev, out, in_, reads=(), writes=(), **kw):
        e = 'pool'
        waits = self._deps(e, reads, writes)

        def emit(eng, waits=waits, sem=sem):
            for s, v in waits:
                eng.wait_ge(s, v)
            eng.sem_clear(sem)
            eng.dma_start(out=out, in_=in_, **kw).then_inc(sem, 16)
        self.prog[e].append(emit)
        self._commit(ev, reads, writes)

    def publish(self, sem, ready_sem):
        def emit(eng, sem=sem, ready_sem=ready_sem):
            eng.wait_ge(sem, 16)
            eng.sem_inc(ready_sem, 1)
        self.prog['pool'].append(emit)

    def barrier(self, engines=('pe', 'act', 'dve', 'pool')):
        evs = [(self.sem[w], self.cnt[w]) for w in engines if self.cnt[w] > 0] + list(self.sp_events)
        self.last_barrier = [(self.sem[w], self.cnt[w]) for w in engines if self.cnt[w] > 0]
        self.sp_events = []
        for e in engines:
            waits = self._filter(e, evs)

            def emit(eng, waits=waits):
                for s, v in waits:
                    eng.wait_ge(s, v)
            self.prog[e].append(emit)

    def final_wait(self, e='sp'):
        evs = list(self.all_dma.values()) + [(self.sem[w], self.cnt[w]) for w in self.names if self.cnt[w] > 0 and w != e]
        waits = self._filter(e, evs)

        def emit(eng, waits=waits):
            for s, v in waits:
                eng.wait_ge(s, v)
        self.prog[e].append(emit)

    def finish(self, block, nc):
        for e in self.names:
            assert not self.pend[e], e
        decos = {'pe': block.tensor, 'act': block.scalar, 'dve': block.vector, 'pool': block.gpsimd, 'sp': block.sync}
        for name in self.names:
            prog = self.prog[name]

            def body(eng, prog=prog):
                for f in prog:
                    f(eng)
            decos[name](body)


def make_consts():
    i = np.arange(128)
    c = {}
    c['ident'] = np.eye(128, dtype=np.float32)
    c['ones'] = np.ones((128, 128), np.float32)
    c['A'] = (i[:, None] > i[None, :]).astype(np.float32)
    c['B'] = (i[:, None] <= i[None, :]).astype(np.float32)
    c['Ms'] = (i[:, None] < i[None, :]).astype(np.float32)
    blk = (i[:, None] // 8 == i[None, :] // 8)
    c['Mblk'] = (blk & ((i[:, None] % 8) <= (i[None, :] % 8))).astype(np.float32)
    rep = np.zeros((128, 128), np.float32)
    for r in range(8):
        rep[r, (i % 8) == r] = 1.0
    c['Rep8'] = rep
    sel = np.zeros((128, 128), np.float32)
    sel[127, :] = 1.0
    c['SelLast'] = sel
    c['Ablk'] = (blk & ((i[:, None] % 8) > (i[None, :] % 8))).astype(np.float32)
    c['SelLastS'] = (blk & ((i[:, None] % 8) == 7)).astype(np.float32)
    ss = np.zeros((128, 128), np.float32)
    for q in range(16):
        ss[q * 8:(q + 1) * 8, q] = 1.0
        ss[q * 8 + 7, 16 + q] = 1.0
    c['SeqSel'] = ss
    names = ['ident', 'ones', 'A', 'B', 'Ms', 'Mblk', 'Rep8', 'SelLast', 'Ablk', 'SelLastS', 'SeqSel']
    return names, np.concatenate([c[n] for n in names], axis=1)


CONST_NAMES, CONST_ARR = make_consts()


class Builder:
    def __init__(self, stages=999, dbg=False, plan=None):
        self.stages = stages
        if plan is None:
            plan = [(l, p) for l in range(DEPTH) for p in range(3)][:min(stages, 12)]
        self.plan = plan
        self.nl = DEPTH
        self.nc = bass.Bass("TRN2", target_bir_lowering=False)
        self.dbg = dbg

    def sb(self, name, shape, dt, off):
        return self.nc.alloc_sbuf_tensor_at(name, shape, dt, offset=self.arena0 + off)

    def din(self, name, shape, dt=F32):
        return self.nc.dram_tensor(name, list(shape), dt, kind="ExternalInput").ap()

    def dout(self, name, shape, dt=F32):
        return self.nc.dram_tensor(name, list(shape), dt, kind="ExternalOutput").ap()

    def wstream(self, sources):
        state = {'issued': 0, 'taken': 0, 'slots': []}

        def issue():
            k = state['issued']
            src, nelem, cid = sources[k]
            gi = self.wcnt
            self.wcnt += 1
            ri = gi % NRING
            ring = self.ring[ri]
            if cid in self.wc_idx:
                idx = self.wc_idx[cid]
                self.S.dma('sp', ring[:, 0:nelem], self.wcache[idx, :, 0:nelem], reads=['wc%d' % idx], writes=['ring%d' % ri])
            else:
                si = self.scnt % NSTG
                self.scnt += 1
                stg = self.stg[si]
                self.S.dma('sp', stg[:, 0:nelem], src, writes=['stg%d' % si])
                ce = CAST_PATTERN[self.scnt % len(CAST_PATTERN)]
                if ce == 'act':
                    self.S.op('act', lambda e: e.activation(out=ring[:, 0:nelem], in_=stg[:, 0:nelem], func=AF.Copy),
                              reads=['stg%d' % si], writes=['ring%d' % ri])
                else:
                    self.S.op(ce, lambda e: e.tensor_copy(out=ring[:, 0:nelem], in_=stg[:, 0:nelem]),
                              reads=['stg%d' % si], writes=['ring%d' % ri])
                if cid is not None and len(self.wc_idx) < WC_UNITS:
                    idx = len(self.wc_idx)
                    self.wc_idx[cid] = idx
                    self.S.dma('sp', self.wcache[idx, :, 0:nelem], ring[:, 0:nelem], reads=['ring%d' % ri], writes=['wc%d' % idx])
            state['slots'].append((ring, 'ring%d' % ri))
            state['issued'] += 1

        def nxt():
            while state['issued'] < len(sources) and state['issued'] <= state['taken'] + LOOKAHEAD:
                issue()
            r = state['slots'][state['taken']]
            state['taken'] += 1
            return r
        return nxt

    def flush_w(self, upto=None):
        if upto is None:
            upto = self.wcnt - 1
        while self.wpub <= upto:
            self.S.publish(self.rsem[self.wpub % NRING], self.wready)
            self.wpub += 1

    def build(self):
        nc = self.nc
        with ExitStack() as st:
            self.st = st
            S = self.S = Sched(nc, st)
            self.wcnt = 0
            self.scnt = 0
            self.wc_idx = {}
            self.wcache = nc.dram_tensor('wcache', [WC_UNITS, 128, WBUF], BF16, kind='Internal').ap()
            self.wpub = 0
            self.wready = st.enter_context(nc.semaphore('wready'))
            self.rsem = [st.enter_context(nc.semaphore('r%d' % i)) for i in range(NRING)]
            I = self.I = {}
            I['xT'] = self.din('xT', [D, T])
            I['consts'] = self.din('consts', [128, CONST_ARR.shape[1]])
            I['nw'] = self.din('nw', [128, DEPTH * 6 * DC])
            I['ffn_w_in'] = self.din('ffn_w_in', [self.nl, 2, FC, 128, 2048])
            I['ffn_w_out'] = self.din('ffn_w_out', [self.nl, 2, 8, 2, 128, 11 * 128])
            I['cm_wv'] = self.din('cm_wv', [8, 128, 2048])
            I['cm_wu'] = self.din('cm_wu', [8, 128, 2048])
            I['cm_wo'] = self.din('cm_wo', [8, 128, 2048])
            I['cm_bv'] = self.din('cm_bv', [128, 2048])
            I['cm_lnw'] = self.din('cm_lnw', [128, 2048])
            I['cm_lnb'] = self.din('cm_lnb', [128, 2048])
            I['cm_cols'] = self.din('cm_cols', [128, 48])
            I['cm_wsT'] = self.din('cm_wsT', [128, 8, 128])
            I['cm_wsblk'] = self.din('cm_wsblk', [128, 8, 128])
            I['cm_bs'] = self.din('cm_bs', [128, 8, 128])
            I['cm_bss'] = self.din('cm_bss', [128, 8, 128])
            I['ss_wz'] = self.din('ss_wz', [8, 128, 2048])
            I['ss_wx'] = self.din('ss_wx', [12, 128, 2048])
            I['ss_wdt'] = self.din('ss_wdt', [128, 256])
            I['ss_wo'] = self.din('ss_wo', [8, 128, 2048])
            I['ss_cols'] = self.din('ss_cols', [128, 120])
            I['ss_nwc'] = self.din('ss_nwc', [128, 16])
            I['ss_rows'] = self.din('ss_rows', [128, 96])
            I['ss_s0T'] = self.din('ss_s0T', [NS, 128, 2048])
            I['ss_cs'] = self.din('ss_cs', [128, 24 * 48])
            I['gd_wx'] = self.din('gd_wx', [2, 12, 128, 2048])
            I['gd_wg'] = self.din('gd_wg', [2, 4, 128, 2048])
            I['gd_wab'] = self.din('gd_wab', [2, 128, 128])
            I['gd_wo'] = self.din('gd_wo', [2, 8, 128, 1024])
            I['gd_cols'] = self.din('gd_cols', [2, 128, 96])
            I['gd_rows'] = self.din('gd_rows', [2, 128, 16])
            I['gd_nwc'] = self.din('gd_nwc', [2, 128, 1])
            I['gd_s0'] = self.din('gd_s0', [2, NS, 8, 128, 128])
            I['gd_cs'] = self.din('gd_cs', [2, 128, 24 * 48])
            O = self.O = {}
            O['yT'] = self.dout('yT', [D, T])
            O['gdn_state_p'] = self.dout('gdn_state_p', [2, 8, 128, 128])
            O['gdn_state_s'] = self.dout('gdn_state_s', [2, NS, 8, 128, 128])
            O['gdn_conv_p'] = self.dout('gdn_conv_p', [2, 128, 72])
            O['gdn_conv_s'] = self.dout('gdn_conv_s', [2, 128, 24 * 48])
            O['ssd_pT'] = self.dout('ssd_pT', [128, 2048])
            O['ssd_sT'] = self.dout('ssd_sT', [NS, 128, 2048])
            O['ssd_conv_p'] = self.dout('ssd_conv_p', [128, 72])
            O['ssd_conv_s'] = self.dout('ssd_conv_s', [128, 24 * 48])
            O['cmlp_v'] = self.dout('cmlp_v', [NS * LS, 2048])
            base0 = nc.sbuf_base
            self.arena0 = (base0 + 31) // 32 * 32
            total = nc.sbuf_top - self.arena0 - 64
            self.arena = st.enter_context(nc.sbuf_tensor("arena", [128, total], U8))
            off = 0
            self.X = self.sb("X", [128, DC, T], F32, off); off += DC * T * 4
            self.ring = []
            for i in range(NRING):
                self.ring.append(self.sb("ring%d" % i, [128, WBUF], BF16, off)); off += WBUF * 2
            self.stg = []
            for i in range(NSTG):
                self.stg.append(self.sb("stg%d" % i, [128, WBUF], F32, off)); off += WBUF * 4
            ncst = CONST_ARR.shape[1]
            self.cst = self.sb("cst", [128, ncst], F32, off); off += ncst * 4
            self.C = {n: self.cst[:, k * 128:(k + 1) * 128] for k, n in enumerate(CONST_NAMES)}
            self.ones_bf = self.sb("ones_bf", [128, 128], BF16, off); off += 256
            self.ident_bf = self.sb("ident_bf", [128, 128], BF16, off); off += 256
            self.nw = self.sb("nw", [128, DEPTH * 6 * DC], F32, off); off += DEPTH * 6 * DC * 4
            self.nwh = self.sb("nwh", [128, DEPTH * 6 * DC], F32, off); off += DEPTH * 6 * DC * 4
            self.epsc = self.sb("epsc", [128, 1], F32, off); off += 32
            self.scr0 = off
            self.scr_size = (total - off) // 64 * 64
            self.scratch = self.sb("scratch", [128, self.scr_size // 4], F32, off)
            self.ps = [st.enter_context(nc.psum_tensor("ps%d" % i, [128, 512], F32)) for i in range(8)]
            block = st.enter_context(nc.Block())
            S.dma('sp', self.cst[:], I['consts'], writes=['cst'])
            S.dma('sp', self.nw[:], I['nw'], writes=['nw'])
            for c in range(DC):
                S.dma('sp', self.X[:, c, :], I['xT'][c * 128:(c + 1) * 128, :], writes=['X'])
            S.op('dve', lambda e: e.tensor_copy(out=self.ones_bf[:], in_=self.C['ones']), reads=['cst'], writes=['ones_bf'])
            S.op('dve', lambda e: e.tensor_copy(out=self.ident_bf[:], in_=self.C['ident']), reads=['cst'], writes=['ident_bf'])
            S.op('dve', lambda e: e.tensor_scalar(out=self.nwh[:], in0=self.nw[:], scalar1=0.5, scalar2=None, op0=ALU.mult),
                 reads=['nw'], writes=['nwh'])
            S.op('dve', lambda e: e.memset(self.epsc[:], EPS), writes=['epsc'])
            S.barrier()
            for (l, part) in self.plan:
                if part == 0:
                    self.ffn(l, 0)
                elif part == 2:
                    self.ffn(l, 1)
                elif l % 3 == 1:
                    self.cmlp(l)
                elif l % 3 == 2:
                    self.ssd(l)
                else:
                    self.gdn(l)
                S.barrier()
            for c in range(DC):
                S.dma('sp', O['yT'][c * 128:(c + 1) * 128, :], self.X[:, c, :], reads=['X', 'Xf0', 'Xf1', 'Xf2'], fence=True)
            S.final_wait('sp')
            S.finish(block, nc)
        return nc

    def nwcol(self, l, i, c, half=False):
        k = (l * 6 + i) * DC + c
        return (self.nwh if half else self.nw)[:, k:k + 1]

    def scr(self, name, shape, dt, off):
        nb = int(np.prod(shape[1:])) * (2 if dt == BF16 else 4)
        assert off + nb <= self.scr_size, (name, off + nb, self.scr_size)
        assert off % 4 == 0 and nb % 4 == 0
        v = self.scratch[:, off // 4:(off + nb) // 4]
        if dt == BF16:
            v = v.bitcast(BF16)
        if len(shape) == 3:
            v = v.rearrange("p (a b) -> p a b", b=shape[2])
        return v

    def prenorm(self, l, idx, t0, n, hn, sq2, rstd, tag):
        S, X = self.S, self.X
        self.pn = getattr(self, 'pn', 0) + 1
        st_ps = self.ps[6 + self.pn % 2]
        kst = 'ps%d' % (6 + self.pn % 2)
        S.op('act', lambda e: e.activation(out=hn[:, :, 0:n], in_=X[:, :, t0:t0 + n], func=AF.Square), reads=['X'], writes=[tag + 'hn'])
        for c in range(DC):
            S.op('pe', lambda e, c=c: e.matmul(st_ps[:, 0:n], lhsT=self.ones_bf[:], rhs=hn[:, c, 0:n], start=(c == 0), stop=(c == DC - 1)),
                 reads=[tag + 'hn', 'ones_bf'], writes=[kst], inc=(c == DC - 1))
        S.op('act', lambda e: e.activation(out=rstd[:, 0:n], in_=st_ps[:, 0:n], func=AF.Ln, scale=1.0 / D, bias=self.epsc[:]),
             reads=[kst, 'epsc'], writes=[tag + 'rstd'])
        S.op('act', lambda e: e.activation(out=rstd[:, 0:n], in_=rstd[:, 0:n], func=AF.Exp, scale=-0.5),
             reads=[tag + 'rstd'], writes=[tag + 'rstd'])
        for c in range(DC):
            S.op('dve', lambda e, c=c: e.scalar_tensor_tensor(
                out=hn[:, c, 0:n], in0=X[:, c, t0:t0 + n], scalar=self.nwcol(l, idx, c), in1=rstd[:, 0:n],
                op0=ALU.mult, op1=ALU.mult), reads=['X', 'nw', tag + 'rstd'], writes=[tag + 'hn'])

    def outproj_residual(self, l, idx, t0, n, wnext, rhs_fn, nk, rhs_key, ysb, sqb, rstd, tag, half=False):
        S, X = self.S, self.X
        for d in range(DC):
            wb, wkey = wnext()
            self.yc = getattr(self, 'yc', 0) + 1
            yi = self.yc % 2
            yps = self.ps[yi]
            for k in range(nk):
                S.op('pe', lambda e, k=k, wb=wb, yps=yps: e.matmul(yps[:, 0:n], lhsT=wb[:, k * 128:(k + 1) * 128], rhs=rhs_fn(k), start=(k == 0), stop=(k == nk - 1)),
                     reads=[wkey, rhs_key], writes=['ps%d' % yi], inc=(k == nk - 1))
            S.op('dve', lambda e, d=d, yps=yps: e.tensor_copy(out=ysb[:, d, 0:n], in_=yps[:, 0:n]), reads=['ps%d' % yi], writes=[tag + 'ysb'])
            S.op('act', lambda e, d=d: e.activation(out=sqb[:, d, 0:n], in_=ysb[:, d, 0:n], func=AF.Square), reads=[tag + 'ysb'], writes=[tag + 'sqb'])
        self.pn = getattr(self, 'pn', 0) + 1
        st_ps = self.ps[6 + self.pn % 2]
        kst = 'ps%d' % (6 + self.pn % 2)
        for d in range(DC):
            S.op('pe', lambda e, d=d: e.matmul(st_ps[:, 0:n], lhsT=self.ones_bf[:], rhs=sqb[:, d, 0:n], start=(d == 0), stop=(d == DC - 1)),
                 reads=[tag + 'sqb', 'ones_bf'], writes=[kst], inc=(d == DC - 1))
        S.op('act', lambda e: e.activation(out=rstd[:, 0:n], in_=st_ps[:, 0:n], func=AF.Ln, scale=1.0 / D, bias=self.epsc[:]),
             reads=[kst, 'epsc'], writes=[tag + 'rstd'])
        S.op('act', lambda e: e.activation(out=rstd[:, 0:n], in_=rstd[:, 0:n], func=AF.Exp, scale=-0.5),
             reads=[tag + 'rstd'], writes=[tag + 'rstd'])
        S.op('dve', lambda e: e.tensor_tensor(out=ysb[:, :, 0:n], in0=ysb[:, :, 0:n], in1=rstd[:, 0:n].unsqueeze(1).broadcast_to([128, DC, n]), op=ALU.mult),
             reads=[tag + 'ysb', tag + 'rstd'], writes=[tag + 'ysb'])
        for d in range(DC):
            S.op('dve', lambda e, d=d: e.scalar_tensor_tensor(
                out=X[:, d, t0:t0 + n], in0=ysb[:, d, 0:n], scalar=self.nwcol(l, idx, d, half=half), in1=X[:, d, t0:t0 + n],
                op0=ALU.mult, op1=ALU.add), reads=[tag + 'ysb', 'nw', 'nwh', 'X'], writes=['X'])

    def proj_conv(self, tag, hn, P, xc, cols, wnext, samp, has_bias, cn):
        S, ps = self.S, self.ps
        N = 128
        pend = []

        def silu(c):
            if has_bias:
                S.op('act', lambda e, c=c: e.activation(out=xc[:, c, :], in_=xc[:, c, :], func=AF.Silu, bias=cols[:, c, 4:5], scale=1.0),
                     reads=[tag + '_xc%d' % c, tag + '_cols'], writes=[tag + '_xc%d' % c])
            else:
                S.op('act', lambda e, c=c: e.activation(out=xc[:, c, :], in_=xc[:, c, :], func=AF.Silu), reads=[tag + '_xc%d' % c], writes=[tag + '_xc%d' % c])
        for u in range(12):
            wb, wkey = wnext()
            wv = wb[:, 0:2048].rearrange("p (c n) -> p c n", n=256)
            for jj in range(2):
                ch = 2 * u + jj
                cn[0] += 1
                bi_ = cn[0] % 2
                b = ps[bi_]
                for c in range(DC):
                    S.op('pe', lambda e, c=c, jj=jj, b=b, wv=wv: e.matmul(b[:, 0:N], lhsT=wv[:, c, jj * 128:(jj + 1) * 128], rhs=hn[:, c, :], start=(c == 0), stop=(c == DC - 1)),
                         reads=[wkey, tag + '_hn'], writes=['ps%d' % bi_], inc=(c == DC - 1))
                if samp:
                    S.op('act', lambda e, ch=ch, b=b: e.activation(out=P[:, ch, :].rearrange("p (s k) -> p s k", k=11)[:, :, 3:11],
                                                                 in_=b[:, 0:N].rearrange("p (s k) -> p s k", k=8), func=AF.Copy),
                         reads=['ps%d' % bi_], writes=[tag + '_P%d' % ch])
                else:
                    S.op('act', lambda e, ch=ch, b=b: e.activation(out=P[:, ch, 3:3 + N], in_=b[:, 0:N], func=AF.Copy),
                         reads=['ps%d' % bi_], writes=[tag + '_P%d' % ch])
            for jj in range(2):
                c = 2 * u + jj
                if samp:
                    pv = lambda j, c=c: P[:, c, :].rearrange("p (s k) -> p s k", k=11)[:, :, j:j + 8]
                    ov = xc[:, c, :].rearrange("p (s k) -> p s k", k=8)
                else:
                    pv = lambda j, c=c: P[:, c, j:j + N]
                    ov = xc[:, c, :]
                S.op('act', lambda e, c=c, pv=pv, ov=ov: e.activation(out=ov, in_=pv(0), func=AF.Copy, scale=cols[:, c, 0:1]),
                     reads=[tag + '_P', tag + '_P%d' % c, tag + '_cols'], writes=[tag + '_xc%d' % c])
                for j in range(1, 4):
                    S.op('dve', lambda e, c=c, j=j, pv=pv, ov=ov: e.scalar_tensor_tensor(out=ov, in0=pv(j), scalar=cols[:, c, j:j + 1], in1=ov, op0=ALU.mult, op1=ALU.add),
                         reads=[tag + '_P', tag + '_P%d' % c, tag + '_cols', tag + '_xc%d' % c], writes=[tag + '_xc%d' % c])
            for c in pend:
                silu(c)
            pend = [2 * u, 2 * u + 1]
        for c in pend:
            silu(c)

    def cmlp(self, l):
        S, I, ps = self.S, self.I, self.ps
        KB = 1024
        hn = self.scr("hn", [128, DC, 512], BF16, 0)
        vt = self.scr("vt", [128, 4, 2048], F32, 8 * KB)
        um_p = self.scr("um", [128, 16, 512], BF16, 8 * KB)
        ysb_p = self.scr("ysb", [128, DC, 512], F32, 24 * KB)
        vbf = self.scr("vbf", [128, 4, 2048], BF16, 40 * KB)
        um_s = self.scr("ums", [128, 16, 128], BF16, 44 * KB)
        ysb_s = self.scr("ysbs", [128, DC, 128], F32, 48 * KB)
        lnw_t = self.scr("lnwt", [128, 2048], F32, 16 * KB)
        lnb_t = self.scr("lnbt", [128, 2048], F32, 24 * KB)
        vout = self.scr("vout", [128, 2048], F32, 32 * KB)
        bv = self.scr("bv", [128, 2048], F32, 56 * KB)
        R = self.scr("R", [128, 16, 128], F32, 64 * KB)
        ws_bf = self.scr("wsbf", [128, 8, 128], BF16, 72 * KB)
        wsb_bf = self.scr("wsbbf", [128, 8, 128], BF16, 74 * KB)
        cols = self.scr("cols", [128, 48], F32, 76 * KB)
        ug = self.scr("ug", [128, 512], F32, 76 * KB + 256)
        rstd = self.scr("rstd", [128, 512], F32, 78 * KB + 256)
        mb = self.scr("mb", [128, 512], F32, 80 * KB + 256)
        sq2 = self.scr("sq2", [128, 2, 512], BF16, 82 * KB + 256)
        bst = self.scr("bst", [128, 4, 6], F32, 84 * KB + 256)
        mv = self.scr("mv", [128, 2], F32, 84 * KB + 384)
        rs1 = self.scr("rs1", [128, 1], F32, 84 * KB + 416)
        ws_st = self.scr("wsst", [128, 8, 128], F32, 8 * KB)
        bs_st = self.scr("bsst", [128, 8, 128], F32, 12 * KB)
        S.barrier()
        S.dma('sp', bv[:], I['cm_bv'], writes=['cm_bv'], fence=True)
        S.dma('sp', cols[:], I['cm_cols'], writes=['cm_cols'], fence=True)

        def setup_R(ws_src, bs_src, mask, wdst, tagk):
            S.dma('sp', ws_st[:], ws_src, writes=['cm_wsst'], fence=True)
            S.dma('sp', bs_st[:], bs_src, writes=['cm_bsst'], fence=True)
            S.op('dve', lambda e: e.tensor_tensor(out=wdst[:], in0=ws_st[:], in1=mask.unsqueeze(1).broadcast_to([128, 8, 128]), op=ALU.mult),
                 reads=['cm_wsst', 'cst'], writes=[tagk])
            for h in range(8):
                b = ps[2 + h // 4]
                S.op('pe', lambda e, h=h, b=b: e.matmul(b[:, (h % 4) * 128:(h % 4 + 1) * 128], lhsT=self.ones_bf[:], rhs=wdst[:, h, :], start=True, stop=True),
                     reads=[tagk, 'ones_bf'], writes=['ps%d' % (2 + h // 4)], inc=True)
            for fc in range(16):
                h = fc // 2
                b = ps[2 + h // 4]
                S.op('dve', lambda e, fc=fc, h=h, b=b: e.scalar_tensor_tensor(
                    out=R[:, fc, :], in0=b[:, (h % 4) * 128:(h % 4 + 1) * 128], scalar=cols[:, 32 + fc:33 + fc], in1=bs_st[:, h, :],
                    op0=ALU.mult, op1=ALU.add), reads=['ps%d' % (2 + h // 4), 'cm_cols', 'cm_bsst'], writes=['cm_R'])

        setup_R(I['cm_wsT'], I['cm_bs'], self.C['B'], ws_bf, 'cm_ws')
        srcs = []
        for _ in BLOCKS:
            srcs += [(I['cm_wv'][q], 2048, ('cv', q)) for q in range(8)]
            srcs += [(I['cm_wu'][q], 2048, ('cu', q)) for q in range(8)]
            srcs += [(I['cm_wo'][d], 2048, ('co', d)) for d in range(8)]
        wnext = self.wstream(srcs)
        vcl = [0]

        def do_block(bi, t0, n):
            samp = (t0 >= SEQ)
            NB = n // 128
            S.barrier()
            if samp:
                setup_R(I['cm_wsblk'], I['cm_bss'], self.C['Mblk'], wsb_bf, 'cm_wsb')
                S.barrier()
                S.dma('sp', lnw_t[:], I['cm_lnw'], writes=['cm_lnwt'], fence=True)
                S.dma('sp', lnb_t[:], I['cm_lnb'], writes=['cm_lnbt'], fence=True)
            wmix = wsb_bf if samp else ws_bf
            wmk = 'cm_wsb' if samp else 'cm_ws'
            um = um_s if samp else um_p
            ysb = ysb_s if samp else ysb_p
            self.prenorm(l, 2, t0, n, hn, sq2, rstd, 'cm_')
            for q in range(8):
                wb, wkey = wnext()
                wv = wb[:, 0:2048].rearrange("p (c n) -> p c n", n=256)
                for tt in range(NB):
                    vi = vcl[0] % 2
                    vcl[0] += 1
                    vps = ps[vi]
                    for c in range(DC):
                        S.op('pe', lambda e, c=c, tt=tt, vps=vps, wv=wv: e.matmul(vps[:, 0:256], lhsT=hn[:, c, tt * 128:(tt + 1) * 128], rhs=wv[:, c, :], start=(c == 0), stop=(c == DC - 1)),
                             reads=[wkey, 'cm_hn'], writes=['ps%d' % vi], inc=(c == DC - 1))
                    S.op('dve', lambda e, q=q, tt=tt, vps=vps: e.tensor_tensor(out=vt[:, tt, q * 256:(q + 1) * 256], in0=vps[:, 0:256], in1=bv[:, q * 256:(q + 1) * 256], op=ALU.add),
                         reads=['ps%d' % vi, 'cm_bv'], writes=['cm_vt%d' % tt])
                    S.op('act', lambda e, q=q, tt=tt: e.activation(out=vt[:, tt, q * 256:(q + 1) * 256], in_=vt[:, tt, q * 256:(q + 1) * 256], func=AF.Gelu),
                         reads=['cm_vt%d' % tt], writes=['cm_vt%d' % tt])
            for tt in range(NB):
                for g in range(4):
                    S.op('dve', lambda e, tt=tt, g=g: e.bn_stats(out=bst[:, g, :], in_=vt[:, tt, g * 512:(g + 1) * 512]),
                         reads=['cm_vt%d' % tt], writes=['cm_bst'])
                S.op('dve', lambda e: e.bn_aggr(out=mv[:], in_=bst[:].rearrange("p a b -> p (a b)")), reads=['cm_bst'], writes=['cm_mv'])
                S.op('act', lambda e: e.activation(out=rs1[:], in_=mv[:, 1:2], func=AF.Ln, scale=1.0, bias=self.epsc[:]),
                     reads=['cm_mv', 'epsc'], writes=['cm_rs1'])
                S.op('act', lambda e: e.activation(out=rs1[:], in_=rs1[:], func=AF.Exp, scale=-0.5),
                     reads=['cm_rs1'], writes=['cm_rs1'])
                S.op('dve', lambda e, tt=tt: e.tensor_scalar(out=vbf[:, tt, :], in0=vt[:, tt, :], scalar1=mv[:, 0:1], scalar2=rs1[:], op0=ALU.subtract, op1=ALU.mult),
                     reads=['cm_vt%d' % tt, 'cm_mv', 'cm_rs1'], writes=['cm_vbf'])
                if samp:
                    S.op('dve', lambda e, tt=tt: e.tensor_scalar(out=vout[:], in0=vt[:, tt, :], scalar1=mv[:, 0:1], scalar2=rs1[:], op0=ALU.subtract, op1=ALU.mult),
                         reads=['cm_vt%d' % tt, 'cm_mv', 'cm_rs1'], writes=['cm_vout'])
                    S.op('dve', lambda e: e.tensor_tensor(out=vout[:], in0=vout[:], in1=lnw_t[:], op=ALU.mult), reads=['cm_vout', 'cm_lnwt'], writes=['cm_vout'])
                    S.op('dve', lambda e: e.tensor_tensor(out=vout[:], in0=vout[:], in1=lnb_t[:], op=ALU.add), reads=['cm_vout', 'cm_lnbt'], writes=['cm_vout'])
                    S.dma('sp', self.O['cmlp_v'], vout[:], reads=['cm_vout'])
            for q in range(8):
                wb, wkey = wnext()
                wv = wb[:, 0:2048].rearrange("p (c n) -> p c n", n=256)
                for jj in range(2):
                    fc = 2 * q + jj
                    mi = fc % 2
                    mps, ups = ps[2 + mi], ps[4 + mi]
                    for tt in range(NB):
                        S.op('pe', lambda e, tt=tt, fc=fc, q=q, mps=mps: e.matmul(mps[:, tt * 128:(tt + 1) * 128], lhsT=vbf[:, tt, fc * 128:(fc + 1) * 128], rhs=wmix[:, q, :], start=True, stop=True),
                             reads=['cm_vbf', wmk], writes=['ps%d' % (2 + mi)], inc=(tt == NB - 1))
                    for c in range(DC):
                        S.op('pe', lambda e, c=c, jj=jj, ups=ups, wv=wv: e.matmul(ups[:, 0:n], lhsT=wv[:, c, jj * 128:(jj + 1) * 128], rhs=hn[:, c, 0:n], start=(c == 0), stop=(c == DC - 1)),
                             reads=[wkey, 'cm_hn'], writes=['ps%d' % (4 + mi)], inc=(c == DC - 1))
                    S.op('act', lambda e, fc=fc, ups=ups: e.activation(out=ug[:, 0:n], in_=ups[:, 0:n], func=AF.Gelu, bias=cols[:, fc:fc + 1], scale=1.0),
                         reads=['ps%d' % (4 + mi), 'cm_cols'], writes=['cm_ug'])
                    S.op('dve', lambda e, fc=fc, mps=mps: e.scalar_tensor_tensor(
                        out=mb[:, 0:n].rearrange("p (a b) -> p a b", b=128), in0=mps[:, 0:n].rearrange("p (a b) -> p a b", b=128),
                        scalar=cols[:, 16 + fc:17 + fc], in1=R[:, fc:fc + 1, :].broadcast_to([128, NB, 128]),
                        op0=ALU.mult, op1=ALU.add), reads=['ps%d' % (2 + mi), 'cm_cols', 'cm_R'], writes=['cm_mb'])
                    S.op('dve', lambda e, fc=fc: e.tensor_tensor(out=um[:, fc, 0:n], in0=ug[:, 0:n], in1=mb[:, 0:n], op=ALU.mult),
                         reads=['cm_ug', 'cm_mb'], writes=['cm_um'])
            self.outproj_residual(l, 3, t0, n, wnext, lambda k: um[:, k, 0:n], 16, 'cm_um', ysb, hn, rstd, 'cm_')

        for bi, (t0, n) in enumerate(BLOCKS):
            do_block(bi, t0, n)

    def ssd(self, l):
        S, I, O, ps, C = self.S, self.I, self.O, self.ps, self.C
        N = 128
        o = [0]

        def take(name, shape, dt):
            v = self.scr(name, shape, dt, o[0])
            o[0] += (int(np.prod(shape[1:])) * (2 if dt == BF16 else 4) + 63) // 64 * 64
            return v
        hn = take("hn", [128, DC, N], BF16)
        p0 = o[0]
        P = take("P", [128, 24, 176], F32)
        p1 = o[0]
        xc = take("xc", [128, 24, N], F32)
        xtok = take("xtok", [128, 2048], F32)
        zt = take("zt", [128, 2048], F32)
        yacc = take("yacc", [128, 2048], F32)
        tmpf = take("tmpf", [128, 2048], F32)
        ST = take("ST", [128, 2048], F32)
        ST_bf = take("STbf", [128, 2048], BF16)
        cols = take("cols", [128, 24, 5], F32)
        nwc = take("nwc", [128, 16], F32)
        rows = take("rows", [128, 3, 32], F32)
        dtt = take("dtt", [128, 32], F32)
        ga = take("ga", [128, 32], F32)
        Gt = take("Gt", [128, 32], F32)
        Et = take("Et", [128, 32], F32)
        dec = take("dec", [128, 32], F32)
        cdbc = take("cdbc", [128, 32], F32)
        arow = take("arow", [128, 32], F32)
        ss4 = take("ss4", [128, 4], F32)
        cdall = take("cdall", [128, 16, 32], F32)
        carry = take("carry", [128, 24, 3], F32)
        rstd = take("rstd", [128, N], F32)
        sq2 = take("sq2", [128, 2, N], BF16)
        stg2 = take("stg2", [128, 2, 512], F32)
        sbf2 = take("sbf2", [128, 2, 512], BF16)
        o[0] = p0
        ynT = take("ynT", [128, 16, N], BF16)
        ysb = take("ysb", [128, DC, N], F32)
        MT = take("MT", [128, 8, N], BF16)
        BT_bf = take("BTbf", [128, 4, N], BF16)
        CT_bf = take("CTbf", [128, 4, N], BF16)
        Btok = take("Btok", [128, 4, N], BF16)
        cbTm = take("cbTm", [128, N], F32)
        gsel = take("gsel", [128, 16, 32], F32)
        Bm2 = take("Bm2", [128, 2, N], BF16)
        assert o[0] <= p1
        o[0] = p1
        xdt = take("xdt", [128, 2048], BF16)
        xdtd = take("xdtd", [128, 2048], BF16)
        CTm = take("CTm", [128, 16, N], BF16)
        assert o[0] <= p1 + 24 * N * 4
        tP = tmpf[:, 0:1152].rearrange("p (c k) -> p c k", k=48)
        gB = tmpf[:, 0:1024].rearrange("p (r i) -> p r i", i=N)
        tY = tmpf[:, 1024:1536]
        outb = tmpf[:, 1536:2048]

        S.barrier()
        S.dma('sp', cols[:], I['ss_cols'].rearrange("p (c k) -> p c k", k=5), writes=['ss_cols'], fence=True)
        S.dma('sp', nwc[:], I['ss_nwc'], writes=['ss_nwc'], fence=True)
        S.dma('sp', rows[:], I['ss_rows'].rearrange("p (a b) -> p a b", b=32), writes=['ss_rows'], fence=True)
        S.op('act', lambda e: e.activation(out=arow[:], in_=rows[:, 0, :], func=AF.Exp), reads=['ss_rows'], writes=['ss_arow'])
        S.op('dve', lambda e: e.tensor_scalar(out=arow[:], in0=arow[:], scalar1=-1.0, scalar2=None, op0=ALU.mult), reads=['ss_arow'], writes=['ss_arow'])
        S.op('dve', lambda e: e.memset(ST[:], 0.0), writes=['ss_ST'])
        S.op('dve', lambda e: e.memset(ST_bf[:], 0.0), writes=['ss_STbf'])
        S.op('dve', lambda e: e.memset(carry[:], 0.0), writes=['ss_carry'])
        srcs = []
        for _ in BLOCKS128:
            srcs += [(I['ss_wx'][u], 2048, ('sx', u)) for u in range(12)]
            srcs += [(I['ss_wdt'], 256, ('sd',))]
            srcs += [(I['ss_wz'][q], 2048, ('sz', q)) for q in range(8)]
            srcs += [(I['ss_wo'][d], 2048, ('so', d)) for d in range(8)]
        wnext = self.wstream(srcs)
        cn = [0]

        def do_block(bi, t0):
            samp = t0 >= SEQ
            last_p = (t0 == SEQ - N)
            Am, Bmk, Sel = (C['Ablk'], C['Mblk'], C['SelLastS']) if samp else (C['A'], C['B'], C['SelLast'])
            S.barrier()
            self.prenorm(l, 2, t0, N, hn, sq2, rstd, 'ss_')
            if samp:
                S.dma('sp', tP, I['ss_cs'].rearrange("p (c k) -> p c k", k=48), writes=['ss_tmpf'], fence=True)
                for c in range(24):
                    S.op('dve', lambda e, c=c: e.tensor_copy(out=P[:, c, :].rearrange("p (s k) -> p s k", k=11)[:, :, 0:3],
                                                            in_=tP[:, c, :].rearrange("p (s k) -> p s k", k=3)),
                         reads=['ss_tmpf'], writes=['ss_P'])
            else:
                S.op('dve', lambda e: e.tensor_copy(out=P[:, :, 0:3], in_=carry[:]), reads=['ss_carry'], writes=['ss_P'])
            self.proj_conv('ss', hn, P, xc, cols, wnext, samp, True, cn)
            if samp:
                for c in range(24):
                    S.op('dve', lambda e, c=c: e.tensor_copy(out=tP[:, c, :].rearrange("p (s k) -> p s k", k=3),
                                                            in_=P[:, c, :].rearrange("p (s k) -> p s k", k=11)[:, :, 8:11]),
                         reads=['ss_P'] + ['ss_P%d' % c_ for c_ in range(24)], writes=['ss_tmpf'])
                S.dma('sp', O['ssd_conv_s'].rearrange("p (c k) -> p c k", k=48), tP, reads=['ss_tmpf'])
                S.barrier()
            else:
                S.op('dve', lambda e: e.tensor_copy(out=carry[:], in_=P[:, :, N:N + 3]), reads=['ss_P'] + ['ss_P%d' % c_ for c_ in range(24)], writes=['ss_carry'])
                if last_p:
                    S.dma('sp', O['ssd_conv_p'].rearrange("p (c k) -> p c k", k=3), carry[:], reads=['ss_carry'])
            xck = ['ss_xc%d' % c for c in range(24)]
            for q in range(4):
                b = ps[2 + q % 2]
                for k in range(4):
                    S.op('pe', lambda e, q=q, k=k, b=b: e.transpose(b[:, k * 128:(k + 1) * 128], xc[:, 4 * q + k, :], C['ident']),
                         reads=xck[4 * q:4 * q + 4] + ['cst'], writes=['ps%d' % (2 + q % 2)], inc=(k == 3))
                S.op('act', lambda e, q=q, b=b: e.activation(out=xtok[:, q * 512:(q + 1) * 512], in_=b[:], func=AF.Copy),
                     reads=['ps%d' % (2 + q % 2)], writes=['ss_xtok'])
            b = ps[2]
            for k in range(4):
                S.op('pe', lambda e, k=k, b=b: e.transpose(b[:, k * 128:(k + 1) * 128], xc[:, 16 + k, :], C['ident']),
                     reads=xck[16:20] + ['cst'], writes=['ps2'], inc=(k == 3))
            S.op('act', lambda e, b=b: e.activation(out=Btok[:].rearrange("p a b -> p (a b)"), in_=b[:], func=AF.Copy), reads=['ps2'], writes=['ss_Btok'])
            S.op('dve', lambda e: e.tensor_copy(out=BT_bf[:], in_=xc[:, 16:20, :]), reads=xck[16:20], writes=['ss_BT'])
            S.op('dve', lambda e: e.tensor_copy(out=CT_bf[:], in_=xc[:, 20:24, :]), reads=xck[20:24], writes=['ss_CT'])
            wb, wkey = wnext()
            wv = wb[:, 0:256].rearrange("p (c n) -> p c n", n=32)
            b = ps[4]
            for c in range(DC):
                S.op('pe', lambda e, c=c, b=b, wv=wv: e.matmul(b[:, 0:32], lhsT=hn[:, c, :], rhs=wv[:, c, :], start=(c == 0), stop=(c == DC - 1)),
                     reads=[wkey, 'ss_hn'], writes=['ps4'], inc=(c == DC - 1))
            S.op('dve', lambda e, b=b: e.tensor_tensor(out=dtt[:], in0=b[:, 0:32], in1=rows[:, 1, :], op=ALU.add), reads=['ps4', 'ss_rows'], writes=['ss_dtt'])
            S.op('act', lambda e: e.activation(out=dtt[:], in_=dtt[:], func=AF.Exp), reads=['ss_dtt'], writes=['ss_dtt'])
            S.op('act', lambda e: e.activation(out=dtt[:], in_=dtt[:], func=AF.Ln, bias=C['ones'][:, 0:1], scale=1.0), reads=['ss_dtt', 'cst'], writes=['ss_dtt'])
            S.op('dve', lambda e: e.tensor_tensor(out=ga[:], in0=dtt[:], in1=arow[:], op=ALU.mult), reads=['ss_dtt', 'ss_arow'], writes=['ss_ga'])
            S.op('pe', lambda e, b=b: e.matmul(b[:, 32:64], lhsT=Bmk, rhs=ga[:], start=True, stop=True), reads=['cst', 'ss_ga'], writes=['ps4'])
            S.op('dve', lambda e, b=b: e.tensor_copy(out=Gt[:], in_=b[:, 32:64]), reads=['ps4'], writes=['ss_Gt'])
            S.op('act', lambda e: e.activation(out=Et[:], in_=Gt[:], func=AF.Exp), reads=['ss_Gt'], writes=['ss_Et'])
            S.op('pe', lambda e, b=b: e.matmul(b[:, 64:96], lhsT=Sel, rhs=Gt[:], start=True, stop=True), reads=['cst', 'ss_Gt'], writes=['ps4'])
            S.op('act', lambda e, b=b: e.activation(out=cdbc[:], in_=b[:, 64:96], func=AF.Exp), reads=['ps4'], writes=['ss_cdbc'])
            S.op('dve', lambda e, b=b: e.tensor_tensor(out=dec[:], in0=b[:, 64:96], in1=Gt[:], op=ALU.subtract), reads=['ps4', 'ss_Gt', 'ss_cdbc'], writes=['ss_dec'])
            S.op('act', lambda e: e.activation(out=dec[:], in_=dec[:], func=AF.Exp), reads=['ss_dec'], writes=['ss_dec'])
            S.op('dve', lambda e: e.tensor_tensor(out=tmpf[:].rearrange("p (h q) -> p h q", q=64), in0=xtok[:].rearrange("p (h q) -> p h q", q=64),
                                                  in1=dtt[:].unsqueeze(2).broadcast_to([128, 32, 64]), op=ALU.mult),
                 reads=['ss_xtok', 'ss_dtt', 'ss_tmpf'], writes=['ss_tmpf'])
            S.op('act', lambda e: e.activation(out=xdt[:], in_=tmpf[:], func=AF.Copy), reads=['ss_tmpf'] + xck, writes=['ss_xdt'])
            S.op('pool', lambda e: e.tensor_tensor(out=xdtd[:].rearrange("p (h q) -> p h q", q=64), in0=tmpf[:].rearrange("p (h q) -> p h q", q=64),
                                                  in1=dec[:].unsqueeze(2).broadcast_to([128, 32, 64]), op=ALU.mult),
                 reads=['ss_tmpf', 'ss_dec'] + xck, writes=['ss_xdtd'])
            for q in range(8):
                wb, wkey = wnext()
                wv = wb[:, 0:2048].rearrange("p (c n) -> p c n", n=256)
                cn[0] += 1
                bi_ = cn[0] % 2
                b = ps[bi_]
                for c in range(DC):
                    S.op('pe', lambda e, c=c, b=b, wv=wv: e.matmul(b[:, 0:256], lhsT=hn[:, c, :], rhs=wv[:, c, :], start=(c == 0), stop=(c == DC - 1)),
                         reads=[wkey, 'ss_hn'], writes=['ps%d' % bi_], inc=(c == DC - 1))
                S.op('act', lambda e, q=q, b=b: e.activation(out=zt[:, q * 256:(q + 1) * 256], in_=b[:, 0:256], func=AF.Silu), reads=['ps%d' % bi_], writes=['ss_zt'])
            S.op('pool', lambda e: e.tensor_tensor(out=yacc[:].rearrange("p (h q) -> p h q", q=64), in0=xtok[:].rearrange("p (h q) -> p h q", q=64),
                                                  in1=rows[:, 2, :].unsqueeze(2).broadcast_to([128, 32, 64]), op=ALU.mult),
                 reads=['ss_xtok', 'ss_rows'], writes=['ss_yacc'])
            if samp:
                S.op('dve', lambda e: e.tensor_tensor(out=gsel[:], in0=Gt[:].unsqueeze(1).broadcast_to([128, 16, 32]),
                                                      in1=C['SeqSel'][:, 16:32].unsqueeze(2).broadcast_to([128, 16, 32]), op=ALU.mult),
                     reads=['ss_Gt', 'cst'], writes=['ss_gsel'])
                S.op('pe', lambda e: e.matmul(ps[4][:], lhsT=C['ones'], rhs=gsel[:].rearrange("p a b -> p (a b)"), start=True, stop=True),
                     reads=['cst', 'ss_gsel'], writes=['ps4'])
                S.op('act', lambda e: e.activation(out=cdall[:].rearrange("p a b -> p (a b)"), in_=ps[4][:], func=AF.Exp), reads=['ps4'], writes=['ss_cdall'])
            for g in range(4):
                S.op('pe', lambda e, g=g: e.matmul(ps[2][:, 0:N], lhsT=BT_bf[:, g, :], rhs=CT_bf[:, g, :], start=True, stop=True),
                     reads=['ss_BT', 'ss_CT'], writes=['ps2'])
                S.op('dve', lambda e: e.tensor_tensor(out=cbTm[:], in0=ps[2][:, 0:N], in1=Bmk, op=ALU.mult), reads=['ps2', 'cst'], writes=['ss_cbTm'])
                S.op('pool', lambda e, g=g: e.tensor_tensor(out=gB, in0=ga[:, g * 8:(g + 1) * 8].unsqueeze(2).broadcast_to([128, 8, N]),
                                                           in1=Bmk.unsqueeze(1).broadcast_to([128, 8, N]), op=ALU.mult),
                     reads=['ss_ga', 'cst', 'ss_tmpf'], writes=['ss_tmpf'])
                for hh in range(2):
                    S.op('pe', lambda e, hh=hh: e.matmul(ps[3 + hh][:], lhsT=Am,
                                                         rhs=gB[:, hh * 4:(hh + 1) * 4, :].rearrange("p a b -> p (a b)"), start=True, stop=True),
                         reads=['cst', 'ss_tmpf'], writes=['ps%d' % (3 + hh)])
                    S.op('act', lambda e, hh=hh: e.activation(out=gB[:, hh * 4:(hh + 1) * 4, :].rearrange("p a b -> p (a b)"), in_=ps[3 + hh][:], func=AF.Exp),
                         reads=['ps%d' % (3 + hh), 'ss_tmpf'], writes=['ss_tmpf'])
                S.op('dve', lambda e: e.tensor_tensor(out=MT[:], in0=gB, in1=cbTm[:].unsqueeze(1).broadcast_to([128, 8, N]), op=ALU.mult),
                     reads=['ss_tmpf', 'ss_cbTm'], writes=['ss_MT'])
                for r in range(8):
                    h = g * 8 + r
                    S.op('pe', lambda e, r=r, h=h: e.matmul(ps[5][:, r * 64:(r + 1) * 64], lhsT=MT[:, r, :], rhs=xdt[:, h * 64:(h + 1) * 64], start=True, stop=True),
                         reads=['ss_MT', 'ss_xdt'], writes=['ps5'], inc=(r == 7))
                S.op('dve', lambda e, g=g: e.tensor_tensor(out=yacc[:, g * 512:(g + 1) * 512], in0=ps[5][:], in1=yacc[:, g * 512:(g + 1) * 512], op=ALU.add),
                     reads=['ps5', 'ss_yacc'], writes=['ss_yacc'])
                if not samp:
                    S.op('pe', lambda e, g=g: e.matmul(ps[6][:], lhsT=CT_bf[:, g, :], rhs=ST_bf[:, g * 512:(g + 1) * 512], start=True, stop=True),
                         reads=['ss_CT', 'ss_STbf'], writes=['ps6'])
                else:
                    S.op('dve', lambda e: e.memset(CTm[:], 0.0), reads=['ss_CTm'], writes=['ss_CTm'])
                    for s_ in range(NS):
                        S.op('dve', lambda e, g=g, s_=s_: e.tensor_copy(out=CTm[:, s_, s_ * 8:(s_ + 1) * 8], in_=CT_bf[:, g, s_ * 8:(s_ + 1) * 8]),
                             reads=['ss_CT', 'ss_CTm'], writes=['ss_CTm'])
                    for s_ in range(NS):
                        k2 = (g * NS + s_) % 2
                        S.dma('sp', stg2[:, k2, :], I['ss_s0T'][s_, :, g * 512:(g + 1) * 512], writes=['ss_stg%d' % k2])
                        S.op('act', lambda e, k2=k2: e.activation(out=sbf2[:, k2, :], in_=stg2[:, k2, :], func=AF.Copy), reads=['ss_stg%d' % k2], writes=['ss_sbf%d' % k2])
                        S.op('pe', lambda e, s_=s_, k2=k2: e.matmul(ps[6][:], lhsT=CTm[:, s_, :], rhs=sbf2[:, k2, :], start=(s_ == 0), stop=(s_ == NS - 1)),
                             reads=['ss_CTm', 'ss_sbf%d' % k2], writes=['ps6'], inc=True)
                        S.op('dve', lambda e, g=g, s_=s_, k2=k2: e.tensor_scalar(out=Bm2[:, k2, :], in0=Btok[:, g, :], scalar1=C['SeqSel'][:, s_:s_ + 1], scalar2=None, op0=ALU.mult),
                             reads=['ss_Btok', 'cst'], writes=['ss_Bm%d' % k2])
                        S.op('pe', lambda e, g=g, k2=k2: e.matmul(ps[7][:], lhsT=Bm2[:, k2, :], rhs=xdtd[:, g * 512:(g + 1) * 512], start=True, stop=True),
                             reads=['ss_Bm%d' % k2, 'ss_xdtd'], writes=['ps7'])
                        S.op('dve', lambda e, g=g, s_=s_, k2=k2: e.tensor_tensor(out=outb.rearrange("p (r q) -> p r q", q=64), in0=stg2[:, k2, :].rearrange("p (r q) -> p r q", q=64),
                                                                       in1=cdall[:, s_, g * 8:(g + 1) * 8].unsqueeze(2).broadcast_to([128, 8, 64]), op=ALU.mult),
                             reads=['ss_stg%d' % k2, 'ss_cdall', 'ss_outb'], writes=['ss_outb'])
                        S.op('dve', lambda e: e.tensor_tensor(out=outb, in0=ps[7][:], in1=outb, op=ALU.add), reads=['ps7', 'ss_outb'], writes=['ss_outb'])
                        S.dma('sp', O['ssd_sT'][s_, :, g * 512:(g + 1) * 512], outb, reads=['ss_outb'])
                S.op('dve', lambda e, g=g: e.tensor_tensor(out=tY.rearrange("p (r q) -> p r q", q=64), in0=ps[6][:].rearrange("p (r q) -> p r q", q=64),
                                                           in1=Et[:, g * 8:(g + 1) * 8].unsqueeze(2).broadcast_to([128, 8, 64]), op=ALU.mult),
                     reads=['ps6', 'ss_Et', 'ss_tY'], writes=['ss_tY'])
                S.op('dve', lambda e, g=g: e.tensor_tensor(out=yacc[:, g * 512:(g + 1) * 512], in0=tY, in1=yacc[:, g * 512:(g + 1) * 512], op=ALU.add),
                     reads=['ss_tY', 'ss_yacc'], writes=['ss_yacc'])
                if not samp:
                    S.op('pe', lambda e, g=g: e.matmul(ps[7][:], lhsT=Btok[:, g, :], rhs=xdtd[:, g * 512:(g + 1) * 512], start=True, stop=True),
                         reads=['ss_Btok', 'ss_xdtd'], writes=['ps7'])
                    S.op('pool', lambda e, g=g: e.tensor_tensor(out=ST[:, g * 512:(g + 1) * 512].rearrange("p (r q) -> p r q", q=64), in0=ST[:, g * 512:(g + 1) * 512].rearrange("p (r q) -> p r q", q=64),
                                                               in1=cdbc[:, g * 8:(g + 1) * 8].unsqueeze(2).broadcast_to([128, 8, 64]), op=ALU.mult),
                         reads=['ss_ST', 'ss_cdbc', 'ss_STbf'], writes=['ss_ST'])
                    S.op('dve', lambda e, g=g: e.tensor_tensor(out=ST[:, g * 512:(g + 1) * 512], in0=ps[7][:], in1=ST[:, g * 512:(g + 1) * 512], op=ALU.add),
                         reads=['ps7', 'ss_ST'], writes=['ss_ST'])
            if not samp:
                S.op('act', lambda e: e.activation(out=ST_bf[:], in_=ST[:], func=AF.Copy), reads=['ss_ST', 'ps6'], writes=['ss_STbf'])
                if last_p:
                    S.dma('sp', O['ssd_pT'], ST[:], reads=['ss_ST'])
            S.op('dve', lambda e: e.tensor_tensor(out=yacc[:], in0=yacc[:], in1=zt[:], op=ALU.mult), reads=['ss_yacc', 'ss_zt'], writes=['ss_yacc'])
            S.op('act', lambda e: e.activation(out=tmpf[:], in_=yacc[:], func=AF.Square), reads=['ss_yacc', 'ss_tmpf', 'ss_tY', 'ss_outb'], writes=['ss_tmpf'])
            S.op('dve', lambda e: e.tensor_reduce(out=ss4[:], in_=tmpf[:].rearrange("p (g q) -> p g q", q=512), axis=AX.X, op=ALU.add), reads=['ss_tmpf'], writes=['ss_ss4'])
            S.op('act', lambda e: e.activation(out=ss4[:], in_=ss4[:], func=AF.Sqrt, scale=1.0 / 512, bias=self.epsc[:]), reads=['ss_ss4', 'epsc'], writes=['ss_ss4'])
            S.op('dve', lambda e: e.reciprocal(out=ss4[:], in_=ss4[:]), reads=['ss_ss4'], writes=['ss_ss4'])
            S.op('dve', lambda e: e.tensor_tensor(out=yacc[:].rearrange("p (g q) -> p g q", q=512), in0=yacc[:].rearrange("p (g q) -> p g q", q=512),
                                                  in1=ss4[:].unsqueeze(2).broadcast_to([128, 4, 512]), op=ALU.mult), reads=['ss_yacc', 'ss_ss4'], writes=['ss_yacc'])
            for q in range(4):
                b = ps[2 + q % 2]
                for k in range(4):
                    S.op('pe', lambda e, q=q, k=k, b=b: e.transpose(b[:, k * 128:(k + 1) * 128], yacc[:, (4 * q + k) * 128:(4 * q + k + 1) * 128], C['ident']),
                         reads=['ss_yacc', 'cst'], writes=['ps%d' % (2 + q % 2)], inc=(k == 3))
                for k in range(4):
                    fc = 4 * q + k
                    S.op('act', lambda e, fc=fc, k=k, b=b: e.activation(out=ynT[:, fc, :], in_=b[:, k * 128:(k + 1) * 128], func=AF.Copy, scale=nwc[:, fc:fc + 1]),
                         reads=['ps%d' % (2 + q % 2), 'ss_nwc'], writes=['ss_ynT'])
            self.outproj_residual(l, 3, t0, N, wnext, lambda k: ynT[:, k, :], 16, 'ss_ynT', ysb, hn, rstd, 'ss_')

        for bi, (t0, n) in enumerate(BLOCKS128):
            do_block(bi, t0)

    def gdn(self, l):
        S, I, O, ps, C = self.S, self.I, self.O, self.ps, self.C
        jl = l // 3
        N = 128
        o = [0]

        def take(name, shape, dt):
            v = self.scr(name, shape, dt, o[0])
            o[0] += (int(np.prod(shape[1:])) * (2 if dt == BF16 else 4) + 63) // 64 * 64
            return v
        hn = take("hn", [128, DC, N], BF16)
        p0 = o[0]
        P = take("P", [128, 24, 176], F32)
        p1 = o[0]
        xc = take("xc", [128, 24, N], F32)
        p2 = o[0]
        gt = take("gt", [128, 8, N], F32)
        qT = take("qT", [128, 8, N], BF16)
        kT = take("kT", [128, 8, N], BF16)
        kdec = take("kdec", [128, 8, N], BF16)
        bV = take("bV", [128, 8, N], F32)
        gB = take("gB", [128, 8, N], F32)
        gam = take("gam", [128, 8, N], F32)
        attnT = take("attnT", [128, 8, N], BF16)
        mM = take("mM", [128, 8, N], F32)
        nM = take("nM", [128, 8, N], F32)
        Rm = take("Rm", [128, 8, N], F32)
        vnb = take("vnb", [128, 8, N], BF16)
        Sst = take("Sst", [128, 8, N], F32)
        Sbf = take("Sbf", [128, 8, N], BF16)
        onT = take("onT", [128, 8, N], BF16)
        ysb = take("ysb", [128, DC, N], F32)
        cols = take("cols", [128, 24, 4], F32)
        rows = take("rows", [128, 2, 8], F32)
        nwcol = take("nwcol", [128, 1], F32)
        negA = take("negA", [128, 8], F32)
        gsc = take("gsc", [128, 8], F32)
        beta = take("beta", [128, 8], F32)
        Gt = take("Gt", [128, 8], F32)
        Et = take("Et", [128, 8], F32)
        dec = take("dec", [128, 8], F32)
        cdbc = take("cdbc", [128, 8], F32)
        bE = take("bE", [128, 8], F32)
        ss8 = take("ss8", [128, 8], F32)
        cdall = take("cdall", [128, 16, 8], F32)
        gsel = take("gsel", [128, 16, 8], F32)
        carry = take("carry", [128, 24, 3], F32)
        rstd = take("rstd", [128, N], F32)
        sq2 = take("sq2", [128, 2, N], BF16)
        kdm = take("kdm", [128, 2, N], BF16)
        o[0] = p0
        Pa = take("Pa", [128, 8, N], F32)
        PTa = take("PTa", [128, 8, N], F32)
        Pb = take("Pb", [128, 8, N], F32)
        PTb = take("PTb", [128, 8, N], F32)
        assert o[0] <= p1
        o[0] = p0
        S0h = take("S0h", [128, 16, N], F32)
        assert o[0] <= p1
        o[0] = p1
        rr = take("rr", [128, 8, N], F32)
        oo = take("oo", [128, 8, N], F32)
        tmp4 = take("tmp4", [128, 8, N], F32)
        assert o[0] <= p2
        S.barrier()
        S.dma('sp', cols[:], I['gd_cols'][jl].rearrange("p (c k) -> p c k", k=4), writes=['gd_cols'], fence=True)
        S.dma('sp', rows[:], I['gd_rows'][jl].rearrange("p (a b) -> p a b", b=8), writes=['gd_rows'], fence=True)
        S.dma('sp', nwcol[:], I['gd_nwc'][jl], writes=['gd_nwc'], fence=True)
        S.op('act', lambda e: e.activation(out=negA[:], in_=rows[:, 0, :], func=AF.Exp), reads=['gd_rows'], writes=['gd_negA'])
        S.op('dve', lambda e: e.tensor_scalar(out=negA[:], in0=negA[:], scalar1=-1.0, scalar2=None, op0=ALU.mult), reads=['gd_negA'], writes=['gd_negA'])
        S.op('dve', lambda e: e.memset(Sst[:], 0.0), writes=['gd_S'])
        S.op('dve', lambda e: e.memset(Sbf[:], 0.0), writes=['gd_Sbf'])
        S.op('dve', lambda e: e.memset(carry[:], 0.0), writes=['gd_carry'])
        srcs = []
        for _ in BLOCKS128:
            srcs += [(I['gd_wx'][jl, u], 2048, ('gx', jl, u)) for u in range(12)]
            srcs += [(I['gd_wab'][jl], 128, ('ga', jl))]
            srcs += [(I['gd_wg'][jl, q], 2048, ('gg', jl, q)) for q in range(4)]
            srcs += [(I['gd_wo'][jl, d], 1024, ('go', jl, d)) for d in range(8)]
        wnext = self.wstream(srcs)
        cn = [0]
        fl = lambda t: t[:].rearrange("p a b -> p (a b)")

        def do_block(bi, t0):
            samp = t0 >= SEQ
            last_p = (t0 == SEQ - N)
            Am, Bmk, Sel = (C['Ablk'], C['Mblk'], C['SelLastS']) if samp else (C['A'], C['B'], C['SelLast'])
            nsq = 2 if samp else 6
            S.barrier()
            self.prenorm(l, 2, t0, N, hn, sq2, rstd, 'gd_')
            tP = self.scr("tP", [128, 24, 48], F32, self._off(gB))
            if samp:
                S.dma('sp', tP, I['gd_cs'][jl].rearrange("p (c k) -> p c k", k=48), writes=['gd_gB', 'gd_gam'], fence=True)
                for c in range(24):
                    S.op('dve', lambda e, c=c: e.tensor_copy(out=P[:, c, :].rearrange("p (s k) -> p s k", k=11)[:, :, 0:3],
                                                            in_=tP[:, c, :].rearrange("p (s k) -> p s k", k=3)),
                         reads=['gd_gB', 'gd_gam'], writes=['gd_P'])
            else:
                S.op('dve', lambda e: e.tensor_copy(out=P[:, :, 0:3], in_=carry[:]), reads=['gd_carry'], writes=['gd_P'])
            self.proj_conv('gd', hn, P, xc, cols, wnext, samp, False, cn)
            if samp:
                for c in range(24):
                    S.op('dve', lambda e, c=c: e.tensor_copy(out=tP[:, c, :].rearrange("p (s k) -> p s k", k=3),
                                                            in_=P[:, c, :].rearrange("p (s k) -> p s k", k=11)[:, :, 8:11]),
                         reads=['gd_P'] + ['gd_P%d' % c_ for c_ in range(24)], writes=['gd_gB', 'gd_gam'])
                S.dma('sp', O['gdn_conv_s'][jl].rearrange("p (c k) -> p c k", k=48), tP, reads=['gd_gB', 'gd_gam'])
            else:
                S.op('dve', lambda e: e.tensor_copy(out=carry[:], in_=P[:, :, N:N + 3]), reads=['gd_P'] + ['gd_P%d' % c_ for c_ in range(24)], writes=['gd_carry'])
                if last_p:
                    S.dma('sp', O['gdn_conv_p'][jl].rearrange("p (c k) -> p c k", k=3), carry[:], reads=['gd_carry'])
            xck = ['gd_xc%d' % c for c in range(24)]
            wb, wkey = wnext()
            wv = wb[:, 0:128].rearrange("p (c n) -> p c n", n=16)
            for c in range(DC):
                S.op('pe', lambda e, c=c, wv=wv: e.matmul(ps[6][:, 0:16], lhsT=hn[:, c, :], rhs=wv[:, c, :], start=(c == 0), stop=(c == DC - 1)),
                     reads=[wkey, 'gd_hn'], writes=['ps6'], inc=(c == DC - 1))
            S.op('dve', lambda e: e.tensor_tensor(out=gsc[:], in0=ps[6][:, 0:8], in1=rows[:, 1, :], op=ALU.add), reads=['ps6', 'gd_rows'], writes=['gd_gsc'])
            S.op('act', lambda e: e.activation(out=beta[:], in_=ps[6][:, 8:16], func=AF.Sigmoid), reads=['ps6', 'gd_gsc'], writes=['gd_beta'])
            S.op('act', lambda e: e.activation(out=gsc[:], in_=gsc[:], func=AF.Exp), reads=['gd_gsc'], writes=['gd_gsc'])
            S.op('act', lambda e: e.activation(out=gsc[:], in_=gsc[:], func=AF.Ln, bias=C['ones'][:, 0:1], scale=1.0), reads=['gd_gsc', 'cst'], writes=['gd_gsc'])
            S.op('dve', lambda e: e.tensor_tensor(out=gsc[:], in0=gsc[:], in1=negA[:], op=ALU.mult), reads=['gd_gsc', 'gd_negA'], writes=['gd_gsc'])
            S.op('pe', lambda e: e.matmul(ps[6][:, 32:40], lhsT=Bmk, rhs=gsc[:], start=True, stop=True), reads=['cst', 'gd_gsc'], writes=['ps6'])
            S.op('dve', lambda e: e.tensor_copy(out=Gt[:], in_=ps[6][:, 32:40]), reads=['ps6'], writes=['gd_Gt'])
            S.op('act', lambda e: e.activation(out=Et[:], in_=Gt[:], func=AF.Exp), reads=['gd_Gt'], writes=['gd_Et'])
            S.op('pe', lambda e: e.matmul(ps[6][:, 64:72], lhsT=Sel, rhs=Gt[:], start=True, stop=True), reads=['cst', 'gd_Gt'], writes=['ps6'])
            S.op('act', lambda e: e.activation(out=cdbc[:], in_=ps[6][:, 64:72], func=AF.Exp), reads=['ps6'], writes=['gd_cdbc'])
            S.op('dve', lambda e: e.tensor_tensor(out=dec[:], in0=ps[6][:, 64:72], in1=Gt[:], op=ALU.subtract), reads=['ps6', 'gd_Gt', 'gd_cdbc'], writes=['gd_dec'])
            S.op('act', lambda e: e.activation(out=dec[:], in_=dec[:], func=AF.Exp), reads=['gd_dec'], writes=['gd_dec'])
            S.op('dve', lambda e: e.tensor_tensor(out=bE[:], in0=beta[:], in1=Et[:], op=ALU.mult), reads=['gd_beta', 'gd_Et'], writes=['gd_bE'])
            for q in range(4):
                wb, wkey = wnext()
                wv = wb[:, 0:2048].rearrange("p (c n) -> p c n", n=256)
                cn[0] += 1
                bi_ = cn[0] % 2
                b = ps[bi_]
                for c in range(DC):
                    S.op('pe', lambda e, c=c, b=b, wv=wv: e.matmul(b[:, 0:256], lhsT=hn[:, c, :], rhs=wv[:, c, :], start=(c == 0), stop=(c == DC - 1)),
                         reads=[wkey, 'gd_hn'], writes=['ps%d' % bi_], inc=(c == DC - 1))
                S.op('act', lambda e, q=q, b=b: e.activation(out=fl(gt)[:, q * 256:(q + 1) * 256], in_=b[:, 0:256], func=AF.Silu), reads=['ps%d' % bi_], writes=['gd_gt'])
            sqv = fl(gam)[:, 0:256].bitcast(BF16)
            rtmp = fl(gam)[:, 512:1024]
            sq4 = fl(gam).bitcast(BF16).rearrange("p (g n) -> p g n", n=512)
            xgs = [xc[:, grp * 4:grp * 4 + 4, :].rearrange("p a b -> p (a b)") for grp in range(4)]
            for grp in range(4):
                S.op('act', lambda e, grp=grp: e.activation(out=sq4[:, grp, :], in_=xgs[grp], func=AF.Square),
                     reads=xck[grp * 4:grp * 4 + 4] + ['gd_gam'], writes=['gd_sq%d' % grp])
            for grp in range(4):
                b = ps[2 + grp]
                for k in range(4):
                    S.op('pe', lambda e, k=k, b=b, grp=grp: e.matmul(b[:, k * 128:(k + 1) * 128], lhsT=self.ones_bf[:], rhs=sq4[:, grp, k * 128:(k + 1) * 128], start=True, stop=True),
                         reads=['gd_sq%d' % grp, 'ones_bf'], writes=['ps%d' % (2 + grp)], inc=(k == 3))
            for grp in range(4):
                b = ps[2 + grp]
                S.op('act', lambda e, b=b: e.activation(out=b[:], in_=b[:], func=AF.Ln, scale=1.0, bias=self.epsc[:]),
                     reads=['ps%d' % (2 + grp), 'epsc'], writes=['ps%d' % (2 + grp)])
                S.op('act', lambda e, b=b: e.activation(out=b[:], in_=b[:], func=AF.Exp, scale=-0.5),
                     reads=['ps%d' % (2 + grp)], writes=['ps%d' % (2 + grp)])
            for grp in range(4):
                b = ps[2 + grp]
                dstT = qT if grp < 2 else kT
                hs = slice((grp % 2) * 4, (grp % 2) * 4 + 4)
                sc = (128.0 ** -0.5) if grp < 2 else 1.0
                S.op('dve', lambda e, grp=grp, dstT=dstT, hs=hs, sc=sc, b=b: e.scalar_tensor_tensor(out=dstT[:, hs, :].rearrange("p a b -> p (a b)"), in0=xgs[grp], scalar=sc, in1=b[:],
                                                                                          op0=ALU.mult, op1=ALU.mult),
                     reads=xck[grp * 4:grp * 4 + 4] + ['ps%d' % (2 + grp)], writes=['gd_qT' if grp < 2 else 'gd_kT'])
            S.op('dve', lambda e: e.memset(rtmp[:, 0:2], 0.0), writes=['gd_gam', 'gd_sq0', 'gd_sq1', 'gd_sq2', 'gd_sq3'])
            for half in range(2):
                b = ps[4 + half]
                for k in range(4):
                    h = half * 4 + k
                    S.op('pe', lambda e, k=k, h=h, b=b: e.matmul(b[:, k * 128:(k + 1) * 128], lhsT=kT[:, h, :], rhs=self.ident_bf[:], start=True, stop=True),
                         reads=['gd_kT', 'ident_bf'], writes=['ps%d' % (4 + half)], inc=(k == 3))
                S.op('dve', lambda e, half=half, b=b: e.tensor_tensor(out=kdec[:, half * 4:half * 4 + 4, :], in0=b[:].rearrange("p (a b) -> p a b", b=N),
                                                                    in1=dec[:, half * 4:half * 4 + 4].unsqueeze(2).broadcast_to([128, 4, N]), op=ALU.mult),
                     reads=['ps%d' % (4 + half), 'gd_dec'], writes=['gd_kdec'])
            for half in range(2):
                b = ps[2 + half]
                for k in range(4):
                    h = half * 4 + k
                    S.op('pe', lambda e, k=k, h=h, b=b: e.transpose(b[:, k * 128:(k + 1) * 128], xc[:, 16 + h, :], C['ident']),
                         reads=xck[16:24] + ['cst'], writes=['ps%d' % (2 + half)], inc=(k == 3))
                S.op('dve', lambda e, half=half, b=b: e.tensor_tensor(out=bV[:, half * 4:half * 4 + 4, :], in0=b[:].rearrange("p (a b) -> p a b", b=N),
                                                                    in1=beta[:, half * 4:half * 4 + 4].unsqueeze(2).broadcast_to([128, 4, N]), op=ALU.mult),
                     reads=['ps%d' % (2 + half), 'gd_beta'], writes=['gd_bV'])
            S.op('pool', lambda e: e.tensor_tensor(out=gB[:], in0=gsc[:].unsqueeze(2).broadcast_to([128, 8, N]), in1=Bmk.unsqueeze(1).broadcast_to([128, 8, N]), op=ALU.mult),
                 reads=['gd_gsc', 'cst', 'gd_gB'], writes=['gd_gB'])
            for hh in range(2):
                S.op('pe', lambda e, hh=hh: e.matmul(ps[6 + hh][:], lhsT=Am, rhs=gB[:, hh * 4:(hh + 1) * 4, :].rearrange("p a b -> p (a b)"), start=True, stop=True),
                     reads=['cst', 'gd_gB'], writes=['ps%d' % (6 + hh)])
                S.op('act', lambda e, hh=hh: e.activation(out=gam[:, hh * 4:(hh + 1) * 4, :].rearrange("p a b -> p (a b)"), in_=ps[6 + hh][:], func=AF.Exp),
                     reads=['ps%d' % (6 + hh), 'gd_gam'], writes=['gd_gam'])
            S.op('dve', lambda e: e.tensor_tensor(out=gam[:], in0=gam[:], in1=Bmk.unsqueeze(1).broadcast_to([128, 8, N]), op=ALU.mult), reads=['gd_gam', 'cst'], writes=['gd_gam'])
            for half in range(2):
                b = ps[4 + half]
                for k in range(4):
                    h = half * 4 + k
                    S.op('pe', lambda e, k=k, h=h, b=b: e.matmul(b[:, k * 128:(k + 1) * 128], lhsT=kT[:, h, :], rhs=qT[:, h, :], start=True, stop=True),
                         reads=['gd_kT', 'gd_qT'], writes=['ps%d' % (4 + half)], inc=(k == 3))
                S.op('dve', lambda e, half=half, b=b: e.tensor_tensor(out=attnT[:, half * 4:half * 4 + 4, :].rearrange("p a b -> p (a b)"), in0=b[:], in1=gam[:, half * 4:half * 4 + 4, :].rearrange("p a b -> p (a b)"), op=ALU.mult),
                     reads=['ps%d' % (4 + half), 'gd_gam'], writes=['gd_attnT'])
            S.op('pool', lambda e: e.tensor_tensor(out=gB[:], in0=gsc[:].unsqueeze(2).broadcast_to([128, 8, N]), in1=Am.unsqueeze(1).broadcast_to([128, 8, N]), op=ALU.mult),
                 reads=['gd_gsc', 'cst', 'gd_gB'], writes=['gd_gB'])
            for hh in range(2):
                S.op('pe', lambda e, hh=hh: e.matmul(ps[6 + hh][:], lhsT=Bmk, rhs=gB[:, hh * 4:(hh + 1) * 4, :].rearrange("p a b -> p (a b)"), start=True, stop=True),
                     reads=['cst', 'gd_gB'], writes=['ps%d' % (6 + hh)])
                S.op('act', lambda e, hh=hh: e.activation(out=gam[:, hh * 4:(hh + 1) * 4, :].rearrange("p a b -> p (a b)"), in_=ps[6 + hh][:], func=AF.Exp),
                     reads=['ps%d' % (6 + hh), 'gd_gam', 'gd_attnT'], writes=['gd_gam'])
            S.op('dve', lambda e: e.tensor_tensor(out=gam[:], in0=gam[:], in1=Am.unsqueeze(1).broadcast_to([128, 8, N]), op=ALU.mult), reads=['gd_gam', 'cst'], writes=['gd_gam'])
            S.op('dve', lambda e: e.tensor_tensor(out=gam[:], in0=gam[:], in1=beta[:].unsqueeze(2).broadcast_to([128, 8, N]), op=ALU.mult), reads=['gd_gam', 'gd_beta'], writes=['gd_gam'])
            for half in range(2):
                b = ps[4 + half]
                for k in range(4):
                    h = half * 4 + k
                    S.op('pe', lambda e, k=k, h=h, b=b: e.matmul(b[:, k * 128:(k + 1) * 128], lhsT=kT[:, h, :], rhs=kT[:, h, :], start=True, stop=True),
                         reads=['gd_kT'], writes=['ps%d' % (4 + half)], inc=(k == 3))
                S.op('dve', lambda e, half=half, b=b: e.tensor_tensor(out=mM[:, half * 4:half * 4 + 4, :].rearrange("p a b -> p (a b)"), in0=b[:], in1=gam[:, half * 4:half * 4 + 4, :].rearrange("p a b -> p (a b)"), op=ALU.mult),
                     reads=['ps%d' % (4 + half), 'gd_gam'], writes=['gd_mM'])
            for half in range(2):
                b = ps[2 + half]
                for k in range(4):
                    h = half * 4 + k
                    S.op('pe', lambda e, k=k, h=h, b=b: e.transpose(b[:, k * 128:(k + 1) * 128], mM[:, h, :], C['ident']),
                         reads=['gd_mM', 'cst'], writes=['ps%d' % (2 + half)], inc=(k == 3))
                S.op('act', lambda e, half=half, b=b: e.activation(out=nM[:, half * 4:half * 4 + 4, :].rearrange("p a b -> p (a b)"), in_=b[:], func=AF.Copy),
                     reads=['ps%d' % (2 + half)], writes=['gd_nM'])
            S.op('pool', lambda e: e.tensor_tensor(out=Rm[:], in0=C['ident'].unsqueeze(1).broadcast_to([128, 8, N]), in1=nM[:], op=ALU.subtract),
                 reads=['cst', 'gd_nM', 'gd_Rm'], writes=['gd_Rm'])
            Pc, PTc, kP, kPT = nM, mM, 'gd_nM', 'gd_mM'
            bufs = [(Pa, PTa, 'gd_Pa', 'gd_PTa'), (Pb, PTb, 'gd_Pb', 'gd_PTb')]
            for it in range(nsq):
                Pn, PTn, kPn, kPTn = bufs[it % 2]
                lastit = (it == nsq - 1)
                for half in range(2):
                    if not lastit:
                        b = ps[2 + half]
                        for k in range(4):
                            h = half * 4 + k
                            S.op('pe', lambda e, k=k, h=h, b=b, Pc=Pc, PTc=PTc: e.matmul(b[:, k * 128:(k + 1) * 128], lhsT=PTc[:, h, :], rhs=Pc[:, h, :], start=True, stop=True),
                                 reads=[kP, kPT], writes=['ps%d' % (2 + half)], inc=(k == 3))
                        S.op('act', lambda e, half=half, b=b, Pn=Pn: e.activation(out=Pn[:, half * 4:half * 4 + 4, :].rearrange("p a b -> p (a b)"), in_=b[:], func=AF.Copy),
                             reads=['ps%d' % (2 + half), kPn], writes=[kPn])
                    b = ps[4 + half]
                    for k in range(4):
                        h = half * 4 + k
                        S.op('pe', lambda e, k=k, h=h, b=b, Pc=Pc, PTc=PTc: e.matmul(b[:, k * 128:(k + 1) * 128], lhsT=Pc[:, h, :], rhs=PTc[:, h, :], start=True, stop=True),
                             reads=[kP, kPT], writes=['ps%d' % (4 + half)], inc=(k == 3))
                    S.op('act', lambda e, half=half, b=b, PTn=PTn: e.activation(out=PTn[:, half * 4:half * 4 + 4, :].rearrange("p a b -> p (a b)"), in_=b[:], func=AF.Copy),
                         reads=['ps%d' % (4 + half), kPTn], writes=[kPTn])
                for half in range(2):
                    b = ps[6 + half]
                    for k in range(4):
                        h = half * 4 + k
                        S.op('pe', lambda e, k=k, h=h, b=b, PTn=PTn: e.matmul(b[:, k * 128:(k + 1) * 128], lhsT=PTn[:, h, :], rhs=Rm[:, h, :], start=True, stop=True),
                             reads=[kPTn, 'gd_Rm'], writes=['ps%d' % (6 + half)], inc=(k == 3))
                for half in range(2):
                    b = ps[6 + half]
                    S.op('dve', lambda e, half=half, b=b: e.tensor_tensor(out=Rm[:, half * 4:half * 4 + 4, :].rearrange("p a b -> p (a b)"), in0=b[:], in1=Rm[:, half * 4:half * 4 + 4, :].rearrange("p a b -> p (a b)"), op=ALU.add),
                         reads=['ps%d' % (6 + half), 'gd_Rm'], writes=['gd_Rm'])
                Pc, PTc, kP, kPT = Pn, PTn, kPn, kPTn
            if not samp:
                for half in range(2):
                    for k in range(4):
                        h = half * 4 + k
                        S.op('pe', lambda e, k=k, h=h, half=half: e.matmul(ps[2 + half][:, k * 128:(k + 1) * 128], lhsT=kT[:, h, :], rhs=Sbf[:, h, :], start=True, stop=True),
                             reads=['gd_kT', 'gd_Sbf'], writes=['ps%d' % (2 + half)], inc=(k == 3))
                        S.op('pe', lambda e, k=k, h=h, half=half: e.matmul(ps[4 + half][:, k * 128:(k + 1) * 128], lhsT=qT[:, h, :], rhs=Sbf[:, h, :], start=True, stop=True),
                             reads=['gd_qT', 'gd_Sbf'], writes=['ps%d' % (4 + half)], inc=(k == 3))
            else:
                S.barrier()
                kTm = self.scr("kTm", [128, 16, N], BF16, self._off(mM))
                qTm = self.scr("qTm", [128, 16, N], BF16, self._off(nM))
                S0b = self.scr("S0b", [128, 16, N], BF16, self._off(gB))
                S.op('dve', lambda e: e.tensor_tensor(out=gsel[:], in0=Gt[:].unsqueeze(1).broadcast_to([128, 16, 8]),
                                                      in1=C['SeqSel'][:, 16:32].unsqueeze(2).broadcast_to([128, 16, 8]), op=ALU.mult), reads=['gd_Gt', 'cst'], writes=['gd_gsel'])
                S.op('pe', lambda e: e.matmul(ps[6][:, 0:128], lhsT=C['ones'], rhs=gsel[:].rearrange("p a b -> p (a b)"), start=True, stop=True), reads=['cst', 'gd_gsel'], writes=['ps6'])
                S.op('act', lambda e: e.activation(out=cdall[:].rearrange("p a b -> p (a b)"), in_=ps[6][:, 0:128], func=AF.Exp), reads=['ps6'], writes=['gd_cdall'])
                for h in range(8):
                    half, k = h // 4, h % 4
                    S.dma('sp', S0h[:], I['gd_s0'][jl, :, h].rearrange("s d e -> d s e"), writes=['gd_S0h'], fence=True)
                    S.op('act', lambda e: e.activation(out=fl(S0b), in_=fl(S0h), func=AF.Copy), reads=['gd_S0h', 'gd_S0b'], writes=['gd_S0b'])
                    S.op('dve', lambda e: e.memset(kTm[:], 0.0), reads=['gd_kTm'], writes=['gd_kTm'])
                    S.op('dve', lambda e: e.memset(qTm[:], 0.0), reads=['gd_qTm'], writes=['gd_qTm'])
                    for s_ in range(NS):
                        S.op('dve', lambda e, h=h, s_=s_: e.tensor_copy(out=kTm[:, s_, s_ * 8:(s_ + 1) * 8], in_=kT[:, h, s_ * 8:(s_ + 1) * 8]), reads=['gd_kT', 'gd_kTm'], writes=['gd_kTm'])
                        S.op('dve', lambda e, h=h, s_=s_: e.tensor_copy(out=qTm[:, s_, s_ * 8:(s_ + 1) * 8], in_=qT[:, h, s_ * 8:(s_ + 1) * 8]), reads=['gd_qT', 'gd_qTm'], writes=['gd_qTm'])
                    for s_ in range(NS):
                        S.op('pe', lambda e, k=k, half=half, s_=s_: e.matmul(ps[2 + half][:, k * 128:(k + 1) * 128], lhsT=kTm[:, s_, :], rhs=S0b[:, s_, :], start=(s_ == 0), stop=(s_ == NS - 1)),
                             reads=['gd_kTm', 'gd_S0b'], writes=['ps%d' % (2 + half)], inc=(s_ == NS - 1))
                    for s_ in range(NS):
                        S.op('pe', lambda e, k=k, half=half, s_=s_: e.matmul(ps[4 + half][:, k * 128:(k + 1) * 128], lhsT=qTm[:, s_, :], rhs=S0b[:, s_, :], start=(s_ == 0), stop=(s_ == NS - 1)),
                             reads=['gd_qTm', 'gd_S0b'], writes=['ps%d' % (4 + half)], inc=(s_ == NS - 1))
            for half in range(2):
                S.op('dve', lambda e, half=half: e.tensor_tensor(out=tmp4[:, half * 4:half * 4 + 4, :], in0=ps[2 + half][:].rearrange("p (a b) -> p a b", b=N),
                                                                 in1=bE[:, half * 4:half * 4 + 4].unsqueeze(2).broadcast_to([128, 4, N]), op=ALU.mult),
                     reads=['ps%d' % (2 + half), 'gd_bE', 'gd_tmp4'], writes=['gd_tmp4'])
                S.op('dve', lambda e, half=half: e.tensor_tensor(out=oo[:, half * 4:half * 4 + 4, :], in0=ps[4 + half][:].rearrange("p (a b) -> p a b", b=N),
                                                                 in1=Et[:, half * 4:half * 4 + 4].unsqueeze(2).broadcast_to([128, 4, N]), op=ALU.mult),
                     reads=['ps%d' % (4 + half), 'gd_Et', 'gd_oo'], writes=['gd_oo'])
            S.op('dve', lambda e: e.tensor_tensor(out=rr[:], in0=bV[:], in1=tmp4[:], op=ALU.subtract), reads=['gd_bV', 'gd_tmp4'] + xck, writes=['gd_rr'])
            for half in range(2):
                for k in range(4):
                    h = half * 4 + k
                    S.op('pe', lambda e, k=k, h=h, half=half: e.matmul(ps[2 + half][:, k * 128:(k + 1) * 128], lhsT=Rm[:, h, :], rhs=rr[:, h, :], start=True, stop=True),
                         reads=['gd_Rm', 'gd_rr'], writes=['ps%d' % (2 + half)], inc=(k == 3))
                S.op('act', lambda e, half=half: e.activation(out=vnb[:, half * 4:half * 4 + 4, :].rearrange("p a b -> p (a b)"), in_=ps[2 + half][:], func=AF.Copy),
                     reads=['ps%d' % (2 + half)], writes=['gd_vnb'])
            for half in range(2):
                for k in range(4):
                    h = half * 4 + k
                    S.op('pe', lambda e, k=k, h=h, half=half: e.matmul(ps[4 + half][:, k * 128:(k + 1) * 128], lhsT=attnT[:, h, :], rhs=vnb[:, h, :], start=True, stop=True),
                         reads=['gd_attnT', 'gd_vnb'], writes=['ps%d' % (4 + half)], inc=(k == 3))
                S.op('dve', lambda e, half=half: e.tensor_tensor(out=oo[:, half * 4:half * 4 + 4, :].rearrange("p a b -> p (a b)"), in0=ps[4 + half][:], in1=oo[:, half * 4:half * 4 + 4, :].rearrange("p a b -> p (a b)"), op=ALU.add),
                     reads=['ps%d' % (4 + half), 'gd_oo'], writes=['gd_oo'])
            if not samp:
                for half in range(2):
                    for k in range(4):
                        h = half * 4 + k
                        S.op('pe', lambda e, k=k, h=h, half=half: e.matmul(ps[6 + half][:, k * 128:(k + 1) * 128], lhsT=kdec[:, h, :], rhs=vnb[:, h, :], start=True, stop=True),
                             reads=['gd_kdec', 'gd_vnb'], writes=['ps%d' % (6 + half)], inc=(k == 3))
                    S.op('pool', lambda e, half=half: e.tensor_tensor(out=Sst[:, half * 4:half * 4 + 4, :], in0=Sst[:, half * 4:half * 4 + 4, :],
                                                                     in1=cdbc[:, half * 4:half * 4 + 4].unsqueeze(2).broadcast_to([128, 4, N]), op=ALU.mult),
                         reads=['gd_S', 'gd_cdbc', 'gd_Sbf'], writes=['gd_S'])
                    S.op('dve', lambda e, half=half: e.tensor_tensor(out=Sst[:, half * 4:half * 4 + 4, :].rearrange("p a b -> p (a b)"), in0=ps[6 + half][:], in1=Sst[:, half * 4:half * 4 + 4, :].rearrange("p a b -> p (a b)"), op=ALU.add),
                         reads=['ps%d' % (6 + half), 'gd_S'], writes=['gd_S'])
                S.op('act', lambda e: e.activation(out=fl(Sbf), in_=fl(Sst), func=AF.Copy), reads=['gd_S'], writes=['gd_Sbf'])
                if last_p:
                    S.dma('sp', O['gdn_state_p'][jl].rearrange("h d e -> d h e"), Sst[:], reads=['gd_S'])
            else:
                for h in range(8):
                    S.dma('sp', S0h[:], I['gd_s0'][jl, :, h].rearrange("s d e -> d s e"), writes=['gd_S0h'])
                    for s_ in range(NS):
                        k2 = s_ % 2
                        S.op('dve', lambda e, h=h, s_=s_, k2=k2: e.tensor_scalar(out=kdm[:, k2, :], in0=kdec[:, h, :], scalar1=C['SeqSel'][:, s_:s_ + 1], scalar2=None, op0=ALU.mult),
                             reads=['gd_kdec', 'cst'], writes=['gd_kdm%d' % k2])
                        S.op('pe', lambda e, h=h, k2=k2: e.matmul(ps[6 + k2][:, 0:N], lhsT=kdm[:, k2, :], rhs=vnb[:, h, :], start=True, stop=True),
                             reads=['gd_kdm%d' % k2, 'gd_vnb'], writes=['ps%d' % (6 + k2)])
                        S.op('dve', lambda e, h=h, s_=s_, k2=k2: e.scalar_tensor_tensor(out=S0h[:, s_, :], in0=S0h[:, s_, :], scalar=cdall[:, s_, h:h + 1], in1=ps[6 + k2][:, 0:N],
                                                                                 op0=ALU.mult, op1=ALU.add),
                             reads=['gd_S0h', 'gd_cdall', 'ps%d' % (6 + k2)], writes=['gd_S0h'])
                    S.dma('sp', O['gdn_state_s'][jl, :, h].rearrange("s d e -> d s e"), S0h[:], reads=['gd_S0h'])
            S.op('act', lambda e: e.activation(out=fl(tmp4), in_=fl(oo), func=AF.Square), reads=['gd_oo', 'gd_tmp4', 'gd_rr'], writes=['gd_tmp4'])
            S.op('dve', lambda e: e.tensor_reduce(out=ss8[:], in_=tmp4[:], axis=AX.X, op=ALU.add), reads=['gd_tmp4'], writes=['gd_ss8'])
            S.op('act', lambda e: e.activation(out=ss8[:], in_=ss8[:], func=AF.Sqrt, scale=1.0 / 128, bias=self.epsc[:]), reads=['gd_ss8', 'epsc'], writes=['gd_ss8'])
            S.op('dve', lambda e: e.reciprocal(out=ss8[:], in_=ss8[:]), reads=['gd_ss8'], writes=['gd_ss8'])
            S.op('dve', lambda e: e.tensor_tensor(out=oo[:], in0=oo[:], in1=ss8[:].unsqueeze(2).broadcast_to([128, 8, N]), op=ALU.mult), reads=['gd_oo', 'gd_ss8'], writes=['gd_oo'])
            S.op('dve', lambda e: e.tensor_tensor(out=oo[:], in0=oo[:], in1=gt[:], op=ALU.mult), reads=['gd_oo', 'gd_gt'], writes=['gd_oo'])
            for half in range(2):
                b = ps[2 + half]
                for k in range(4):
                    h = half * 4 + k
                    S.op('pe', lambda e, k=k, h=h, b=b: e.transpose(b[:, k * 128:(k + 1) * 128], oo[:, h, :], C['ident']),
                         reads=['gd_oo', 'cst'], writes=['ps%d' % (2 + half)], inc=(k == 3))
                S.op('act', lambda e, half=half, b=b: e.activation(out=onT[:, half * 4:half * 4 + 4, :].rearrange("p a b -> p (a b)"), in_=b[:], func=AF.Copy, scale=nwcol[:]),
                     reads=['ps%d' % (2 + half), 'gd_nwc'], writes=['gd_onT'])
            self.outproj_residual(l, 3, t0, N, wnext, lambda k: onT[:, k, :], 8, 'gd_onT', ysb, hn, rstd, 'gd_')

        for bi, (t0, n) in enumerate(BLOCKS128):
            do_block(bi, t0)

    def _sqv(self, tmp4):
        return tmp4[:].rearrange("p a b -> p (a b)")[:, 0:256].bitcast(BF16)

    def _off(self, v):
        return (v.offset - self.scratch[:, 0:1].offset) * (2 if v.dtype == BF16 else 4)

    def ffn(self, l, f):
        S = self.S
        X = self.X
        ipre, ipost = (0, 1) if f == 0 else (4, 5)
        o = 0
        xn = self.scr("xn", [128, DC, PMAX], BF16, o); o += DC * PMAX * 2
        a = self.scr("a", [128, FC, PMAX], BF16, o); o += FC * PMAX * 2
        ysb = self.scr("ysb", [128, DC, PMAX], F32, o); o += DC * PMAX * 4
        sg = [self.scr("sg%d" % i, [128, 512], F32, o + i * 2048) for i in range(2)]; o += 4096
        sq = [self.scr("sq%d" % i, [128, 512], BF16, o + i * 1024) for i in range(2)]; o += 2048
        rstd = self.scr("rstd", [128, PMAX], F32, o); o += PMAX * 4
        assert o <= self.scr_size
        ps = self.ps
        cnt = 0
        import os
        PH = int(os.environ.get('FFN_PH', '4'))
        srcs = []
        for _ in FFN_PASSES:
            srcs += [(self.I['ffn_w_in'][l, f, k], 2048, ('fi', l, f, k)) for k in range(FC)]
            srcs += [(self.I['ffn_w_out'][l, f, d, h], 1408, ('fo', l, f, d, h)) for d in range(DC) for h in range(2)]
        wnext = self.wstream(srcs)
        for pi, subt in enumerate(FFN_PASSES):
            if pi == 0:
                S.barrier()
            xk = 'Xf%d' % pi
            p0 = subt[0][0]
            for si, (t0, n) in enumerate(subt):
                lo = t0 - p0
                st_ps = ps[6 + si % 2]
                kst = 'ps%d' % (6 + si % 2)
                S.op('act', lambda e, lo=lo, t0=t0, n=n: e.activation(out=xn[:, :, lo:lo + n], in_=X[:, :, t0:t0 + n], func=AF.Square),
                     reads=['X', xk], writes=['xn%d' % si])
                for c in range(DC):
                    S.op('pe', lambda e, c=c, lo=lo, n=n, st_ps=st_ps: e.matmul(st_ps[:, 0:n], lhsT=self.ones_bf[:], rhs=xn[:, c, lo:lo + n], start=(c == 0), stop=(c == DC - 1)),
                         reads=['xn%d' % si, 'ones_bf'], writes=[kst], inc=(c == DC - 1))
                S.op('act', lambda e, lo=lo, n=n, st_ps=st_ps: e.activation(out=rstd[:, lo:lo + n], in_=st_ps[:, 0:n], func=AF.Ln, scale=1.0 / D, bias=self.epsc[:]),
                     reads=[kst, 'epsc'], writes=['rstd%d' % si])
                S.op('act', lambda e, lo=lo, n=n: e.activation(out=rstd[:, lo:lo + n], in_=rstd[:, lo:lo + n], func=AF.Exp, scale=-0.5),
                     reads=['rstd%d' % si], writes=['rstd%d' % si])
                for c in range(DC):
                    S.op('dve', lambda e, c=c, lo=lo, n=n, t0=t0: e.scalar_tensor_tensor(
                        out=xn[:, c, lo:lo + n], in0=X[:, c, t0:t0 + n], scalar=self.nwcol(l, ipre, c), in1=rstd[:, lo:lo + n],
                        op0=ALU.mult, op1=ALU.mult), reads=['X', xk, 'nw', 'rstd%d' % si], writes=['xn%d' % si])
            if PH < 2:
                continue
            for k in range(FC):
                wb, wkey = wnext()
                wv = wb[:, 0:2048].rearrange("p (c n) -> p c n", n=256)
                for si, (t0, n) in enumerate(subt):
                    lo = t0 - p0
                    for jj in range(1):
                        j = k
                        gi = cnt % 2
                        cnt += 1
                        gps, ups = ps[gi], ps[2 + gi]
                        for c in range(DC):
                            S.op('pe', lambda e, c=c, jj=jj, lo=lo, n=n, gps=gps, wv=wv: e.matmul(
                                gps[:, 0:n], lhsT=wv[:, c, 0:128], rhs=xn[:, c, lo:lo + n], start=(c == 0), stop=(c == DC - 1)),
                                reads=[wkey, 'xn%d' % si], writes=['ps%d' % gi], inc=(c == DC - 1))
                        for c in range(DC):
                            S.op('pe', lambda e, c=c, jj=jj, lo=lo, n=n, ups=ups, wv=wv: e.matmul(
                                ups[:, 0:n], lhsT=wv[:, c, 128:256], rhs=xn[:, c, lo:lo + n], start=(c == 0), stop=(c == DC - 1)),
                                reads=[wkey, 'xn%d' % si], writes=['ps%d' % (2 + gi)], inc=(c == DC - 1))
                        S.op('act', lambda e, gi=gi, n=n, gps=gps: e.activation(out=sg[gi][:, 0:n], in_=gps[:, 0:n], func=AF.Silu),
                             reads=['ps%d' % gi], writes=['sg%d' % gi])
                        S.op('dve', lambda e, gi=gi, n=n, ups=ups, j=j, lo=lo: e.tensor_tensor(
                            out=a[:, j, lo:lo + n], in0=sg[gi][:, 0:n], in1=ups[:, 0:n], op=ALU.mult),
                            reads=['sg%d' % gi, 'ps%d' % (2 + gi)], writes=['a%d' % si])
            if PH < 3:
                continue
            for d in range(DC):
                wb0, wkey0 = wnext()
                wb1, wkey1 = wnext()
                for si, (t0, n) in enumerate(subt):
                    lo = t0 - p0
                    yi = cnt % 2
                    cnt += 1
                    yps = ps[4 + yi]
                    st_ps = ps[si] if si < 2 else ps[6]
                    kst = 'ps%d' % (si if si < 2 else 6)
                    for c in range(FC):
                        wb, wkey, cc = (wb0, wkey0, c) if c < 11 else (wb1, wkey1, c - 11)
                        S.op('pe', lambda e, c=c, cc=cc, lo=lo, n=n, yps=yps, wb=wb: e.matmul(
                            yps[:, 0:n], lhsT=wb[:, cc * 128:(cc + 1) * 128], rhs=a[:, c, lo:lo + n], start=(c == 0), stop=(c == FC - 1)),
                            reads=[wkey, 'a%d' % si], writes=['ps%d' % (4 + yi)], inc=(c == FC - 1))
                    S.op('dve', lambda e, d=d, lo=lo, n=n, yps=yps: e.tensor_copy(out=ysb[:, d, lo:lo + n], in_=yps[:, 0:n]),
                         reads=['ps%d' % (4 + yi)], writes=['ysb%d' % si])
                    S.op('act', lambda e, d=d, lo=lo, n=n: e.activation(out=xn[:, d, lo:lo + n], in_=ysb[:, d, lo:lo + n], func=AF.Square),
                         reads=['ysb%d' % si], writes=['xn%d' % si])
            if PH < 4:
                continue
            for si, (t0, n) in enumerate(subt):
                lo = t0 - p0
                st_ps = ps[6 + si % 2]
                kst = 'ps%d' % (6 + si % 2)
                for d in range(DC):
                    S.op('pe', lambda e, d=d, lo=lo, n=n, st_ps=st_ps: e.matmul(st_ps[:, 0:n], lhsT=self.ones_bf[:], rhs=xn[:, d, lo:lo + n], start=(d == 0), stop=(d == DC - 1)),
                         reads=['xn%d' % si, 'ones_bf'], writes=[kst], inc=(d == DC - 1))
                S.op('act', lambda e, lo=lo, n=n, st_ps=st_ps: e.activation(out=rstd[:, lo:lo + n], in_=st_ps[:, 0:n], func=AF.Ln, scale=1.0 / D, bias=self.epsc[:]),
                     reads=[kst, 'epsc'], writes=['rstd%d' % si])
                S.op('act', lambda e, lo=lo, n=n: e.activation(out=rstd[:, lo:lo + n], in_=rstd[:, lo:lo + n], func=AF.Exp, scale=-0.5),
                     reads=['rstd%d' % si], writes=['rstd%d' % si])
                for d in range(DC):
                    S.op('dve', lambda e, d=d, lo=lo, n=n: e.tensor_tensor(out=ysb[:, d, lo:lo + n], in0=ysb[:, d, lo:lo + n], in1=rstd[:, lo:lo + n], op=ALU.mult),
                         reads=['ysb%d' % si, 'rstd%d' % si], writes=['ysb%d' % si])
                    S.op('dve', lambda e, d=d, lo=lo, n=n, t0=t0: e.scalar_tensor_tensor(
                        out=X[:, d, t0:t0 + n], in0=ysb[:, d, lo:lo + n], scalar=self.nwcol(l, ipost, d, half=True), in1=X[:, d, t0:t0 + n],
                        op0=ALU.mult, op1=ALU.add), reads=['ysb%d' % si, 'nwh', 'X', xk], writes=[xk])


_CACHE = {}


def prep_shared(inp, nl=DEPTH):
    sh = {}
    f32 = lambda k: np.asarray(inp[k], np.float32)
    rep = lambda v: np.ascontiguousarray(np.broadcast_to(v, (128,) + v.shape))
    w = f32('cmlp_w_in')[0].reshape(DC, 128, 2, 8, 256)
    sh['cm_wu'] = np.ascontiguousarray(w[:, :, 0].transpose(2, 1, 0, 3)).reshape(8, 128, 2048)
    sh['cm_wv'] = np.ascontiguousarray(w[:, :, 1].transpose(2, 1, 0, 3)).reshape(8, 128, 2048)
    w = f32('cmlp_w_out')[0].reshape(16, 128, DC, 128)
    sh['cm_wo'] = np.ascontiguousarray(w.transpose(2, 1, 0, 3)).reshape(8, 128, 2048)
    b = f32('cmlp_b_in')[0]
    sh['cm_bv'] = rep(b[2048:])
    sh['cm_lnw'] = rep(f32('cmlp_ln_w')[0])
    sh['cm_lnb'] = rep(f32('cmlp_ln_b')[0])
    sh['cm_cols'] = np.ascontiguousarray(np.concatenate([b[:2048].reshape(16, 128).T, f32('cmlp_ln_w')[0].reshape(16, 128).T,
                                                         f32('cmlp_ln_b')[0].reshape(16, 128).T], axis=1))
    w = f32('gdn_w_in')
    sh['gd_wx'] = np.ascontiguousarray(w[:, :, :3072].reshape(2, DC, 128, 12, 256).transpose(0, 3, 2, 1, 4)).reshape(2, 12, 128, 2048)
    sh['gd_wg'] = np.ascontiguousarray(w[:, :, 3072:4096].reshape(2, DC, 128, 4, 256).transpose(0, 3, 2, 1, 4)).reshape(2, 4, 128, 2048)
    sh['gd_wab'] = np.ascontiguousarray(w[:, :, 4096:].reshape(2, DC, 128, 16).transpose(0, 2, 1, 3)).reshape(2, 128, 128)
    w = f32('gdn_w_out').reshape(2, 8, 128, DC, 128)
    sh['gd_wo'] = np.ascontiguousarray(w.transpose(0, 3, 2, 1, 4)).reshape(2, 8, 128, 1024)
    sh['gd_cols'] = np.ascontiguousarray(f32('gdn_conv_w').reshape(2, 4, 24, 128).transpose(0, 3, 2, 1)).reshape(2, 128, 96)
    sh['gd_rows'] = np.ascontiguousarray(np.broadcast_to(np.concatenate([f32('gdn_a_log'), f32('gdn_dt_bias')], axis=1)[:, None, :], (2, 128, 16)))
    sh['gd_nwc'] = np.ascontiguousarray(f32('gdn_norm_w').reshape(2, 128, 1))
    w = f32('ssd_w_in')[0]
    sh['ss_wz'] = np.ascontiguousarray(w[:, :2048].reshape(DC, 128, 8, 256).transpose(2, 1, 0, 3)).reshape(8, 128, 2048)
    sh['ss_wx'] = np.ascontiguousarray(w[:, 2048:5120].reshape(DC, 128, 12, 256).transpose(2, 1, 0, 3)).reshape(12, 128, 2048)
    sh['ss_wdt'] = np.ascontiguousarray(w[:, 5120:].reshape(DC, 128, 32).transpose(1, 0, 2)).reshape(128, 256)
    w = f32('ssd_w_out')[0].reshape(16, 128, DC, 128)
    sh['ss_wo'] = np.ascontiguousarray(w.transpose(2, 1, 0, 3)).reshape(8, 128, 2048)
    cw = f32('ssd_conv_w')[0].reshape(4, 24, 128)
    cb = f32('ssd_conv_b')[0].reshape(1, 24, 128)
    sh['ss_cols'] = np.ascontiguousarray(np.concatenate([cw, cb], axis=0).transpose(2, 1, 0)).reshape(128, 120)
    sh['ss_nwc'] = np.ascontiguousarray(f32('ssd_norm_w')[0].reshape(16, 128).T)
    sh['ss_rows'] = rep(np.concatenate([f32('ssd_a_log')[0], f32('ssd_dt_bias')[0], f32('ssd_d')[0]]))
    ws = f32('cmlp_w_s')[0]
    sh['cm_wsT'] = np.ascontiguousarray(ws.transpose(2, 0, 1))
    blk = np.zeros((128, 8, 128), np.float32)
    for q in range(NS):
        blk[q * 8:(q + 1) * 8, :, q * 8:(q + 1) * 8] = ws[:, :8, :8].transpose(2, 0, 1)
    sh['cm_wsblk'] = blk
    bs = f32('cmlp_b_s')[0]
    sh['cm_bs'] = rep(bs)
    sh['cm_bss'] = rep(np.ascontiguousarray(np.tile(bs[:, :8], (1, NS))))
    sh['consts'] = CONST_ARR
    nw = np.asarray(inp['norm_w'], np.float32).reshape(DEPTH, 6, DC, 128)
    sh['nw'] = np.ascontiguousarray(nw.transpose(3, 0, 1, 2).reshape(128, DEPTH * 6 * DC))
    w = np.asarray(inp['ffn_w_in'], np.float32).reshape(DEPTH, 2, DC, 128, 2, FC, 128)
    sh['ffn_w_in'] = np.ascontiguousarray(w[:nl].transpose(0, 1, 5, 3, 2, 4, 6)).reshape(nl, 2, FC, 128, 2048)
    w = np.asarray(inp['ffn_w_out'], np.float32).reshape(DEPTH, 2, 2, 11, 128, DC, 128)
    sh['ffn_w_out'] = np.ascontiguousarray(w[:nl].transpose(0, 1, 5, 2, 4, 3, 6)).reshape(nl, 2, DC, 2, 128, 11 * 128)
    return sh


def prep_core(inp, c):
    xp = np.asarray(inp['x_prompt'], np.float32)[c]
    xs = np.asarray(inp['x_sample'], np.float32)[c * NS:(c + 1) * NS].reshape(NS * LS, D)
    m = {}
    m['xT'] = np.ascontiguousarray(np.concatenate([xp, xs], axis=0).T)
    m['gd_s0'] = np.ascontiguousarray(np.asarray(inp['state_gdn'], np.float32)[:, c * NS:(c + 1) * NS])
    cs = np.asarray(inp['state_gdn_conv'], np.float32)[:, c * NS:(c + 1) * NS].reshape(2, NS, 3, 24, 128)
    m['gd_cs'] = np.ascontiguousarray(cs.transpose(0, 4, 3, 1, 2)).reshape(2, 128, 24 * 48)
    st = np.asarray(inp['state_ssd'], np.float32)[0, c * NS:(c + 1) * NS].reshape(NS, 2048, 128)
    m['ss_s0T'] = np.ascontiguousarray(st.transpose(0, 2, 1))
    cs = np.asarray(inp['state_ssd_conv'], np.float32)[0, c * NS:(c + 1) * NS].reshape(NS, 3, 24, 128)
    m['ss_cs'] = np.ascontiguousarray(cs.transpose(3, 2, 0, 1)).reshape(128, 24 * 48)
    return m


def run(inp, stages=999, plan=None):
    key = str(plan)
    if key not in _CACHE:
        b = Builder(stages, plan=plan)
        _CACHE[key] = (b.build(), b.nl)
    nc, nl = _CACHE[key]
    sh = prep_shared(inp, nl)
    in_maps = []
    for c in range(NCORES):
        m = dict(sh)
        m.update(prep_core(inp, c))
        in_maps.append(m)
    res = run_bass_kernel_spmd(nc, in_maps, core_ids=list(range(NCORES)))
    return res.results


def assemble(results):
    yT = [r['yT'] for r in results]
    y_prompt = np.stack([y[:, :SEQ].T for y in yT]).astype(np.float32)
    y_sample = np.concatenate([y[:, SEQ:].T.reshape(NS, LS, D) for y in yT]).astype(np.float32)
    return y_prompt, y_sample


def kernel(**inputs):
    results = run(inputs)
    y_prompt, y_sample = assemble(results)
    g = [gdn_outputs(r) for r in results]
    d = [ssd_outputs(r) for r in results]
    f = np.float32
    gdn_state_p = np.stack([x[0] for x in g], axis=1).astype(f)
    gdn_conv_p = np.stack([x[2] for x in g], axis=1).astype(f)
    ssd_state_p = np.stack([x[0] for x in d], axis=0)[None].astype(f)
    ssd_conv_p = np.stack([x[2] for x in d], axis=0)[None].astype(f)
    gdn_state_s = np.concatenate([x[1] for x in g], axis=1).astype(f)
    gdn_conv_s = np.concatenate([x[3] for x in g], axis=1).astype(f)
    ssd_state_s = np.concatenate([x[1] for x in d], axis=0)[None].astype(f)
    ssd_conv_s = np.concatenate([x[3] for x in d], axis=0)[None].astype(f)
    cmlp_v_s = np.concatenate([r['cmlp_v'].reshape(NS, LS, 2048) for r in results], axis=0)[None].astype(f)
    return (y_prompt, y_sample, gdn_state_p, gdn_conv_p, ssd_state_p, ssd_conv_p,
            gdn_state_s, gdn_conv_s, ssd_state_s, ssd_conv_s, cmlp_v_s)


def ssd_outputs(r):
    sp = r['ssd_pT'].T.reshape(32, 64, 128)
    ssn = r['ssd_sT'].transpose(0, 2, 1).reshape(NS, 32, 64, 128)
    cp = r['ssd_conv_p'].reshape(128, 24, 3).transpose(2, 1, 0).reshape(3, 3072)
    csn = r['ssd_conv_s'].reshape(128, 24, NS, 3).transpose(2, 3, 1, 0).reshape(NS, 3, 3072)
    return sp, ssn, cp, csn


def gdn_outputs(r):
    cp = r['gdn_conv_p'].reshape(2, 128, 24, 3).transpose(0, 3, 2, 1).reshape(2, 3, 3072)
    csn = r['gdn_conv_s'].reshape(2, 128, 24, NS, 3).transpose(0, 3, 4, 2, 1).reshape(2, NS, 3, 3072)
    return r['gdn_state_p'], r['gdn_state_s'], cp, csn


def check_extra(res, core, extra, rv):
    for i, (nsp, nss) in extra.items():
        if i % 3 == 0:
            sp, ssn, cp, csn = gdn_outputs(res[core])
            j = i // 3
            print("  layer", i, "gdn state_p", rv(sp[j], np.asarray(nsp[1])[0]), "state_s", rv(ssn[j], np.asarray(nss[1])),
                  "conv_p", rv(cp[j], np.asarray(nsp[0])[0]), "conv_s", rv(csn[j], np.asarray(nss[0])))
        if i % 3 == 2:
            sp, ssn, cp, csn = ssd_outputs(res[core])
            print("  layer", i, "ssd state_p", rv(sp, np.asarray(nsp[1])[0]), "state_s", rv(ssn, np.asarray(nss[1])),
                  "conv_p", rv(cp, np.asarray(nsp[0])[0]), "conv_s", rv(csn, np.asarray(nss[0])))
        if i % 3 == 1:
            ref = np.asarray(nss[0]).reshape(NS * LS, 2048)
            print("  layer", i, "cmlp_v resvar", rv(res[core]['cmlp_v'], ref))
```
